# Optimizing a Trainium2 kernel written in Bass

```python
import jax, jax.numpy as jnp
from jax import lax
import numpy as np

D_MODEL = 1024
BATCH = 16
SEQ = 2048
DEPTH = 1
DEC_BATCH = 16
DEC_SEQ = 64
PAST_LEN = 4096

CHUNK = 64
Q_BLOCK = 128
H_RET = 4
DK_RET = 128
DV_RET = 128
H_FOX = 8
D_FOX = 64
W_RET = H_RET * DV_RET
W_FOX = H_FOX * D_FOX
D_MIX = W_RET + W_FOX
D_IN = 2 * H_RET * DK_RET + 2 * W_RET + 3 * W_FOX + H_FOX
D_FF = ((8 * D_MODEL // 3 + 255) // 256) * 256
ROPE_BASE = 10000.0
NORM_EPS = 1e-6
GN_EPS = 1e-5
NEG_INF = -1e30

kernel_name = 'hybrid_retention_fox_stream_step'


def _rmsnorm(x, g):
    xf = x.astype(jnp.float32)
    r = lax.rsqrt(jnp.mean(xf * xf, axis=-1, keepdims=True) + NORM_EPS)
    return (xf * r * g.astype(jnp.float32)).astype(x.dtype)


def _rotary(x, pos):
    half = x.shape[-1] // 2
    inv = ROPE_BASE ** (-jnp.arange(half, dtype=jnp.float32) / half)
    ang = pos[:, None] * inv[None, :]
    cos = jnp.cos(ang)[None, :, None, :]
    sin = jnp.sin(ang)[None, :, None, :]
    x1, x2 = x[..., :half], x[..., half:]
    return jnp.concatenate([x1 * cos - x2 * sin, x1 * sin + x2 * cos], axis=-1)


def _project(h, w_in, b_f):
    B, T = h.shape[0], h.shape[1]
    z = jnp.einsum('btd,dc->btc', h, w_in).astype(jnp.float32)
    sizes = (H_RET * DK_RET, H_RET * DK_RET, W_RET, W_RET, W_FOX, W_FOX, W_FOX)
    idx = []
    acc = 0
    for s in sizes:
        acc += s
        idx.append(acc)
    q_r, k_r, v_r, g_r, q_f, k_f, v_f, f_logit = jnp.split(z, idx, axis=-1)
    log_f = jax.nn.log_sigmoid(f_logit + b_f.astype(jnp.float32))
    return (q_r.reshape(B, T, H_RET, DK_RET), k_r.reshape(B, T, H_RET, DK_RET),
            v_r.reshape(B, T, H_RET, DV_RET), g_r,
            q_f.reshape(B, T, H_FOX, D_FOX), k_f.reshape(B, T, H_FOX, D_FOX),
            v_f.reshape(B, T, H_FOX, D_FOX), log_f)


def _mixer_inputs(x, pos, g_mix, w_in, b_f):
    h = _rmsnorm(x, g_mix)
    q_r, k_r, v_r, g_r, q_f, k_f, v_f, log_f = _project(h, w_in, b_f)
    q_r = _rotary(q_r, pos) * (DK_RET ** -0.5)
    k_r = _rotary(k_r, pos)
    return q_r, k_r, v_r, g_r, q_f, k_f, v_f, log_f


def _ret_log_gamma():
    return jnp.log1p(-jnp.exp2(-5.0 - jnp.arange(H_RET, dtype=jnp.float32)))


def _ret_intra(q, k, v):
    n = q.shape[-3]
    lg = _ret_log_gamma()
    i = jnp.arange(n, dtype=jnp.float32)
    dec = jnp.exp(lg[:, None, None] * jnp.abs(i[:, None] - i[None, :]))
    s = jnp.einsum('...ihd,...jhd->...hij', q, k) * dec
    return jnp.einsum('...hij,...jhe->...ihe', s, v)


def _ret_inter(q, s_prev):
    n = q.shape[-3]
    lg = _ret_log_gamma()
    i = jnp.arange(n, dtype=jnp.float32)
    qdec = jnp.exp(lg[None, :] * (i[:, None] + 1.0))
    return jnp.einsum('...ihd,...hde->...ihe', q, s_prev) * qdec[:, :, None]


def _ret_update(k, v):
    n = k.shape[-3]
    lg = _ret_log_gamma()
    i = jnp.arange(n, dtype=jnp.float32)
    kdec = jnp.exp(lg[None, :] * (n - 1.0 - i)[:, None])
    return jnp.einsum('...jhd,...jhe->...hde', k * kdec[:, :, None], v)


def _ret_chunk_decay(n):
    return jnp.exp(_ret_log_gamma() * float(n))[:, None, None]


def _retention_prompt(q, k, v):
    B, T = q.shape[0], q.shape[1]
    N = T // CHUNK
    qc = q.reshape(B, N, CHUNK, H_RET, DK_RET)
    kc = k.reshape(B, N, CHUNK, H_RET, DK_RET)
    vc = v.reshape(B, N, CHUNK, H_RET, DV_RET)
    intra = _ret_intra(qc, kc, vc)
    u = _ret_update(kc, vc)
    cdec = _ret_chunk_decay(CHUNK)

    def step(s, u_c):
        return cdec * s + u_c, s

    s0 = jnp.zeros((B, H_RET, DK_RET, DV_RET), jnp.float32)
    s_fin, s_prev = lax.scan(step, s0, jnp.moveaxis(u, 1, 0))
    inter = _ret_inter(qc, jnp.moveaxis(s_prev, 0, 1))
    o = (intra + inter).reshape(B, T, H_RET, DV_RET)
    return o, s_fin


def _retention_sample(q, k, v, s):
    n = q.shape[1]
    o = _ret_intra(q, k, v) + _ret_inter(q, s)
    s_new = _ret_chunk_decay(n) * s + _ret_update(k, v)
    return o, s_new


def _ret_output(o, g_r, gn_gain):
    B, T = o.shape[0], o.shape[1]
    mu = jnp.mean(o, axis=-1, keepdims=True)
    var = jnp.mean(jnp.square(o - mu), axis=-1, keepdims=True)
    on = ((o - mu) * lax.rsqrt(var + GN_EPS)).reshape(B, T, W_RET)
    return on * gn_gain.astype(jnp.float32) * jax.nn.silu(g_r)


def _fox_prompt(q, k, v, log_f):
    T = q.shape[1]
    c = lax.cumsum(log_f, axis=1)
    c_h = jnp.moveaxis(c, 2, 1)
    scale = D_FOX ** -0.5
    outs = []
    for b0 in range(0, T, Q_BLOCK):
        e = b0 + Q_BLOCK
        s = jnp.einsum('bqhd,bkhd->bhqk', q[:, b0:e], k[:, :e]) * scale
        s = s + c_h[:, :, b0:e, None] - c_h[:, :, None, :e]
        causal = (b0 + jnp.arange(Q_BLOCK))[:, None] >= jnp.arange(e)[None, :]
        p = jax.nn.softmax(jnp.where(causal, s, NEG_INF), axis=-1)
        outs.append(jnp.einsum('bhqk,bkhd->bqhd', p, v[:, :e]))
    return jnp.concatenate(outs, axis=1)


def _fox_sample(q, k, v, log_f, cache_k, cache_v, cache_logf):
    n = q.shape[1]
    ck = cache_k.astype(jnp.float32)
    cv = cache_v.astype(jnp.float32)
    clf = cache_logf.astype(jnp.float32)
    P = ck.shape[1]
    scale = D_FOX ** -0.5
    a = lax.cumsum(clf, axis=1, reverse=True) - clf
    bq = lax.cumsum(log_f, axis=1)
    a_h = jnp.moveaxis(a, 2, 1)
    bq_h = jnp.moveaxis(bq, 2, 1)
    s_past = jnp.einsum('bqhd,bkhd->bhqk', q, ck) * scale + bq_h[:, :, :, None] + a_h[:, :, None, :]
    s_new = jnp.einsum('bqhd,bkhd->bhqk', q, k) * scale + bq_h[:, :, :, None] - bq_h[:, :, None, :]
    causal = jnp.arange(n)[:, None] >= jnp.arange(n)[None, :]
    s_new = jnp.where(causal, s_new, NEG_INF)
    p = jax.nn.softmax(jnp.concatenate([s_past, s_new], axis=-1), axis=-1)
    return (jnp.einsum('bhqk,bkhd->bqhd', p[..., :P], cv)
            + jnp.einsum('bhqk,bkhd->bqhd', p[..., P:], v))


def _swiglu(h, w_gate, w_up, w_down):
    g = jnp.einsum('btd,df->btf', h, w_gate)
    u = jnp.einsum('btd,df->btf', h, w_up)
    return jnp.einsum('btf,fd->btd', jax.nn.silu(g) * u, w_down)


def _merge_and_ffn(x, o_r, g_r, o_f, gn_gain, w_out, g_ffn, w_gate, w_up, w_down):
    B, T = x.shape[0], x.shape[1]
    mix = jnp.concatenate([_ret_output(o_r, g_r, gn_gain), o_f.reshape(B, T, W_FOX)], axis=-1)
    x = x + jnp.einsum('btc,cd->btd', mix, w_out.astype(jnp.float32)).astype(x.dtype)
    x = x + _swiglu(_rmsnorm(x, g_ffn), w_gate, w_up, w_down).astype(x.dtype)
    return x


def setup_inputs(seed: int = 0) -> dict:
    key = jax.random.key(seed)
    ks = jax.random.split(key, 20)
    f32 = jnp.float32
    nrm = lambda k, shape, s: jax.random.normal(k, shape, f32) * s
    return {
        'x_prompt': nrm(ks[0], (BATCH, SEQ, D_MODEL), 1.0),
        'x_sample': nrm(ks[1], (DEC_BATCH, DEC_SEQ, D_MODEL), 1.0),
        'cache_fox_k': nrm(ks[2], (DEPTH, DEC_BATCH, PAST_LEN, H_FOX, D_FOX), 1.0),
        'cache_fox_v': nrm(ks[3], (DEPTH, DEC_BATCH, PAST_LEN, H_FOX, D_FOX), 1.0),
        'cache_fox_logf': jax.nn.log_sigmoid(3.0 + nrm(ks[4], (DEPTH, DEC_BATCH, PAST_LEN, H_FOX), 1.0)),
        'state_ret': nrm(ks[5], (DEPTH, DEC_BATCH, H_RET, DK_RET, DV_RET), 2.0),
        'w_in': nrm(ks[6], (DEPTH, D_MODEL, D_IN), D_MODEL ** -0.5),
        'b_forget': 3.0 + nrm(ks[7], (DEPTH, H_FOX), 0.5),
        'ret_gn_gain': 1.0 + nrm(ks[8], (DEPTH, W_RET), 0.01),
        'w_out': nrm(ks[9], (DEPTH, D_MIX, D_MODEL), D_MIX ** -0.5),
        'norm_mix_gain': 1.0 + nrm(ks[10], (DEPTH, D_MODEL), 0.01),
        'norm_ffn_gain': 1.0 + nrm(ks[11], (DEPTH, D_MODEL), 0.01),
        'w_gate': nrm(ks[12], (DEPTH, D_MODEL, D_FF), D_MODEL ** -0.5),
        'w_up': nrm(ks[13], (DEPTH, D_MODEL, D_FF), D_MODEL ** -0.5),
        'w_down': nrm(ks[14], (DEPTH, D_FF, D_MODEL), D_FF ** -0.5),
        'norm_final_gain': 1.0 + nrm(ks[15], (D_MODEL,), 0.01),
    }


def reference(x_prompt, x_sample, cache_fox_k, cache_fox_v, cache_fox_logf, state_ret,
              w_in, b_forget, ret_gn_gain, w_out, norm_mix_gain, norm_ffn_gain,
              w_gate, w_up, w_down, norm_final_gain):
    xp, xs = x_prompt, x_sample
    T = xp.shape[1]
    n = xs.shape[1]
    past_len = cache_fox_k.shape[2]
    pos_p = jnp.arange(T, dtype=jnp.float32)
    pos_s = jnp.arange(n, dtype=jnp.float32) + float(past_len)
    sr_p, kf_p, vf_p, lf_p = [], [], [], []
    sr_s, kf_s, vf_s, lf_s = [], [], [], []
    for l in range(DEPTH):
        q_r, k_r, v_r, g_r, q_f, k_f, v_f, lf = _mixer_inputs(xp, pos_p, norm_mix_gain[l], w_in[l], b_forget[l])
        o_r, s_ret = _retention_prompt(q_r, k_r, v_r)
        o_f = _fox_prompt(q_f, k_f, v_f, lf)
        xp = _merge_and_ffn(xp, o_r, g_r, o_f, ret_gn_gain[l], w_out[l], norm_ffn_gain[l],
                            w_gate[l], w_up[l], w_down[l])
        sr_p.append(s_ret)
        kf_p.append(k_f.astype(x_prompt.dtype))
        vf_p.append(v_f.astype(x_prompt.dtype))
        lf_p.append(lf)
        q_r, k_r, v_r, g_r, q_f, k_f, v_f, lf = _mixer_inputs(xs, pos_s, norm_mix_gain[l], w_in[l], b_forget[l])
        o_r, s_ret = _retention_sample(q_r, k_r, v_r, state_ret[l].astype(jnp.float32))
        o_f = _fox_sample(q_f, k_f, v_f, lf, cache_fox_k[l], cache_fox_v[l], cache_fox_logf[l])
        xs = _merge_and_ffn(xs, o_r, g_r, o_f, ret_gn_gain[l], w_out[l], norm_ffn_gain[l],
                            w_gate[l], w_up[l], w_down[l])
        sr_s.append(s_ret)
        kf_s.append(k_f.astype(x_sample.dtype))
        vf_s.append(v_f.astype(x_sample.dtype))
        lf_s.append(lf)
    y_prompt = _rmsnorm(xp, norm_final_gain)
    y_sample = _rmsnorm(xs, norm_final_gain)
    return (y_prompt, y_sample,
            jnp.stack(sr_p, 0), jnp.stack(kf_p, 0), jnp.stack(vf_p, 0), jnp.stack(lf_p, 0),
            jnp.stack(sr_s, 0), jnp.stack(kf_s, 0), jnp.stack(vf_s, 0), jnp.stack(lf_s, 0))
```

```python
import numpy as np
from contextlib import ExitStack
import concourse.bass as bass
import concourse.mybir as mybir
from concourse.bass_utils import run_bass_kernel_spmd

F32 = mybir.dt.float32
BF16 = mybir.dt.bfloat16
AF = mybir.ActivationFunctionType
ALU = mybir.AluOpType

NCORES = 8
D = 1024
DIN = 3592
DFF = 2816
NFC = DFF // 128
T = 2048
NS = 64
PAST = 4096
NKT = PAST // 128
G = 4
ENGINES = ("pe", "act", "dve", "pool", "sp")


class Op:
    __slots__ = ("eng", "fn", "reads", "writes", "gidx", "eidx", "dma_sem", "dma_val",
                 "waits", "signal", "count", "clock")

    def __init__(self, eng, fn, reads, writes, dma_sem=None):
        self.eng = eng
        self.fn = fn
        self.reads = reads
        self.writes = writes
        self.dma_sem = dma_sem
        self.dma_val = 0
        self.waits = []
        self.signal = False
        self.count = 0
        self.clock = None


class Sched:
    def __init__(self, nc, same_engine_dist=10 ** 9):
        self.nc = nc
        self.ops = []
        self.by_eng = {e: [] for e in ENGINES}
        self.last_writer = {}
        self.readers = {}
        self.dma_cum = {}
        self.observed = {}
        self.same_engine_dist = same_engine_dist
        self.bar_ops = []
        self.bank_last = {}

    def op(self, eng, fn, reads=(), writes=()):
        o = Op(eng, fn, tuple(reads), tuple(writes))
        self._add(o)
        return o

    def dma(self, eng, fn, reads=(), writes=(), sem=None):
        o = Op(eng, fn, tuple(reads), tuple(writes), dma_sem=sem)
        self.dma_cum.setdefault(sem, 0)
        self._add(o)
        self.dma_cum[sem] += 16
        o.dma_val = self.dma_cum[sem]
        return o

    def _dep_on(self, o, d, obs):
        if d.dma_sem is not None:
            key = ("dma", d.dma_sem)
            val = self.dma_cum[d.dma_sem]
            if obs.get(key, 0) >= val:
                return
            obs[key] = val
            o.waits.append(("dma", d.dma_sem, val))
        else:
            key = ("eng", d.eng)
            if obs.get(key, -1) >= d.eidx:
                return
            o.waits.append(("eng", d.eng, d))
            d.signal = True
            for k, v in d.clock.items():
                if obs.get(k, -1) < v:
                    obs[k] = v

    def _add(self, o, force_deps=()):
        o.gidx = len(self.ops)
        o.eidx = len(self.by_eng[o.eng])
        deps = set(self.bar_ops)
        for t in o.reads:
            w = self.last_writer.get(t)
            if w is not None:
                deps.add(w)
        bank_toks = []
        for t in o.writes:
            if isinstance(t, str) and len(t) == 2 and t[0] == "B":
                bank_toks.append(t)
                for eng2, last in self.bank_last.setdefault(t, {}).items():
                    if eng2 != o.eng:
                        deps.add(last)
                continue
            w = self.last_writer.get(t)
            if w is not None:
                deps.add(w)
            for r in self.readers.get(t, ()):
                deps.add(r)
        deps.discard(o)
        obs = self.observed.setdefault(o.eng, {})
        for d in sorted(deps, key=lambda x: x.gidx):
            if d.dma_sem is None and d.eng == o.eng:
                if o.eng == "pe":
                    continue
                if o.dma_sem is None and o.eidx - d.eidx > self.same_engine_dist:
                    continue
            self._dep_on(o, d, obs)
        for d in force_deps:
            self._dep_on(o, d, obs)
        if o.dma_sem is None:
            clock = dict(obs)
            clock[("eng", o.eng)] = o.eidx
            o.clock = clock
        for t in o.reads:
            self.readers.setdefault(t, []).append(o)
        for t in o.writes:
            if t in bank_toks:
                self.bank_last[t][o.eng] = o
                continue
            self.last_writer[t] = o
            self.readers[t] = []
        self.ops.append(o)
        self.by_eng[o.eng].append(o)

    def barrier(self, fns):
        new = []
        for e, fn in fns.items():
            o = Op(e, fn, (), ())
            force = []
            if self.by_eng["pe"]:
                force.append(self.by_eng["pe"][-1])
            for other in fns:
                if other != e and self.by_eng[other]:
                    force.append(self.by_eng[other][-1])
            obs = self.observed.setdefault(e, {})
            for key, val in self.dma_cum.items():
                if obs.get(("dma", key), 0) < val:
                    obs[("dma", key)] = val
                    o.waits.append(("dma", key, val))
            self._add(o, force_deps=force)
            new.append(o)
        self.bar_ops = new
        self.last_writer = {}
        self.readers = {}
        self.bank_last = {}

    def emit(self, final_wait_sems=()):
        nc = self.nc
        with ExitStack() as st:
            esem = {e: st.enter_context(nc.semaphore("s_" + e)) for e in ENGINES}
            dsem = {k: st.enter_context(nc.semaphore("d_%s" % (str(k),))) for k in self.dma_cum}
            for e in ENGINES:
                c = 0
                for o in self.by_eng[e]:
                    if o.dma_sem is None and o.signal:
                        c += 1
                        o.count = c
            block = st.enter_context(nc.Block())

            def run(e, eng):
                for o in self.by_eng[e]:
                    for w in o.waits:
                        if w[0] == "dma":
                            eng.wait_ge(dsem[w[1]], w[2])
                        else:
                            eng.wait_ge(esem[w[1]], w[2].count)
                    inst = o.fn(eng)
                    if o.dma_sem is not None:
                        inst.then_inc(dsem[o.dma_sem], 16)
                    elif o.signal:
                        inst.then_inc(esem[e], 1)
                if e == "sp":
                    for k in final_wait_sems:
                        eng.wait_ge(dsem[k], self.dma_cum[k])

            @block.tensor
            def _(eng):
                run("pe", eng)

            @block.scalar
            def _(eng):
                run("act", eng)

            @block.vector
            def _(eng):
                run("dve", eng)

            @block.gpsimd
            def _(eng):
                run("pool", eng)

            @block.sync
            def _(eng):
                run("sp", eng)


class _Rec:
    def __getattr__(self, name):
        def mk(*a, **kw):
            return lambda eng: getattr(eng, name)(*a, **kw)
        return mk


_E = _Rec()


class _Stop(Exception):
    pass


def stop_at(name):
    import os
    if os.environ.get("MK_STOP", "") == name:
        raise _Stop(name)


class Arena:
    def __init__(self, t, nwords):
        self.t = t
        self.n = nwords
        self.off = 0

    def f32(self, ncols):
        assert self.off + ncols <= self.n, ("arena overflow", self.off, ncols, self.n)
        ap = self.t[:, self.off:self.off + ncols]
        self.off += ncols
        return ap

    def bf16(self, ncols):
        w = (ncols + 1) // 2
        ap = self.f32(w).bitcast(BF16)
        return ap[:, 0:ncols]


CF_LAYOUT = {}


def _cf_layout():
    off = 0
    for name, n in (("ident", 128), ("U", 128), ("ones", 128), ("Lst", 128), ("negm", 128),
                    ("Mp", 512), ("qd", 512), ("kdp", 4), ("kds", 4), ("coss", 64), ("sins", 64),
                    ("mhalf", 4), ("mone", 4)):
        CF_LAYOUT[name] = (off, n)
        off += n
    return off


CF_N = _cf_layout()


def _host_consts():
    lg = np.log1p(-np.exp2(-5.0 - np.arange(4, dtype=np.float32))).astype(np.float64)
    sc = 128.0 ** -0.5
    i = np.arange(128)
    cf = np.zeros((128, CF_N), np.float32)

    def put(name, arr):
        o, n = CF_LAYOUT[name]
        cf[:arr.shape[0], o:o + n] = arr.reshape(arr.shape[0], n)

    put("ident", np.eye(128, dtype=np.float32))
    put("U", (i[:, None] <= i[None, :]).astype(np.float32))
    put("ones", np.ones((128, 128), np.float32))
    put("Lst", (i[:, None] > i[None, :]).astype(np.float32))
    put("negm", np.where(i[None, :] >= i[:, None], 0.0, -30000.0).astype(np.float32))
    diff = (i[None, :] - i[:, None]).astype(np.float64)
    same = (i[:, None] // 64) == (i[None, :] // 64)
    fwd = (i[:, None] < 64) & (i[None, :] >= 64)
    Mp = np.zeros((128, 4, 128), np.float64)
    qd = np.zeros((128, 4, 128), np.float64)
    for h in range(4):
        Mp[:, h, :] = np.where(same, np.exp(lg[h] * np.abs(diff)),
                               np.where(fwd, np.exp(lg[h] * diff), 0.0)) * sc
        qd[:, h, :] = (sc * np.exp(lg[h] * (i + 1.0)))[None, :]
    put("Mp", Mp.astype(np.float32))
    put("qd", qd.astype(np.float32))
    put("kdp", np.exp(lg[None, :] * (127.0 - i[:, None])).astype(np.float32))
    kds = np.exp(lg[None, :] * (63.0 - i[:64, None])).astype(np.float32)
    put("kds", kds)
    inv = (10000.0 ** (-np.arange(64, dtype=np.float32) / 64.0)).astype(np.float32)
    pos = np.arange(T, dtype=np.float32)
    ang = (pos[:, None] * inv[None, :]).astype(np.float32)
    cosp = np.cos(ang).astype(np.float32).reshape(T // 128, 128, 64)
    sinp = np.sin(ang).astype(np.float32).reshape(T // 128, 128, 64)
    poss = np.arange(NS, dtype=np.float32) + float(PAST)
    angs = (poss[:, None] * inv[None, :]).astype(np.float32)
    put("coss", np.cos(angs).astype(np.float32))
    put("sins", np.sin(angs).astype(np.float32))
    put("mhalf", np.full((128, 4), -0.5, np.float32))
    put("mone", np.full((128, 4), -1.0, np.float32))
    g128 = [float(np.exp(lg[h] * 128.0)) for h in range(4)]
    g64 = [float(np.exp(lg[h] * 64.0)) for h in range(4)]
    return cf, cosp, sinp, g128, g64


def build_program(g128, g64):
    nc = bass.Bass("TRN2", target_bir_lowering=False)

    def din(name, shape):
        return nc.dram_tensor(name, list(shape), F32, kind="ExternalInput").ap()

    def dout(name, shape):
        return nc.dram_tensor(name, list(shape), F32, kind="ExternalOutput").ap()

    xp = din("xp", (2, T, D))
    xs_d = din("xs", (2, NS, D))
    ck = din("ck", (2, PAST, 512))
    cv = din("cv", (2, PAST, 512))
    clf = din("clf", (2, PAST, 8))
    sret = din("sret", (2, 4, 128, 128))
    w_in = din("w_in", (D, DIN))
    w_out = din("w_out", (D, D))
    w_gate = din("w_gate", (D, DFF))
    w_up = din("w_up", (D, DFF))
    w_down = din("w_down", (DFF, D))
    cf_d = din("cf32", (128, CF_N))
    cosp_d = din("cosp", (T // 128, 128, 64))
    sinp_d = din("sinp", (T // 128, 128, 64))
    vec_d = din("vecs", (128, 8 + 8 + 4 + 8))
    gfin_d = din("gfin_b", (128, D))

    y_p = dout("y_p", (2 * T, D))
    y_s = dout("y_s", (2 * NS, D))
    sr_p = dout("sr_p", (2, 4, 128, 128))
    kf_p = dout("kf_p", (2, T, 512))
    vf_p = dout("vf_p", (2, T, 512))
    lf_p = dout("lf_p", (2, T, 8))
    sr_s = dout("sr_s", (2, 4, 128, 128))
    kf_s = dout("kf_s", (2, NS, 512))
    vf_s = dout("vf_s", (2, NS, 512))
    lf_s = dout("lf_s", (2, NS, 8))
    NTOK = 2 * T + 2 * NS
    x1s = nc.dram_tensor("x1s", [NTOK, D], F32, kind="Internal").ap()

    with ExitStack() as st:
        AW = 52600
        arena_t = st.enter_context(nc.sbuf_tensor("arena", [128, AW], F32))
        ps = st.enter_context(nc.psum_tensor("ps", [128, 4096], F32))
        S = Sched(nc)
        A = Arena(arena_t, AW)

        def B(i):
            return ps[:, i * 512:(i + 1) * 512]

        def Bt(i):
            return "B%d" % i

        try:
            cf = A.f32(CF_N)
            vecs = A.f32(28)
            identb = A.bf16(128)
            negmb = A.bf16(128)
            dummy = A.f32(4)

            def C(name, rows=128):
                o, n = CF_LAYOUT[name]
                return cf[0:rows, o:o + n]

            gmixc = vecs[:, 0:8]
            gffnc = vecs[:, 8:16]
            gnc = vecs[:, 16:20]
            bfb = vecs[:, 20:28]

            S.dma("sp", _E.dma_start(out=cf, in_=cf_d), writes=["cf"], sem="cf")
            S.dma("sp", _E.dma_start(out=vecs, in_=vec_d), writes=["vecs"], sem="cf")
            S.op("pool", _E.tensor_copy(out=identb, in_=C("ident")), reads=["cf"], writes=["identb"])
            S.op("pool", _E.tensor_copy(out=negmb, in_=C("negm")), reads=["cf"], writes=["negmb"])

            def bar():
                S.barrier({
                    "act": _E.copy(out=dummy[:, 0:1], in_=dummy[:, 1:2]),
                    "dve": _E.memset(dummy[:, 2:3], 0.0),
                    "pool": _E.memset(dummy[:, 3:4], 0.0),
                })

            S.op("pool", _E.memset(dummy, 0.0), writes=["dummy"])

            mark_persist = A.off
            stop_at("pro0")

            w_in_bf = A.bf16(8 * DIN).rearrange("p (k c) -> p k c", k=8)
            w_out_bf = A.bf16(8 * D).rearrange("p (k c) -> p k c", k=8)
            mark_w1 = A.off
            stage = [A.f32(DIN), A.f32(DIN)]
            A.off = mark_w1
            xt = A.f32(G * D).rearrange("p (j c) -> p j c", j=G)
            xsb = [A.bf16(D), A.bf16(D)]
            hT = A.bf16(8 * G * 128).rearrange("p (k t) -> p k t", k=8)
            QB = A.bf16(4 * G * 2 * 128).rearrange("p (c j e t) -> p c j e t", c=4, j=G, e=2)
            kfT = A.bf16(4 * T).rearrange("p (c t) -> p c t", c=4)
            vaug_off = A.off
            Vaug = A.bf16(16 * 8 * 65).rearrange("p (t h e) -> p t h e", t=16, h=8)
            qkrot = [A.bf16(1024), A.bf16(1024)]
            rt = [A.f32(512) for _ in range(4)]
            vtok = [A.bf16(512), A.bf16(512)]
            gsb1 = A.f32(512)
            egs1 = A.f32(512)
            gsb = [gsb1, gsb1]
            egs = [egs1, egs1]
            onb = [rt[2], rt[3]]
            kfo = [A.f32(512), A.f32(512)]
            vfo = [A.f32(512), A.f32(512)]
            qkT = [A.bf16(1024).rearrange("p (h t) -> p h t", h=8) for _ in range(2)]
            qdT = [A.bf16(512).rearrange("p (h t) -> p h t", h=4) for _ in range(2)]
            kdtok = [A.bf16(512), A.bf16(512)]
            sTm = [A.bf16(512).rearrange("p (h t) -> p h t", h=4) for _ in range(2)]
            Sst = A.f32(512).rearrange("p (h e) -> p h e", h=4)
            Sbf = A.bf16(512).rearrange("p (h e) -> p h e", h=4)
            NPT = 12
            PT = [A.bf16(128) for _ in range(NPT)]
            mixb = [A.bf16(1024), A.bf16(1024)]
            mixT = [A.bf16(1024).rearrange("p (k t) -> p k t", k=8) for _ in range(2)]
            ctab = A.f32(128).rearrange("p (t h) -> p t h", h=8)
            Btab = [A.f32(128).rearrange("p (t h) -> p t h", h=8) for _ in range(2)]
            Rcar = A.f32(8)
            cst = [A.f32(128), A.f32(128)]
            sm = A.f32(128)
            clfs = A.f32(256).rearrange("p (t h) -> p t h", h=8)
            atab = A.f32(256).rearrange("p (t h) -> p t h", h=8)
            tots = A.f32(256).rearrange("p (t h) -> p t h", h=8)
            sufs = A.f32(256).rearrange("p (t h) -> p t h", h=8)
            negbq = A.f32(8)
            _save = A.off
            A.off = vaug_off + 8 * 65
            kst = [A.f32(512), A.f32(512)]
            vst = [A.f32(512), A.f32(512)]
            kTt = [A.bf16(512).rearrange("p (c t) -> p c t", c=4) for _ in range(2)]
            Vt = [A.bf16(8 * 65).rearrange("p (h e) -> p h e", h=8) for _ in range(2)]
            PTs = [A.bf16(512).rearrange("p (h q) -> p h q", h=8) for _ in range(2)]
            assert A.off <= vaug_off + 8 * 65 * 8, (A.off, vaug_off)
            A.off = _save
            print("phase1 arena words used:", A.off, "of", AW)

            ss = sm[:, 0:G]
            rs = sm[:, 8:8 + G]
            st6 = sm[:, 16:40].rearrange("p (h s) -> p h s", h=4)
            mv = sm[:, 40:48].rearrange("p (h s) -> p h s", h=4)
            rstdg = sm[:, 48:52]
            rcp = sm[:, 56:64]
            xl = sm[:, 64:72]
            la = sm[:, 72:80]
            le = sm[:, 80:88]
            ll = sm[:, 88:96]
            lmn = sm[:, 96:104]
            lft = [sm[:, 104:112], sm[:, 112:120]]

            mhalf = C("mhalf")

            def convert_rows(dst, src_sb, ncols, scale_col, engines=("act", "dve", "pool"), rd=()):
                shares = {"act": 0.44, "dve": 0.44, "pool": 0.12}
                bounds = [0]
                acc_ = 0.0
                for eng in engines:
                    acc_ += shares[eng]
                    bounds.append(min(ncols, int(round(ncols * acc_ / 8.0)) * 8))
                bounds[-1] = ncols
                for i, eng in enumerate(engines):
                    c0, c1 = bounds[i], bounds[i + 1]
                    if c0 >= c1:
                        continue
                    if scale_col is None:
                        if eng == "act":
                            S.op("act", _E.copy(out=dst[:, c0:c1], in_=src_sb[:, c0:c1]), reads=rd)
                        else:
                            S.op(eng, _E.tensor_copy(out=dst[:, c0:c1], in_=src_sb[:, c0:c1]), reads=rd)
                    else:
                        if eng == "act":
                            S.op("act", _E.activation(out=dst[:, c0:c1], in_=src_sb[:, c0:c1],
                                                                           func=AF.Identity, scale=scale_col), reads=rd)
                        else:
                            S.op(eng, _E.tensor_scalar(out=dst[:, c0:c1], in0=src_sb[:, c0:c1],
                                                                            scalar1=scale_col, scalar2=None,
                                                                            op0=ALU.mult), reads=rd)

            for k in range(8):
                sg = stage[k % 2]
                tok = "stage%d" % (k % 2)
                S.dma("sp", _E.dma_start(out=sg[:, 0:DIN], in_=w_in[k * 128:(k + 1) * 128, :]),
                      writes=[tok], sem=tok)
                convert_rows(w_in_bf[:, k, :], sg, DIN, gmixc[:, k:k + 1], rd=[tok, "vecs"])
            for k in range(8):
                sg = stage[k % 2]
                tok = "stage%d" % (k % 2)
                S.dma("sp", _E.dma_start(out=sg[:, 0:D], in_=w_out[k * 128:(k + 1) * 128, :]),
                      writes=[tok], sem=tok)
                convert_rows(w_out_bf[:, k, :], sg, D, gnc[:, k:k + 1] if k < 4 else None, rd=[tok, "vecs"])
            bar()
            stop_at("pro")

            tpb = B(0).bitcast(BF16)
            tp3 = tpb.rearrange("p (k t) -> p k t", k=8)
            tpf3 = B(0).rearrange("p (k t) -> p k t", k=4)
            cnt = {"tile": 0, "pt": 0, "sl": 0, "ev": 0}

            def load_x(src_rows, n, j):
                S.dma("sp", _E.dma_start(out=xt[0:n, j, :], in_=src_rows), writes=["xt%d" % j], sem="xt%d" % j)

            def stage_A(n, j):
                b = cnt["tile"] % 2
                S.op("pool", _E.memset(ss[0:n, j:j + 1], 0.0), writes=["ss%d" % j])
                S.op("act", _E.activation(out=xsb[b][0:n, :], in_=xt[0:n, j, :], func=AF.Square,
                                                   accum_out=ss[0:n, j:j + 1]),
                     reads=["xt%d" % j], writes=["xsb%d" % b, "ss%d" % j])
                S.op("dve", _E.tensor_scalar(out=rs[0:n, j:j + 1], in0=ss[0:n, j:j + 1], scalar1=1.0 / D,
                                                      scalar2=1e-6, op0=ALU.mult, op1=ALU.add),
                     reads=["ss%d" % j], writes=["rs%d" % j])
                S.op("pool", _E.tensor_tensor(out=rs[0:n, j:j + 1], in0=rs[0:n, j:j + 1], in1=mhalf[0:n, 0:1],
                                                       op=ALU.pow), reads=["rs%d" % j], writes=["rs%d" % j])
                S.op("dve", _E.tensor_scalar(out=xsb[b][0:n, :], in0=xt[0:n, j, :], scalar1=rs[0:n, j:j + 1],
                                                      scalar2=None, op0=ALU.mult),
                     reads=["xt%d" % j, "rs%d" % j], writes=["xsb%d" % b])
                for k in range(8):
                    S.op("pe", _E.transpose(out=tpb[:, k * 128:k * 128 + n],
                                                          in_=xsb[b][0:n, k * 128:(k + 1) * 128],
                                                          identity=identb[0:n, 0:n]),
                         reads=["xsb%d" % b], writes=[Bt(0)])
                S.op("act", _E.copy(out=hT[:, :, j * 128:j * 128 + n], in_=tp3[:, :, 0:n]),
                     writes=[Bt(0), "hT%d" % j])
                cnt["tile"] += 1

            def evac_copy(dst, src, btok, wtoks):
                eng = "act" if cnt["ev"] % 2 == 0 else "dve"
                cnt["ev"] += 1
                if eng == "act":
                    S.op("act", _E.copy(out=dst, in_=src), writes=[btok] + wtoks)
                else:
                    S.op("dve", _E.tensor_copy(out=dst, in_=src), writes=[btok] + wtoks)

            def stage_B(NT, ntl, tok0):
                hts = ["hT%d" % j for j in range(ntl)]
                for c in range(8):
                    col0 = 2048 + c * 128 if c < 4 else 2560 + (c - 4) * 128
                    bk = 1 + (c % 4)
                    for k in range(8):
                        S.op("pe", _E.matmul(
                            B(bk)[:, 0:NT], lhsT=w_in_bf[:, k, col0:col0 + 128], rhs=hT[:, k, 0:NT],
                            start=(k == 0), stop=(k == 7)), reads=hts, writes=[Bt(bk)])
                    if c < 4:
                        tw = min(NT, 128)
                        for e_ in range(2):
                            r0_, r1_ = e_ * 64, (e_ + 1) * 64
                            evac_copy(QB[r0_:r1_, c, 0:ntl, e_, 0:tw],
                                      B(bk)[r0_:r1_, 0:NT].rearrange("p (j t) -> p j t", t=tw),
                                      Bt(bk), ["qfT%d_%d" % (c, e_)])
                    else:
                        evac_copy(kfT[:, c - 4, tok0:tok0 + NT], B(bk)[:, 0:NT], Bt(bk), ["kfT%d" % (c - 4)])

            def inproj_tok(n, j, bk, col0, ncols):
                for k in range(8):
                    S.op("pe", _E.matmul(B(bk)[0:n, 0:ncols], lhsT=hT[:, k, j * 128:j * 128 + n],
                                                       rhs=w_in_bf[:, k, col0:col0 + ncols],
                                                       start=(k == 0), stop=(k == 7)),
                         reads=["hT%d" % j], writes=[Bt(bk)])

            def stage_C(n, j, b, cos_ap, sin_ap, cs_tok, kt, kf_dst, vf_dst, lf_dst, vaug_dst):
                inproj_tok(n, j, 1, 0, 512)
                yield
                inproj_tok(n, j, 2, 512, 512)
                yield
                inproj_tok(n, j, 3, 1024, 512)
                S.op("act", _E.copy(out=vtok[b][0:n, :], in_=B(3)[0:n, :]), writes=[Bt(3), "vtok%d" % b])
                yield
                qk4 = ps[0:n, 512:1536].rearrange("p (h t f) -> p h t f", t=2, f=64)
                x1 = qk4[:, :, 0, :]
                x2 = qk4[:, :, 1, :]
                cosb = cos_ap.unsqueeze(1).to_broadcast([n, 8, 64])
                sinb = sin_ap.unsqueeze(1).to_broadcast([n, 8, 64])
                r3 = [r[0:n, :].rearrange("p (h f) -> p h f", f=64) for r in rt]
                qr4 = qkrot[b][0:n, :].rearrange("p (h t f) -> p h t f", t=2, f=64)
                bb = [Bt(1), Bt(2)]
                S.op("dve", _E.tensor_tensor(out=r3[0], in0=x1, in1=cosb, op=ALU.mult), reads=[cs_tok], writes=bb + ["rt0"])
                S.op("dve", _E.tensor_tensor(out=r3[1], in0=x2, in1=sinb, op=ALU.mult), reads=[cs_tok], writes=bb + ["rt1"])
                S.op("pool", _E.tensor_tensor(out=qr4[:, :, 0, :], in0=r3[0], in1=r3[1], op=ALU.subtract),
                     reads=["rt0", "rt1"], writes=["qkrot%da" % b])
                S.op("dve", _E.tensor_tensor(out=r3[2], in0=x1, in1=sinb, op=ALU.mult), reads=[cs_tok], writes=bb + ["rt2"])
                S.op("dve", _E.tensor_tensor(out=r3[3], in0=x2, in1=cosb, op=ALU.mult), reads=[cs_tok], writes=bb + ["rt3"])
                S.op("pool", _E.tensor_tensor(out=qr4[:, :, 1, :], in0=r3[2], in1=r3[3], op=ALU.add),
                     reads=["rt2", "rt3"], writes=["qkrot%db" % b])
                inproj_tok(n, j, 3, 1536, 512)
                S.op("act", _E.copy(out=gsb[b][0:n, :], in_=B(3)[0:n, :]), writes=[Bt(3), "gsb"])
                S.op("act", _E.activation(out=egs[b][0:n, :], in_=B(3)[0:n, :], func=AF.Exp, scale=-1.0),
                     writes=[Bt(3), "egs"])
                yield
                S.op("act", _E.activation(out=egs[b][0:n, :], in_=egs[b][0:n, :], func=AF.Ln, bias=1.0),
                     reads=["egs"], writes=["egs"])
                S.op("act", _E.activation(out=egs[b][0:n, :], in_=egs[b][0:n, :], func=AF.Exp, scale=-1.0),
                     reads=["egs"], writes=["egs"])
                yield
                inproj_tok(n, j, 1, 2560, 512)
                yield
                inproj_tok(n, j, 2, 3072, 512)
                yield
                inproj_tok(n, j, 3, 3584, 8)
                S.op("act", _E.copy(out=kfo[b][0:n, :], in_=B(1)[0:n, :]), writes=[Bt(1), "kfo%d" % b])
                S.op("act", _E.copy(out=vfo[b][0:n, :], in_=B(2)[0:n, :]), writes=[Bt(2), "vfo%d" % b])
                S.dma("sp", _E.dma_start(out=kf_dst, in_=kfo[b][0:n, :]), reads=["kfo%d" % b], sem="kfo%d" % b)
                S.dma("sp", _E.dma_start(out=vf_dst, in_=vfo[b][0:n, :]), reads=["vfo%d" % b], sem="vfo%d" % b)
                S.op("pool", _E.tensor_copy(out=vaug_dst, in_=vfo[b][0:n, :].rearrange("p (h f) -> p h f", f=64)),
                     reads=["vfo%d" % b], writes=["vaug"])
                S.op("dve", _E.tensor_tensor(out=xl[0:n, :], in0=B(3)[0:n, 0:8], in1=bfb[0:n, :], op=ALU.add),
                     writes=[Bt(3), "xl"])
                S.op("act", _E.activation(out=la[0:n, :], in_=xl[0:n, :], func=AF.Abs), reads=["xl"], writes=["la"])
                S.op("act", _E.activation(out=le[0:n, :], in_=la[0:n, :], func=AF.Exp, scale=-1.0),
                     reads=["la"], writes=["le"])
                S.op("act", _E.activation(out=ll[0:n, :], in_=le[0:n, :], func=AF.Ln, bias=1.0),
                     reads=["le"], writes=["ll"])
                S.op("dve", _E.tensor_scalar(out=lmn[0:n, :], in0=xl[0:n, :], scalar1=0.0, scalar2=None,
                                                      op0=ALU.min), reads=["xl"], writes=["lmn"])
                S.op("dve", _E.tensor_tensor(out=lft[b][0:n, :], in0=lmn[0:n, :], in1=ll[0:n, :],
                                                      op=ALU.subtract), reads=["lmn", "ll"], writes=["lft%d" % b])
                S.dma("sp", _E.dma_start(out=lf_dst, in_=lft[b][0:n, :]), reads=["lft%d" % b], sem="lft%d" % b)
                yield

            def stage_ret(n, b, kd_ap, gdec):
                yield
                for h in range(8):
                    S.op("pe", _E.transpose(out=tpb[:, h * 128:h * 128 + n],
                                                          in_=qkrot[b][0:n, h * 128:(h + 1) * 128],
                                                          identity=identb[0:n, 0:n]),
                         reads=["qkrot%da" % b, "qkrot%db" % b], writes=[Bt(0)])
                S.op("act", _E.copy(out=qkT[b][:, :, 0:n], in_=tp3[:, :, 0:n]), writes=[Bt(0), "qkT%d" % b])
                yield
                qd3 = C("qd").rearrange("p (h t) -> p h t", h=4)
                S.op("pool", _E.tensor_tensor(out=qdT[b][:, :, 0:n], in0=qkT[b][:, 0:4, 0:n], in1=qd3[:, :, 0:n],
                                                       op=ALU.mult), reads=["qkT%d" % b], writes=["qdT%d" % b])
                for h in range(4):
                    S.op("pool", _E.tensor_scalar(out=kdtok[b][0:n, h * 128:(h + 1) * 128],
                                                                in0=qkrot[b][0:n, 512 + h * 128:512 + (h + 1) * 128],
                                                                scalar1=kd_ap[0:n, h:h + 1], scalar2=None, op0=ALU.mult),
                         reads=["qkrot%da" % b, "qkrot%db" % b], writes=["kdtok%d_%d" % (b, h)])
                for h in range(4):
                    S.op("pe", _E.matmul(B(1)[0:n, h * 128:h * 128 + n], lhsT=qkT[b][:, 4 + h, 0:n],
                                                       rhs=qkT[b][:, h, 0:n], start=True, stop=True),
                         reads=["qkT%d" % b], writes=[Bt(1)])
                Mp3 = C("Mp").rearrange("p (h t) -> p h t", h=4)
                s43 = B(1).rearrange("p (h t) -> p h t", h=4)
                S.op("dve", _E.tensor_tensor(out=sTm[b][0:n, :, 0:n], in0=s43[0:n, :, 0:n], in1=Mp3[0:n, :, 0:n],
                                                      op=ALU.mult), writes=[Bt(1), "sTm%d" % b])
                yield
                for h in range(4):
                    S.op("pe", _E.matmul(B(3)[0:n, h * 128:(h + 1) * 128], lhsT=sTm[b][0:n, h, 0:n],
                                                       rhs=vtok[b][0:n, h * 128:(h + 1) * 128], start=True, stop=False),
                         reads=["sTm%d" % b, "vtok%d" % b], writes=[Bt(3)])
                    S.op("pe", _E.matmul(B(3)[0:n, h * 128:(h + 1) * 128], lhsT=qdT[b][:, h, 0:n],
                                                       rhs=Sbf[:, h, :], start=False, stop=True),
                         reads=["qdT%d" % b, "Sbf"], writes=[Bt(3)])
                for h in range(4):
                    S.op("pe", _E.matmul(B(2)[:, h * 128:(h + 1) * 128],
                                                       lhsT=kdtok[b][0:n, h * 128:(h + 1) * 128],
                                                       rhs=vtok[b][0:n, h * 128:(h + 1) * 128], start=True, stop=True),
                         reads=["kdtok%d_%d" % (b, h), "vtok%d" % b], writes=[Bt(2)])
                for h in range(4):
                    S.op("dve", _E.scalar_tensor_tensor(out=Sst[:, h, :], in0=Sst[:, h, :], scalar=gdec[h],
                                                                      in1=B(2)[:, h * 128:(h + 1) * 128],
                                                                      op0=ALU.mult, op1=ALU.add),
                         writes=[Bt(2), "Sst"])
                S.op("pool", _E.tensor_copy(out=Sbf, in_=Sst), reads=["Sst"], writes=["Sbf"])
                yield
                for h in range(4):
                    S.op("dve", _E.bn_stats(out=st6[0:n, h, :], in_=B(3)[0:n, h * 128:(h + 1) * 128]),
                         writes=[Bt(3), "st6_%d" % h])
                for h in range(4):
                    S.op("dve", _E.bn_aggr(out=mv[0:n, h, :], in_=st6[0:n, h, :]),
                         reads=["st6_%d" % h], writes=["mv%d" % h])
                S.op("dve", _E.tensor_scalar(out=rstdg[0:n, :], in0=mv[0:n, :, 1], scalar1=1e-5, scalar2=None,
                                                      op0=ALU.add), reads=["mv%d" % h for h in range(4)], writes=["rstdg"])
                S.op("pool", _E.tensor_tensor(out=rstdg[0:n, :], in0=rstdg[0:n, :], in1=mhalf[0:n, :], op=ALU.pow),
                     reads=["rstdg"], writes=["rstdg"])
                for h in range(4):
                    S.op("dve", _E.tensor_scalar(out=onb[b][0:n, h * 128:(h + 1) * 128],
                                                               in0=B(3)[0:n, h * 128:(h + 1) * 128],
                                                               scalar1=mv[0:n, h, 0:1], scalar2=rstdg[0:n, h:h + 1],
                                                               op0=ALU.subtract, op1=ALU.mult),
                         reads=["rstdg", "mv%d" % h], writes=[Bt(3), "rt%d" % (2 + b)])
                S.op("pool", _E.tensor_tensor(out=gsb[b][0:n, :], in0=gsb[b][0:n, :], in1=onb[b][0:n, :], op=ALU.mult),
                     reads=["gsb", "rt%d" % (2 + b)], writes=["gsb"])
                S.op("pool", _E.tensor_tensor(out=mixb[b][0:n, 0:512], in0=gsb[b][0:n, :], in1=egs[b][0:n, :],
                                                       op=ALU.mult), reads=["gsb", "egs"], writes=["mixr%d" % b])
                yield

            def cumsum_tile(n, b, first):
                S.op("pe", _E.matmul(B(4)[0:n, 0:8], lhsT=C("U")[0:n, 0:n], rhs=lft[b][0:n, :], start=True, stop=True),
                     reads=["lft%d" % b, "cf"], writes=[Bt(4)])
                S.op("pe", _E.matmul(B(4)[:, 8:16], lhsT=C("ones")[0:n, :], rhs=lft[b][0:n, :], start=True, stop=True),
                     reads=["lft%d" % b, "cf"], writes=[Bt(4)])

            def fox_finish_head(n, b, h, acc_bank):
                S.op("dve", _E.reciprocal(out=rcp[0:n, h:h + 1], in_=B(acc_bank)[0:n, 64:65]),
                     writes=[Bt(acc_bank), "rcp%d" % h])
                S.op("dve", _E.tensor_scalar(out=mixb[b][0:n, 512 + h * 64:512 + (h + 1) * 64],
                                                      in0=B(acc_bank)[0:n, 0:64], scalar1=rcp[0:n, h:h + 1],
                                                      scalar2=None, op0=ALU.mult),
                     reads=["rcp%d" % h], writes=[Bt(acc_bank), "mixf%d_%d" % (b, h)])

            def stage_fox_prompt(j, b, t, filler=None):
                cumsum_tile(128, b, t == 0)
                S.op("dve", _E.tensor_tensor(out=ctab[:, t, :], in0=B(4)[:, 0:8], in1=Rcar, op=ALU.add),
                     reads=["Rcar"], writes=[Bt(4), "ctab"])
                S.op("dve", _E.tensor_tensor(out=Rcar, in0=B(4)[:, 8:16], in1=Rcar, op=ALU.add),
                     writes=[Bt(4), "Rcar"])
                bt_ = Btab[t % 2]
                S.op("dve", _E.tensor_tensor(out=bt_[:, 0:t + 1, :],
                                                      in0=Rcar.unsqueeze(1).to_broadcast([128, t + 1, 8]),
                                                      in1=ctab[:, 0:t + 1, :], op=ALU.subtract),
                     reads=["Rcar", "ctab"], writes=["Btab%d" % (t % 2)])
                macros = []
                for p_ in range(4):
                    kts = list(range(t + 1))
                    for i0_ in range(0, len(kts), 2):
                        macros.append((p_, kts[i0_:i0_ + 2]))
                slots = (5, 6)
                LAGM = 2
                pend = []

                def emit_pv(mi):
                    p_, kts_ = macros[mi]
                    for q_, kt in enumerate(kts_):
                        for e_ in range(2):
                            h = 2 * p_ + e_
                            pt = pend[mi][q_ * 2 + e_]
                            accb = 7 if e_ == 0 else 4
                            S.op("pe", _E.matmul(B(accb)[:, 0:65], lhsT=PT[pt], rhs=Vaug[:, kt, h, :],
                                                 start=(kt == 0), stop=(kt == t)),
                                 reads=["PT%d" % pt, "vaug"], writes=[Bt(accb)])
                            if kt == t:
                                fox_finish_head(128, b, h, accb)

                for mi, (p_, kts_) in enumerate(macros):
                    sl = slots[cnt["sl"] % 2]
                    cnt["sl"] += 1
                    pts = []
                    for q_, kt in enumerate(kts_):
                        S.op("pe", _E.matmul(
                            B(sl)[:, q_ * 256:(q_ + 1) * 256], lhsT=kfT[:, p_, kt * 128:(kt + 1) * 128],
                            rhs=QB[:, p_, j, :, :].rearrange("p e t -> p (e t)"), start=True, stop=(kt != t)),
                            reads=["kfT%d" % p_, "qfT%d_0" % p_, "qfT%d_1" % p_], writes=[Bt(sl)])
                        if kt == t:
                            for e_ in range(2):
                                c0_ = q_ * 256 + e_ * 128
                                S.op("pe", _E.matmul(B(sl)[:, c0_:c0_ + 128], lhsT=identb, rhs=negmb,
                                                     start=False, stop=(e_ == 1)),
                                     reads=["identb", "negmb"], writes=[Bt(sl)])
                    for q_, kt in enumerate(kts_):
                        for e_ in range(2):
                            h = 2 * p_ + e_
                            c0_ = q_ * 256 + e_ * 128
                            pt = cnt["pt"] % NPT
                            cnt["pt"] += 1
                            pts.append(pt)
                            S.op("act", _E.activation(
                                out=PT[pt], in_=B(sl)[:, c0_:c0_ + 128], func=AF.Exp, bias=bt_[:, kt, h:h + 1], scale=0.125),
                                reads=["Btab%d" % (t % 2)], writes=[Bt(sl), "PT%d" % pt])
                    pend.append(pts)
                    if mi >= LAGM:
                        emit_pv(mi - LAGM)
                    if filler is not None:
                        for _ in range(3 if t < 4 else (2 if t < 8 else 1)):
                            next(filler, None)
                for mi in range(max(0, len(macros) - LAGM), len(macros)):
                    emit_pv(mi)
                if filler is not None:
                    for _ in filler:
                        pass

            def stage_E(n, j, b, x1_rows):
                mixtoks = ["mixr%d" % b] + ["mixf%d_%d" % (b, h) for h in range(8)]
                for k in range(8):
                    S.op("pe", _E.transpose(out=tpb[:, k * 128:k * 128 + n],
                                                          in_=mixb[b][0:n, k * 128:(k + 1) * 128],
                                                          identity=identb[0:n, 0:n]),
                         reads=mixtoks, writes=[Bt(0)])
                S.op("act", _E.copy(out=mixT[b][:, :, 0:n], in_=tp3[:, :, 0:n]), writes=[Bt(0), "mixT%d" % b])
                yield
                for half in range(2):
                    if half == 1:
                        yield
                    for k in range(8):
                        S.op("pe", _E.matmul(B(1 + half)[0:n, :], lhsT=mixT[b][:, k, 0:n],
                                                                      rhs=w_out_bf[:, k, half * 512:(half + 1) * 512],
                                                                      start=(k == 0), stop=(k == 7)),
                             reads=["mixT%d" % b], writes=[Bt(1 + half)])
                S.op("dve", _E.tensor_tensor(out=xt[0:n, j, :], in0=xt[0:n, j, :], in1=ps[0:n, 512:1536], op=ALU.add),
                     writes=[Bt(1), Bt(2), "xt%d" % j])
                S.dma("sp", _E.dma_start(out=x1_rows, in_=xt[0:n, j, :]), reads=["xt%d" % j], sem="x1st%d" % j)
                yield

            def load_cs(t):
                c = cst[t % 2]
                S.dma("sp", _E.dma_start(out=c[:, 0:64], in_=cosp_d[t]), writes=["cs%d" % (t % 2)], sem="cs%d" % (t % 2))
                S.dma("sp", _E.dma_start(out=c[:, 64:128], in_=sinp_d[t]), writes=["cs%d" % (t % 2)], sem="cs%d" % (t % 2))

            S.op("pool", _E.memset(Vaug.rearrange("p t h e -> p (t h e)"), 1.0), writes=["vaug"])
            S.op("pool", _E.memset(QB.rearrange("p c j e t -> p (c j e t)"), 0.0),
                 writes=["qfT%d_%d" % (c_, e_) for c_ in range(4) for e_ in range(2)])
            for s in range(2):
                S.op("pool", _E.memset(Sst.rearrange("p h e -> p (h e)"), 0.0), writes=["Sst"])
                S.op("pool", _E.memset(Sbf.rearrange("p h e -> p (h e)"), 0.0), writes=["Sbf"])
                S.op("pool", _E.memset(Rcar, 0.0), writes=["Rcar"])
                for j in range(G):
                    load_x(xp[s, j * 128:(j + 1) * 128, :], 128, j)
                load_cs(0)
                for g in range(T // (128 * G)):
                    for j in range(G):
                        stage_A(128, j)
                    stage_B(G * 128, G, g * G * 128)
                    def mixer_front(t_, j_):
                        c_ = cst[t_ % 2]
                        r_ = t_ * 128
                        yield from stage_C(128, j_, t_ % 2, c_[:, 0:64], c_[:, 64:128], "cs%d" % (t_ % 2), t_,
                                           kf_p[s, r_:r_ + 128, :], vf_p[s, r_:r_ + 128, :], lf_p[s, r_:r_ + 128, :],
                                           Vaug[:, t_, :, 0:64])
                        yield from stage_ret(128, t_ % 2, C("kdp"), g128)

                    def mixer_tail(t_, j_):
                        r_ = t_ * 128
                        yield from stage_E(128, j_, t_ % 2, x1s[s * T + r_:s * T + r_ + 128, :])
                        if g + 1 < T // (128 * G):
                            load_x(xp[s, (t_ + G) * 128:(t_ + G + 1) * 128, :], 128, j_)
                        yield

                    def chain_(*gens):
                        for g_ in gens:
                            yield from g_

                    for j in range(G):
                        t = g * G + j
                        b = t % 2
                        if t + 1 < 16:
                            load_cs(t + 1)
                        if j == 0:
                            for _ in mixer_front(t, j):
                                pass
                        parts = []
                        if j > 0:
                            parts.append(mixer_tail(t - 1, j - 1))
                        if j + 1 < G:
                            parts.append(mixer_front(t + 1, j + 1))
                        stage_fox_prompt(j, b, t, chain_(*parts) if parts else None)
                        if j == G - 1:
                            for _ in mixer_tail(t, j):
                                pass
                        if s == 0 and t == 3:
                            stop_at("g0")
                S.dma("sp", _E.dma_start(out=sr_p[s].rearrange("h d e -> d h e"), in_=Sst),
                      reads=["Sst"], sem="srout")
                if s == 0:
                    stop_at("s0")

            stop_at("pp")
            bar()
            S.op("pool", _E.memset(Vaug[:, 0, :, 64:65], 1.0), writes=["vaug"])
            for s in range(2):
                S.dma("sp", _E.dma_start(out=Sst, in_=sret[s].rearrange("h d e -> d h e")),
                      writes=["Sst"], sem="sldS")
                S.op("pool", _E.tensor_copy(out=Sbf, in_=Sst), reads=["Sst"], writes=["Sbf"])
                S.dma("sp", _E.dma_start(out=clfs, in_=clf[s].rearrange("(t p) h -> p t h", p=128)),
                      writes=["clfs"], sem="sldC")
                load_x(xs_d[s], NS, 0)
                clf2 = clfs.rearrange("p t h -> p (t h)")
                S.op("pe", _E.matmul(B(3)[:, 0:256], lhsT=C("Lst"), rhs=clf2, start=True, stop=True),
                     reads=["clfs", "cf"], writes=[Bt(3)])
                S.op("pe", _E.matmul(B(3)[:, 256:512], lhsT=C("ones"), rhs=clf2, start=True, stop=True),
                     reads=["clfs", "cf"], writes=[Bt(3)])
                S.op("dve", _E.tensor_copy(out=tots.rearrange("p t h -> p (t h)"), in_=B(3)[:, 256:512]),
                     writes=[Bt(3), "tots"])
                S.op("dve", _E.memset(sufs[:, NKT - 1, :], 0.0), writes=["sufs"])
                for tt in range(NKT - 2, -1, -1):
                    S.op("dve", _E.tensor_tensor(out=sufs[:, tt, :], in0=sufs[:, tt + 1, :],
                                                                 in1=tots[:, tt + 1, :], op=ALU.add),
                         reads=["tots"], writes=["sufs"])
                S.op("dve", _E.tensor_tensor(out=atab.rearrange("p t h -> p (t h)"), in0=B(3)[:, 0:256],
                                                      in1=sufs.rearrange("p t h -> p (t h)"), op=ALU.add),
                     reads=["sufs"], writes=[Bt(3), "atab"])
                n = NS
                stage_A(n, 0)
                stage_B(n, 1, 0)
                b = s % 2
                for _ in stage_C(n, 0, b, C("coss", n), C("sins", n), "cf", 0,
                                 kf_s[s], vf_s[s], lf_s[s], Vaug[0:n, 0, :, 0:64]):
                    pass
                for _ in stage_ret(n, b, C("kds"), g64):
                    pass
                S.dma("sp", _E.dma_start(out=sr_s[s].rearrange("h d e -> d h e"), in_=Sst),
                      reads=["Sst"], sem="srout")
                cumsum_tile(n, b, True)
                S.op("dve", _E.tensor_scalar(out=negbq[0:n, :], in0=B(4)[0:n, 0:8], scalar1=-1.0, scalar2=None,
                                                      op0=ALU.mult), reads=["b4data"], writes=[Bt(4), "negbq"])
                S.op("dve", _E.memset(B(7)[0:n, 0:260], 0.0), writes=[Bt(7)])
                S.op("dve", _E.memset(B(4)[0:n, 0:260], 0.0), writes=[Bt(4), "b4data"])
                for kt in range(NKT + 1):
                    kb = kt % 2
                    last = (kt == NKT)
                    rows = n if last else 128
                    if not last:
                        S.dma("sp", _E.dma_start(out=kst[kb], in_=ck[s, kt * 128:(kt + 1) * 128, :]),
                              writes=["kst%d" % kb], sem="kst%d" % kb)
                        S.dma("sp", _E.dma_start(out=vst[kb], in_=cv[s, kt * 128:(kt + 1) * 128, :]),
                              writes=["vst%d" % kb], sem="vst%d" % kb)
                        for c in range(4):
                            S.op("pe", _E.transpose(out=B(0)[:, c * 128:(c + 1) * 128],
                                                                          in_=kst[kb][:, c * 128:(c + 1) * 128],
                                                                          identity=C("ident")),
                                 reads=["kst%d" % kb, "cf"], writes=[Bt(0)])
                        S.op("dve", _E.tensor_copy(out=kTt[kb], in_=tpf3), writes=[Bt(0), "kTt%d" % kb])
                        S.op("pool", _E.tensor_copy(out=Vt[kb][:, :, 0:64],
                                                                    in_=vst[kb].rearrange("p (h f) -> p h f", f=64)),
                             reads=["vst%d" % kb], writes=["Vt%d" % kb])
                        if kt < 2:
                            S.op("pool", _E.memset(Vt[kb][:, :, 64:65], 1.0), writes=["Vt%d" % kb])
                        kT_src, kT_tok = kTt[kb], "kTt%d" % kb
                        v_src, v_tok = Vt[kb], "Vt%d" % kb
                        bias_of = (lambda h, kt=kt: atab[:, kt, h:h + 1])
                        bias_tok = "atab"
                    else:
                        kT_src, kT_tok = kfT[:, :, 0:n], "kfTnew"
                        v_src, v_tok = Vaug[:, 0, :, :], "vaug"
                        bias_of = (lambda h: negbq[0:n, h:h + 1])
                        bias_tok = "negbq"
                    bk0, bk1 = (5, 6) if kt % 2 == 0 else (2, 3)
                    for h in range(8):
                        p_, e_ = divmod(h, 2)
                        bk = bk0 if e_ == 0 else bk1
                        S.op("pe", _E.matmul(
                            B(bk)[0:rows, p_ * 64:(p_ + 1) * 64], lhsT=kT_src[e_ * 64:(e_ + 1) * 64, p_, 0:rows],
                            rhs=QB[e_ * 64:(e_ + 1) * 64, p_, 0, e_, 0:n], start=True, stop=(not last)),
                            reads=([kT_tok] if kT_tok != "kfTnew" else ["kfT%d" % p_]) + ["qfT%d_%d" % (p_, e_)], writes=[Bt(bk)])
                        if last:
                            S.op("pe", _E.matmul(B(bk)[0:n, p_ * 64:(p_ + 1) * 64],
                                                                        lhsT=identb[e_ * 64:e_ * 64 + n, e_ * 64:e_ * 64 + n],
                                                                        rhs=negmb[e_ * 64:e_ * 64 + n, e_ * 64:e_ * 64 + n],
                                                                        start=False, stop=True),
                                 reads=["identb", "negmb"], writes=[Bt(bk)])
                    for h in range(8):
                        p_, e_ = divmod(h, 2)
                        bk = bk0 if e_ == 0 else bk1
                        S.op("act", _E.activation(
                            out=PTs[kb][0:rows, h, :], in_=B(bk)[0:rows, p_ * 64:(p_ + 1) * 64], func=AF.Exp,
                            bias=bias_of(h)[0:rows, :], scale=0.125),
                            reads=[bias_tok], writes=[Bt(bk), "PTs%d_%d" % (kb, h)])
                    for h in range(8):
                        accb = 7 if h < 4 else 4
                        hh = h % 4
                        S.op("pe", _E.matmul(
                            B(accb)[0:n, hh * 65:(hh + 1) * 65], lhsT=PTs[kb][0:rows, h, :], rhs=v_src[0:rows, h, :],
                            start=False, stop=last, skip_group_check=True),
                            reads=["PTs%d_%d" % (kb, h), v_tok], writes=[Bt(accb)])
                for h in range(8):
                    accb = 7 if h < 4 else 4
                    hh = h % 4
                    S.op("dve", _E.reciprocal(out=rcp[0:n, h:h + 1],
                                                                             in_=B(accb)[0:n, hh * 65 + 64:hh * 65 + 65]),
                         writes=[Bt(accb), "rcp%d" % h])
                    S.op("dve", _E.tensor_scalar(
                        out=mixb[b][0:n, 512 + h * 64:512 + (h + 1) * 64], in0=B(accb)[0:n, hh * 65:hh * 65 + 64],
                        scalar1=rcp[0:n, h:h + 1], scalar2=None, op0=ALU.mult),
                        reads=["rcp%d" % h], writes=[Bt(accb), "mixf%d_%d" % (b, h)])
                for _ in stage_E(n, 0, b, x1s[2 * T + s * NS:2 * T + (s + 1) * NS, :]):
                    pass
                if s == 0:
                    stop_at("ss0")

            bar()

            stop_at("p1")
            A.off = mark_persist
            wg_bf = A.bf16(8 * DFF).rearrange("p (k c) -> p k c", k=8)
            wu_bf = A.bf16(8 * DFF).rearrange("p (k c) -> p k c", k=8)
            wd_bf = A.bf16(NFC * D).rearrange("p (k c) -> p k c", k=NFC)
            gfin = A.f32(D)
            mark_w2 = A.off
            stage2 = [A.f32(DFF), A.f32(DFF)]
            A.off = mark_w2
            NSL = G + 1
            x2t = A.f32(NSL * D).rearrange("p (j c) -> p j c", j=NSL)
            xsb2 = [A.bf16(D), A.bf16(D)]
            junk2 = A.bf16(D)
            h2T = A.bf16(8 * G * 128).rearrange("p (k t) -> p k t", k=8)
            actT = A.bf16(NFC * G * 128).rearrange("p (c t) -> p c t", c=NFC)
            sgt = [A.f32(G * 128), A.f32(G * 128)]
            sm2 = A.f32(32)
            print("phase2 arena words used:", A.off, "of", AW)
            ss2 = sm2[:, 0:G]
            rs2 = sm2[:, 8:8 + G]
            ss3 = sm2[:, 16:16 + G]
            rs3 = sm2[:, 24:24 + G]

            S.dma("sp", _E.dma_start(out=gfin, in_=gfin_d), writes=["gfin"], sem="gfin")
            for (wsrc, wdst) in ((w_gate, wg_bf), (w_up, wu_bf)):
                for k in range(8):
                    sg = stage2[k % 2]
                    tok = "stg2_%d" % (k % 2)
                    S.dma("sp", _E.dma_start(out=sg[:, 0:DFF], in_=wsrc[k * 128:(k + 1) * 128, :]),
                          writes=[tok], sem=tok)
                    convert_rows(wdst[:, k, :], sg, DFF, gffnc[:, k:k + 1], rd=[tok])
            for k in range(NFC):
                sg = stage2[k % 2]
                tok = "stg2_%d" % (k % 2)
                S.dma("sp", _E.dma_start(out=sg[:, 0:D], in_=w_down[k * 128:(k + 1) * 128, :]),
                      writes=[tok], sem=tok)
                convert_rows(wd_bf[:, k, :], sg, D, None, rd=[tok])
            bar()

            groups = [(g * G * 128, G, y_p[g * G * 128:(g + 1) * G * 128, :]) for g in range(2 * T // (G * 128))]
            groups.append((2 * T, 1, y_s))
            pcnt = {"b": 0}
            pre_b = {}

            def p2_pre(row0, j, sl):
                b = pcnt["b"] % 2
                pcnt["b"] += 1
                pre_b[j] = b
                S.dma("sp", _E.dma_start(out=x2t[:, sl, :], in_=x1s[row0 + j * 128:row0 + (j + 1) * 128, :]),
                      writes=["x2t%d" % sl], sem="x2t%d" % sl)
                S.op("pool", _E.memset(ss2[:, j:j + 1], 0.0), writes=["ss2_%d" % j])
                S.op("act", _E.activation(out=junk2, in_=x2t[:, sl, :], func=AF.Square, accum_out=ss2[:, j:j + 1]),
                     reads=["x2t%d" % sl], writes=["junk2", "ss2_%d" % j])
                S.op("dve", _E.tensor_scalar(out=rs2[:, j:j + 1], in0=ss2[:, j:j + 1], scalar1=1.0 / D,
                                             scalar2=1e-6, op0=ALU.mult, op1=ALU.add),
                     reads=["ss2_%d" % j], writes=["rs2_%d" % j])
                S.op("pool", _E.tensor_tensor(out=rs2[:, j:j + 1], in0=rs2[:, j:j + 1], in1=mhalf[:, 0:1], op=ALU.pow),
                     reads=["rs2_%d" % j], writes=["rs2_%d" % j])
                S.op("dve", _E.tensor_scalar(out=xsb2[b], in0=x2t[:, sl, :], scalar1=rs2[:, j:j + 1], scalar2=None,
                                             op0=ALU.mult),
                     reads=["x2t%d" % sl, "rs2_%d" % j], writes=["xsb2_%d" % b])

            def p2_tp(j):
                b = pre_b[j]
                for k in range(8):
                    S.op("pe", _E.transpose(out=tpb[:, k * 128:(k + 1) * 128], in_=xsb2[b][:, k * 128:(k + 1) * 128],
                                            identity=identb), reads=["xsb2_%d" % b], writes=[Bt(0)])
                S.op("act", _E.copy(out=h2T[:, :, j * 128:(j + 1) * 128], in_=tp3), writes=[Bt(0), "h2T%d" % j])

            def p2_chunks(NT, ntl):
                h2toks = ["h2T%d" % j for j in range(ntl)]
                for c in range(NFC):
                    gb, ub = (1, 2) if c % 2 == 0 else (3, 4)
                    sb_ = c % 2
                    for k in range(8):
                        S.op("pe", _E.matmul(B(gb)[:, 0:NT], lhsT=wg_bf[:, k, c * 128:(c + 1) * 128],
                                             rhs=h2T[:, k, 0:NT], start=(k == 0), stop=(k == 7)),
                             reads=h2toks, writes=[Bt(gb)])
                    for k in range(8):
                        S.op("pe", _E.matmul(B(ub)[:, 0:NT], lhsT=wu_bf[:, k, c * 128:(c + 1) * 128],
                                             rhs=h2T[:, k, 0:NT], start=(k == 0), stop=(k == 7)),
                             reads=h2toks, writes=[Bt(ub)])
                    S.op("act", _E.activation(out=sgt[sb_][:, 0:NT], in_=B(gb)[:, 0:NT], func=AF.Silu),
                         writes=[Bt(gb), "sgt%d" % sb_])
                    S.op("dve", _E.tensor_tensor(out=actT[:, c, 0:NT], in0=B(ub)[:, 0:NT], in1=sgt[sb_][:, 0:NT],
                                                 op=ALU.mult), reads=["sgt%d" % sb_], writes=[Bt(ub), "actT"])

            def p2_down(j):
                pb = (5, 6)
                for half in range(2):
                    for c in range(NFC):
                        S.op("pe", _E.matmul(B(pb[half]), lhsT=actT[:, c, j * 128:(j + 1) * 128],
                                             rhs=wd_bf[:, c, half * 512:(half + 1) * 512],
                                             start=(c == 0), stop=(c == NFC - 1)), reads=["actT"], writes=[Bt(pb[half])])

            def p2_fin(j, ydst, sl):
                S.op("dve", _E.tensor_tensor(out=x2t[:, sl, :], in0=x2t[:, sl, :], in1=ps[:, 2560:3584], op=ALU.add),
                     writes=[Bt(5), Bt(6), "x2t%d" % sl])
                S.op("pool", _E.memset(ss3[:, j:j + 1], 0.0), writes=["ss3_%d" % j])
                S.op("act", _E.activation(out=junk2, in_=x2t[:, sl, :], func=AF.Square, accum_out=ss3[:, j:j + 1]),
                     reads=["x2t%d" % sl], writes=["junk2", "ss3_%d" % j])
                S.op("dve", _E.tensor_scalar(out=rs3[:, j:j + 1], in0=ss3[:, j:j + 1], scalar1=1.0 / D,
                                             scalar2=1e-6, op0=ALU.mult, op1=ALU.add),
                     reads=["ss3_%d" % j], writes=["rs3_%d" % j])
                S.op("pool", _E.tensor_tensor(out=rs3[:, j:j + 1], in0=rs3[:, j:j + 1], in1=mhalf[:, 0:1], op=ALU.pow),
                     reads=["rs3_%d" % j], writes=["rs3_%d" % j])
                S.op("dve", _E.scalar_tensor_tensor(out=x2t[:, sl, :], in0=x2t[:, sl, :], scalar=rs3[:, j:j + 1],
                                                    in1=gfin, op0=ALU.mult, op1=ALU.mult),
                     reads=["rs3_%d" % j, "gfin"], writes=["x2t%d" % sl])
                S.dma("sp", _E.dma_start(out=ydst[j * 128:(j + 1) * 128, :], in_=x2t[:, sl, :]),
                      reads=["x2t%d" % sl], sem="yout%d" % sl)

            def slot(gi_, j_):
                return (gi_ * G + j_) % NSL

            for j in range(groups[0][1]):
                p2_pre(groups[0][0], j, slot(0, j))
                p2_tp(j)
            for gi, (row0, ntl, ydst) in enumerate(groups):
                nxt = groups[gi + 1] if gi + 1 < len(groups) else None
                nn = nxt[1] if nxt else 0
                p2_chunks(ntl * 128, ntl)
                if nn > 0:
                    p2_pre(nxt[0], 0, slot(gi + 1, 0))
                tp_done = 0
                for j in range(ntl):
                    p2_down(j)
                    if j >= 1 and j - 1 < nn:
                        p2_tp(j - 1)
                        tp_done = j
                    p2_fin(j, ydst, slot(gi, j))
                    if j + 1 < nn:
                        p2_pre(nxt[0], j + 1, slot(gi + 1, j + 1))
                for jj in range(tp_done, nn):
                    if jj > ntl:
                        p2_pre(nxt[0], jj, slot(gi + 1, jj))
                    p2_tp(jj)

        except _Stop as ex:
            print('STOPPED at', ex)
        S.emit(final_wait_sems=list(S.dma_cum.keys()))
        print("ops per engine:", {e: len(v) for e, v in S.by_eng.items()})
    return nc


_PROG = {}


def kernel(x_prompt, x_sample, cache_fox_k, cache_fox_v, cache_fox_logf, state_ret,
           w_in, b_forget, ret_gn_gain, w_out, norm_mix_gain, norm_ffn_gain,
           w_gate, w_up, w_down, norm_final_gain):
    f = lambda a: np.ascontiguousarray(np.asarray(a, dtype=np.float32))
    x_prompt, x_sample = f(x_prompt), f(x_sample)
    ckf, cvf, clff, srf = f(cache_fox_k), f(cache_fox_v), f(cache_fox_logf), f(state_ret)
    cf, cosp, sinp, g128, g64 = _host_consts()
    if "nc" not in _PROG:
        _PROG["nc"] = build_program(g128, g64)
    nc = _PROG["nc"]
    vecs = np.zeros((128, 28), np.float32)
    vecs[:, 0:8] = f(norm_mix_gain)[0].reshape(8, 128).T
    vecs[:, 8:16] = f(norm_ffn_gain)[0].reshape(8, 128).T
    vecs[:, 16:20] = f(ret_gn_gain)[0].reshape(4, 128).T
    vecs[:, 20:28] = np.broadcast_to(f(b_forget)[0][None, :], (128, 8))
    gfin_b = np.ascontiguousarray(np.broadcast_to(f(norm_final_gain)[None, :], (128, D)))
    shared = {
        "w_in": f(w_in)[0], "w_out": f(w_out)[0], "w_gate": f(w_gate)[0], "w_up": f(w_up)[0],
        "w_down": f(w_down)[0], "cf32": cf, "cosp": cosp, "sinp": sinp, "vecs": vecs, "gfin_b": gfin_b,
    }
    in_maps = []
    for c in range(NCORES):
        m = dict(shared)
        m["xp"] = x_prompt[2 * c:2 * c + 2]
        m["xs"] = x_sample[2 * c:2 * c + 2]
        m["ck"] = ckf[0, 2 * c:2 * c + 2].reshape(2, PAST, 512)
        m["cv"] = cvf[0, 2 * c:2 * c + 2].reshape(2, PAST, 512)
        m["clf"] = clff[0, 2 * c:2 * c + 2]
        m["sret"] = srf[0, 2 * c:2 * c + 2]
        in_maps.append(m)
    res = run_bass_kernel_spmd(nc, in_maps, core_ids=list(range(NCORES)))
    R = res.results
    cat = lambda name: np.concatenate([np.asarray(r[name]) for r in R], axis=0)
    y_prompt = cat("y_p").reshape(16, T, D)
    y_sample = cat("y_s").reshape(16, NS, D)
    sr_p = cat("sr_p").reshape(1, 16, 4, 128, 128)
    kf_p = cat("kf_p").reshape(1, 16, T, 8, 64)
    vf_p = cat("vf_p").reshape(1, 16, T, 8, 64)
    lf_p = cat("lf_p").reshape(1, 16, T, 8)
    sr_s = cat("sr_s").reshape(1, 16, 4, 128, 128)
    kf_s = cat("kf_s").reshape(1, 16, NS, 8, 64)
    vf_s = cat("vf_s").reshape(1, 16, NS, 8, 64)
    lf_s = cat("lf_s").reshape(1, 16, NS, 8)
    return (y_prompt.astype(np.float32), y_sample.astype(np.float32), sr_p, kf_p, vf_p, lf_p,
            sr_s, kf_s, vf_s, lf_s)
```

```python
import numpy as np
from contextlib import ExitStack
import concourse.bass as bass
import concourse.mybir as mybir
from concourse.bass_utils import run_bass_kernel_spmd

F32 = mybir.dt.float32
BF16 = mybir.dt.bfloat16
AF = mybir.ActivationFunctionType
ALU = mybir.AluOpType

NCORES = 8
D = 1024
DIN = 3592
DFF = 2816
NFC = DFF // 128
T = 2048
NS = 64
PAST = 4096
NKT = PAST // 128
G = 4
ENGINES = ("pe", "act", "dve", "pool", "sp")


class Op:
    __slots__ = ("eng", "fn", "reads", "writes", "gidx", "eidx", "dma_sem", "dma_val",
                 "waits", "signal", "count", "clock")

    def __init__(self, eng, fn, reads, writes, dma_sem=None):
        self.eng = eng
        self.fn = fn
        self.reads = reads
        self.writes = writes
        self.dma_sem = dma_sem
        self.dma_val = 0
        self.waits = []
        self.signal = False
        self.count = 0
        self.clock = None


class Sched:
    def __init__(self, nc, same_engine_dist=10 ** 9):
        self.nc = nc
        self.ops = []
        self.by_eng = {e: [] for e in ENGINES}
        self.last_writer = {}
        self.readers = {}
        self.dma_cum = {}
        self.observed = {}
        self.same_engine_dist = same_engine_dist
        self.bar_ops = []
        self.bank_last = {}

    def op(self, eng, fn, reads=(), writes=()):
        o = Op(eng, fn, tuple(reads), tuple(writes))
        self._add(o)
        return o

    def dma(self, eng, fn, reads=(), writes=(), sem=None):
        o = Op(eng, fn, tuple(reads), tuple(writes), dma_sem=sem)
        self.dma_cum.setdefault(sem, 0)
        self._add(o)
        self.dma_cum[sem] += 16
        o.dma_val = self.dma_cum[sem]
        return o

    def _dep_on(self, o, d, obs):
        if d.dma_sem is not None:
            key = ("dma", d.dma_sem)
            val = self.dma_cum[d.dma_sem]
            if obs.get(key, 0) >= val:
                return
            obs[key] = val
            o.waits.append(("dma", d.dma_sem, val))
        else:
            key = ("eng", d.eng)
            if obs.get(key, -1) >= d.eidx:
                return
            o.waits.append(("eng", d.eng, d))
            d.signal = True
            for k, v in d.clock.items():
                if obs.get(k, -1) < v:
                    obs[k] = v

    def _add(self, o, force_deps=()):
        o.gidx = len(self.ops)
        o.eidx = len(self.by_eng[o.eng])
        deps = set(self.bar_ops)
        for t in o.reads:
            w = self.last_writer.get(t)
            if w is not None:
                deps.add(w)
        bank_toks = []
        for t in o.writes:
            if isinstance(t, str) and len(t) == 2 and t[0] == "B":
                bank_toks.append(t)
                for eng2, last in self.bank_last.setdefault(t, {}).items():
                    if eng2 != o.eng:
                        deps.add(last)
                continue
            w = self.last_writer.get(t)
            if w is not None:
                deps.add(w)
            for r in self.readers.get(t, ()):
                deps.add(r)
        deps.discard(o)
        obs = self.observed.setdefault(o.eng, {})
        for d in sorted(deps, key=lambda x: x.gidx):
            if d.dma_sem is None and d.eng == o.eng:
                if o.eng == "pe":
                    continue
                if o.dma_sem is None and o.eidx - d.eidx > self.same_engine_dist:
                    continue
            self._dep_on(o, d, obs)
        for d in force_deps:
            self._dep_on(o, d, obs)
        if o.dma_sem is None:
            clock = dict(obs)
            clock[("eng", o.eng)] = o.eidx
            o.clock = clock
        for t in o.reads:
            self.readers.setdefault(t, []).append(o)
        for t in o.writes:
            if t in bank_toks:
                self.bank_last[t][o.eng] = o
                continue
            self.last_writer[t] = o
            self.readers[t] = []
        self.ops.append(o)
        self.by_eng[o.eng].append(o)

    def barrier(self, fns):
        new = []
        for e, fn in fns.items():
            o = Op(e, fn, (), ())
            force = []
            if self.by_eng["pe"]:
                force.append(self.by_eng["pe"][-1])
            for other in fns:
                if other != e and self.by_eng[other]:
                    force.append(self.by_eng[other][-1])
            obs = self.observed.setdefault(e, {})
            for key, val in self.dma_cum.items():
                if obs.get(("dma", key), 0) < val:
                    obs[("dma", key)] = val
                    o.waits.append(("dma", key, val))
            self._add(o, force_deps=force)
            new.append(o)
        self.bar_ops = new
        self.last_writer = {}
        self.readers = {}
        self.bank_last = {}

    def emit(self, final_wait_sems=()):
        nc = self.nc
        with ExitStack() as st:
            esem = {e: st.enter_context(nc.semaphore("s_" + e)) for e in ENGINES}
            dsem = {k: st.enter_context(nc.semaphore("d_%s" % (str(k),))) for k in self.dma_cum}
            for e in ENGINES:
                c = 0
                for o in self.by_eng[e]:
                    if o.dma_sem is None and o.signal:
                        c += 1
                        o.count = c
            block = st.enter_context(nc.Block())

            def run(e, eng):
                for o in self.by_eng[e]:
                    for w in o.waits:
                        if w[0] == "dma":
                            eng.wait_ge(dsem[w[1]], w[2])
                        else:
                            eng.wait_ge(esem[w[1]], w[2].count)
                    inst = o.fn(eng)
                    if o.dma_sem is not None:
                        inst.then_inc(dsem[o.dma_sem], 16)
                    elif o.signal:
                        inst.then_inc(esem[e], 1)
                if e == "sp":
                    for k in final_wait_sems:
                        eng.wait_ge(dsem[k], self.dma_cum[k])

            @block.tensor
            def _(eng):
                run("pe", eng)

            @block.scalar
            def _(eng):
                run("act", eng)

            @block.vector
            def _(eng):
                run("dve", eng)

            @block.gpsimd
            def _(eng):
                run("pool", eng)

            @block.sync
            def _(eng):
                run("sp", eng)


class _Rec:
    def __getattr__(self, name):
        def mk(*a, **kw):
            return lambda eng: getattr(eng, name)(*a, **kw)
        return mk


_E = _Rec()


class _Stop(Exception):
    pass


def stop_at(name):
    import os
    if os.environ.get("MK_STOP", "") == name:
        raise _Stop(name)


class Arena:
    def __init__(self, t, nwords):
        self.t = t
        self.n = nwords
        self.off = 0

    def f32(self, ncols):
        assert self.off + ncols <= self.n, ("arena overflow", self.off, ncols, self.n)
        ap = self.t[:, self.off:self.off + ncols]
        self.off += ncols
        return ap

    def bf16(self, ncols):
        w = (ncols + 1) // 2
        ap = self.f32(w).bitcast(BF16)
        return ap[:, 0:ncols]


CF_LAYOUT = {}


def _cf_layout():
    off = 0
    for name, n in (("ident", 128), ("U", 128), ("ones", 128), ("Lst", 128), ("negm", 128),
                    ("Mp", 512), ("qd", 512), ("kdp", 4), ("kds", 4), ("coss", 64), ("sins", 64),
                    ("mhalf", 4), ("mone", 4)):
        CF_LAYOUT[name] = (off, n)
        off += n
    return off


CF_N = _cf_layout()


def _host_consts():
    lg = np.log1p(-np.exp2(-5.0 - np.arange(4, dtype=np.float32))).astype(np.float64)
    sc = 128.0 ** -0.5
    i = np.arange(128)
    cf = np.zeros((128, CF_N), np.float32)

    def put(name, arr):
        o, n = CF_LAYOUT[name]
        cf[:arr.shape[0], o:o + n] = arr.reshape(arr.shape[0], n)

    put("ident", np.eye(128, dtype=np.float32))
    put("U", (i[:, None] <= i[None, :]).astype(np.float32))
    put("ones", np.ones((128, 128), np.float32))
    put("Lst", (i[:, None] > i[None, :]).astype(np.float32))
    put("negm", np.where(i[None, :] >= i[:, None], 0.0, -30000.0).astype(np.float32))
    diff = (i[None, :] - i[:, None]).astype(np.float64)
    same = (i[:, None] // 64) == (i[None, :] // 64)
    fwd = (i[:, None] < 64) & (i[None, :] >= 64)
    Mp = np.zeros((128, 4, 128), np.float64)
    qd = np.zeros((128, 4, 128), np.float64)
    for h in range(4):
        Mp[:, h, :] = np.where(same, np.exp(lg[h] * np.abs(diff)),
                               np.where(fwd, np.exp(lg[h] * diff), 0.0)) * sc
        qd[:, h, :] = (sc * np.exp(lg[h] * (i + 1.0)))[None, :]
    put("Mp", Mp.astype(np.float32))
    put("qd", qd.astype(np.float32))
    put("kdp", np.exp(lg[None, :] * (127.0 - i[:, None])).astype(np.float32))
    kds = np.exp(lg[None, :] * (63.0 - i[:64, None])).astype(np.float32)
    put("kds", kds)
    inv = (10000.0 ** (-np.arange(64, dtype=np.float32) / 64.0)).astype(np.float32)
    pos = np.arange(T, dtype=np.float32)
    ang = (pos[:, None] * inv[None, :]).astype(np.float32)
    cosp = np.cos(ang).astype(np.float32).reshape(T // 128, 128, 64)
    sinp = np.sin(ang).astype(np.float32).reshape(T // 128, 128, 64)
    poss = np.arange(NS, dtype=np.float32) + float(PAST)
    angs = (poss[:, None] * inv[None, :]).astype(np.float32)
    put("coss", np.cos(angs).astype(np.float32))
    put("sins", np.sin(angs).astype(np.float32))
    put("mhalf", np.full((128, 4), -0.5, np.float32))
    put("mone", np.full((128, 4), -1.0, np.float32))
    g128 = [float(np.exp(lg[h] * 128.0)) for h in range(4)]
    g64 = [float(np.exp(lg[h] * 64.0)) for h in range(4)]
    return cf, cosp, sinp, g128, g64


def build_program(g128, g64):
    nc = bass.Bass("TRN2", target_bir_lowering=False)

    def din(name, shape):
        return nc.dram_tensor(name, list(shape), F32, kind="ExternalInput").ap()

    def dout(name, shape):
        return nc.dram_tensor(name, list(shape), F32, kind="ExternalOutput").ap()

    xp = din("xp", (2, T, D))
    xs_d = din("xs", (2, NS, D))
    ck = din("ck", (2, PAST, 512))
    cv = din("cv", (2, PAST, 512))
    clf = din("clf", (2, PAST, 8))
    sret = din("sret", (2, 4, 128, 128))
    w_in = din("w_in", (D, DIN))
    w_out = din("w_out", (D, D))
    w_gate = din("w_gate", (D, DFF))
    w_up = din("w_up", (D, DFF))
    w_down = din("w_down", (DFF, D))
    cf_d = din("cf32", (128, CF_N))
    cosp_d = din("cosp", (T // 128, 128, 64))
    sinp_d = din("sinp", (T // 128, 128, 64))
    vec_d = din("vecs", (128, 8 + 8 + 4 + 8))
    gfin_d = din("gfin_b", (128, D))

    y_p = dout("y_p", (2 * T, D))
    y_s = dout("y_s", (2 * NS, D))
    sr_p = dout("sr_p", (2, 4, 128, 128))
    kf_p = dout("kf_p", (2, T, 512))
    vf_p = dout("vf_p", (2, T, 512))
    lf_p = dout("lf_p", (2, T, 8))
    sr_s = dout("sr_s", (2, 4, 128, 128))
    kf_s = dout("kf_s", (2, NS, 512))
    vf_s = dout("vf_s", (2, NS, 512))
    lf_s = dout("lf_s", (2, NS, 8))
    NTOK = 2 * T + 2 * NS
    x1s = nc.dram_tensor("x1s", [NTOK, D], F32, kind="Internal").ap()

    with ExitStack() as st:
        AW = 52600
        arena_t = st.enter_context(nc.sbuf_tensor("arena", [128, AW], F32))
        ps = st.enter_context(nc.psum_tensor("ps", [128, 4096], F32))
        S = Sched(nc)
        A = Arena(arena_t, AW)

        def B(i):
            return ps[:, i * 512:(i + 1) * 512]

        def Bt(i):
            return "B%d" % i

        try:
            cf = A.f32(CF_N)
            vecs = A.f32(28)
            identb = A.bf16(128)
            negmb = A.bf16(128)
            dummy = A.f32(4)

            def C(name, rows=128):
                o, n = CF_LAYOUT[name]
                return cf[0:rows, o:o + n]

            gmixc = vecs[:, 0:8]
            gffnc = vecs[:, 8:16]
            gnc = vecs[:, 16:20]
            bfb = vecs[:, 20:28]

            S.dma("sp", _E.dma_start(out=cf, in_=cf_d), writes=["cf"], sem="cf")
            S.dma("sp", _E.dma_start(out=vecs, in_=vec_d), writes=["vecs"], sem="cf")
            S.op("pool", _E.tensor_copy(out=identb, in_=C("ident")), reads=["cf"], writes=["identb"])
            S.op("pool", _E.tensor_copy(out=negmb, in_=C("negm")), reads=["cf"], writes=["negmb"])

            def bar():
                S.barrier({
                    "act": _E.copy(out=dummy[:, 0:1], in_=dummy[:, 1:2]),
                    "dve": _E.memset(dummy[:, 2:3], 0.0),
                    "pool": _E.memset(dummy[:, 3:4], 0.0),
                })

            S.op("pool", _E.memset(dummy, 0.0), writes=["dummy"])

            mark_persist = A.off
            stop_at("pro0")

            w_in_bf = A.bf16(8 * DIN).rearrange("p (k c) -> p k c", k=8)
            w_out_bf = A.bf16(8 * D).rearrange("p (k c) -> p k c", k=8)
            mark_w1 = A.off
            NSTG = 4
            stage = [A.f32(DIN) for _ in range(NSTG)]
            A.off = mark_w1
            xt = A.f32(G * D).rearrange("p (j c) -> p j c", j=G)
            xsb = [A.bf16(D), A.bf16(D)]
            hT = A.bf16(8 * G * 128).rearrange("p (k t) -> p k t", k=8)
            QB = A.bf16(4 * G * 2 * 128).rearrange("p (c j e t) -> p c j e t", c=4, j=G, e=2)
            kfT = A.bf16(4 * T).rearrange("p (c t) -> p c t", c=4)
            vaug_off = A.off
            Vaug = A.bf16(16 * 8 * 65).rearrange("p (t h e) -> p t h e", t=16, h=8)
            qkrot = [A.bf16(1024), A.bf16(1024)]
            rt = [A.f32(512) for _ in range(4)]
            vtok = [A.bf16(512), A.bf16(512)]
            gsb1 = A.f32(512)
            egs1 = A.f32(512)
            gsb = [gsb1, gsb1]
            egs = [egs1, egs1]
            onb = [rt[2], rt[3]]
            kfo = [A.f32(512), A.f32(512)]
            vfo = [A.f32(512), A.f32(512)]
            qkT = [A.bf16(1024).rearrange("p (h t) -> p h t", h=8) for _ in range(2)]
            qdT = [A.bf16(512).rearrange("p (h t) -> p h t", h=4) for _ in range(2)]
            kdtok = [A.bf16(512), A.bf16(512)]
            sTm = [A.bf16(512).rearrange("p (h t) -> p h t", h=4) for _ in range(2)]
            Sst = A.f32(512).rearrange("p (h e) -> p h e", h=4)
            Sbf = A.bf16(512).rearrange("p (h e) -> p h e", h=4)
            NPT = 12
            PT = [A.bf16(128) for _ in range(NPT)]
            mixb = [A.bf16(1024), A.bf16(1024)]
            mixT = [A.bf16(1024).rearrange("p (k t) -> p k t", k=8) for _ in range(2)]
            ctab = A.f32(128).rearrange("p (t h) -> p t h", h=8)
            Btab = [A.f32(128).rearrange("p (t h) -> p t h", h=8) for _ in range(2)]
            Rcar = A.f32(8)
            cst = [A.f32(128), A.f32(128)]
            sm = A.f32(128)
            clfs = A.f32(256).rearrange("p (t h) -> p t h", h=8)
            atab = A.f32(256).rearrange("p (t h) -> p t h", h=8)
            tots = A.f32(256).rearrange("p (t h) -> p t h", h=8)
            sufs = A.f32(256).rearrange("p (t h) -> p t h", h=8)
            negbq = A.f32(8)
            _save = A.off
            A.off = vaug_off + 8 * 65
            kst = [A.f32(512), A.f32(512)]
            vst = [A.f32(512), A.f32(512)]
            kTt = [A.bf16(512).rearrange("p (c t) -> p c t", c=4) for _ in range(2)]
            Vt = [A.bf16(8 * 65).rearrange("p (h e) -> p h e", h=8) for _ in range(2)]
            PTs = [A.bf16(512).rearrange("p (h q) -> p h q", h=8) for _ in range(2)]
            assert A.off <= vaug_off + 8 * 65 * 8, (A.off, vaug_off)
            A.off = _save
            print("phase1 arena words used:", A.off, "of", AW)

            ss = sm[:, 0:G]
            rs = sm[:, 8:8 + G]
            st6 = sm[:, 16:40].rearrange("p (h s) -> p h s", h=4)
            mv = sm[:, 40:48].rearrange("p (h s) -> p h s", h=4)
            rstdg = sm[:, 48:52]
            rcp = sm[:, 56:64]
            xl = sm[:, 64:72]
            la = sm[:, 72:80]
            le = sm[:, 80:88]
            ll = sm[:, 88:96]
            lmn = sm[:, 96:104]
            lft = [sm[:, 104:112], sm[:, 112:120]]

            mhalf = C("mhalf")

            def convert_rows(dst, src_sb, ncols, scale_col, engines=("act", "dve", "pool"), rd=()):
                shares = {"act": 0.44, "dve": 0.44, "pool": 0.12}
                bounds = [0]
                acc_ = 0.0
                for eng in engines:
                    acc_ += shares[eng]
                    bounds.append(min(ncols, int(round(ncols * acc_ / 8.0)) * 8))
                bounds[-1] = ncols
                for i, eng in enumerate(engines):
                    c0, c1 = bounds[i], bounds[i + 1]
                    if c0 >= c1:
                        continue
                    if scale_col is None:
                        if eng == "act":
                            S.op("act", _E.copy(out=dst[:, c0:c1], in_=src_sb[:, c0:c1]), reads=rd)
                        else:
                            S.op(eng, _E.tensor_copy(out=dst[:, c0:c1], in_=src_sb[:, c0:c1]), reads=rd)
                    else:
                        if eng == "act":
                            S.op("act", _E.activation(out=dst[:, c0:c1], in_=src_sb[:, c0:c1],
                                                                           func=AF.Identity, scale=scale_col), reads=rd)
                        else:
                            S.op(eng, _E.tensor_scalar(out=dst[:, c0:c1], in0=src_sb[:, c0:c1],
                                                                            scalar1=scale_col, scalar2=None,
                                                                            op0=ALU.mult), reads=rd)

            for k in range(8):
                sg = stage[k % NSTG]
                tok = "stage%d" % (k % NSTG)
                S.dma("sp", _E.dma_start(out=sg[:, 0:DIN], in_=w_in[k * 128:(k + 1) * 128, :]),
                      writes=[tok], sem=tok)
                convert_rows(w_in_bf[:, k, :], sg, DIN, gmixc[:, k:k + 1], rd=[tok, "vecs"])
            for k in range(8):
                sg = stage[k % NSTG]
                tok = "stage%d" % (k % NSTG)
                S.dma("sp", _E.dma_start(out=sg[:, 0:D], in_=w_out[k * 128:(k + 1) * 128, :]),
                      writes=[tok], sem=tok)
                convert_rows(w_out_bf[:, k, :], sg, D, gnc[:, k:k + 1] if k < 4 else None, rd=[tok, "vecs"])
            bar()
            stop_at("pro")

            tpb = B(0).bitcast(BF16)
            tp3 = tpb.rearrange("p (k t) -> p k t", k=8)
            tpf3 = B(0).rearrange("p (k t) -> p k t", k=4)
            cnt = {"tile": 0, "pt": 0, "sl": 0, "ev": 0}

            def load_x(src_rows, n, j):
                S.dma("sp", _E.dma_start(out=xt[0:n, j, :], in_=src_rows), writes=["xt%d" % j], sem="xt%d" % j)

            def stage_A(n, j):
                b = cnt["tile"] % 2
                S.op("pool", _E.memset(ss[0:n, j:j + 1], 0.0), writes=["ss%d" % j])
                S.op("act", _E.activation(out=xsb[b][0:n, :], in_=xt[0:n, j, :], func=AF.Square,
                                                   accum_out=ss[0:n, j:j + 1]),
                     reads=["xt%d" % j], writes=["xsb%d" % b, "ss%d" % j])
                S.op("dve", _E.tensor_scalar(out=rs[0:n, j:j + 1], in0=ss[0:n, j:j + 1], scalar1=1.0 / D,
                                                      scalar2=1e-6, op0=ALU.mult, op1=ALU.add),
                     reads=["ss%d" % j], writes=["rs%d" % j])
                S.op("pool", _E.tensor_tensor(out=rs[0:n, j:j + 1], in0=rs[0:n, j:j + 1], in1=mhalf[0:n, 0:1],
                                                       op=ALU.pow), reads=["rs%d" % j], writes=["rs%d" % j])
                S.op("dve", _E.tensor_scalar(out=xsb[b][0:n, :], in0=xt[0:n, j, :], scalar1=rs[0:n, j:j + 1],
                                                      scalar2=None, op0=ALU.mult),
                     reads=["xt%d" % j, "rs%d" % j], writes=["xsb%d" % b])
                for k in range(8):
                    S.op("pe", _E.transpose(out=tpb[:, k * 128:k * 128 + n],
                                                          in_=xsb[b][0:n, k * 128:(k + 1) * 128],
                                                          identity=identb[0:n, 0:n]),
                         reads=["xsb%d" % b], writes=[Bt(0)])
                S.op("act", _E.copy(out=hT[:, :, j * 128:j * 128 + n], in_=tp3[:, :, 0:n]),
                     writes=[Bt(0), "hT%d" % j])
                cnt["tile"] += 1

            def evac_copy(dst, src, btok, wtoks):
                eng = "act" if cnt["ev"] % 2 == 0 else "dve"
                cnt["ev"] += 1
                if eng == "act":
                    S.op("act", _E.copy(out=dst, in_=src), writes=[btok] + wtoks)
                else:
                    S.op("dve", _E.tensor_copy(out=dst, in_=src), writes=[btok] + wtoks)

            def stage_B(NT, ntl, tok0):
                hts = ["hT%d" % j for j in range(ntl)]
                for c in range(8):
                    col0 = 2048 + c * 128 if c < 4 else 2560 + (c - 4) * 128
                    bk = 1 + (c % 4)
                    for k in range(8):
                        S.op("pe", _E.matmul(
                            B(bk)[:, 0:NT], lhsT=w_in_bf[:, k, col0:col0 + 128], rhs=hT[:, k, 0:NT],
                            start=(k == 0), stop=(k == 7)), reads=hts, writes=[Bt(bk)])
                    if c < 4:
                        tw = min(NT, 128)
                        for e_ in range(2):
                            r0_, r1_ = e_ * 64, (e_ + 1) * 64
                            evac_copy(QB[r0_:r1_, c, 0:ntl, e_, 0:tw],
                                      B(bk)[r0_:r1_, 0:NT].rearrange("p (j t) -> p j t", t=tw),
                                      Bt(bk), ["qfT%d_%d" % (c, e_)])
                    else:
                        evac_copy(kfT[:, c - 4, tok0:tok0 + NT], B(bk)[:, 0:NT], Bt(bk), ["kfT%d" % (c - 4)])

            def inproj_tok(n, j, bk, col0, ncols):
                for k in range(8):
                    S.op("pe", _E.matmul(B(bk)[0:n, 0:ncols], lhsT=hT[:, k, j * 128:j * 128 + n],
                                                       rhs=w_in_bf[:, k, col0:col0 + ncols],
                                                       start=(k == 0), stop=(k == 7)),
                         reads=["hT%d" % j], writes=[Bt(bk)])

            def stage_C(n, j, b, cos_ap, sin_ap, cs_tok, kt, kf_dst, vf_dst, lf_dst, vaug_dst):
                inproj_tok(n, j, 1, 0, 512)
                yield
                inproj_tok(n, j, 2, 512, 512)
                yield
                inproj_tok(n, j, 3, 1024, 512)
                S.op("act", _E.copy(out=vtok[b][0:n, :], in_=B(3)[0:n, :]), writes=[Bt(3), "vtok%d" % b])
                yield
                qk4 = ps[0:n, 512:1536].rearrange("p (h t f) -> p h t f", t=2, f=64)
                x1 = qk4[:, :, 0, :]
                x2 = qk4[:, :, 1, :]
                cosb = cos_ap.unsqueeze(1).to_broadcast([n, 8, 64])
                sinb = sin_ap.unsqueeze(1).to_broadcast([n, 8, 64])
                r3 = [r[0:n, :].rearrange("p (h f) -> p h f", f=64) for r in rt]
                qr4 = qkrot[b][0:n, :].rearrange("p (h t f) -> p h t f", t=2, f=64)
                bb = [Bt(1), Bt(2)]
                S.op("dve", _E.tensor_tensor(out=r3[0], in0=x1, in1=cosb, op=ALU.mult), reads=[cs_tok], writes=bb + ["rt0"])
                S.op("dve", _E.tensor_tensor(out=r3[1], in0=x2, in1=sinb, op=ALU.mult), reads=[cs_tok], writes=bb + ["rt1"])
                S.op("pool", _E.tensor_tensor(out=qr4[:, :, 0, :], in0=r3[0], in1=r3[1], op=ALU.subtract),
                     reads=["rt0", "rt1"], writes=["qkrot%da" % b])
                S.op("dve", _E.tensor_tensor(out=r3[2], in0=x1, in1=sinb, op=ALU.mult), reads=[cs_tok], writes=bb + ["rt2"])
                S.op("dve", _E.tensor_tensor(out=r3[3], in0=x2, in1=cosb, op=ALU.mult), reads=[cs_tok], writes=bb + ["rt3"])
                S.op("pool", _E.tensor_tensor(out=qr4[:, :, 1, :], in0=r3[2], in1=r3[3], op=ALU.add),
                     reads=["rt2", "rt3"], writes=["qkrot%db" % b])
                inproj_tok(n, j, 3, 1536, 512)
                S.op("act", _E.copy(out=gsb[b][0:n, :], in_=B(3)[0:n, :]), writes=[Bt(3), "gsb"])
                S.op("act", _E.activation(out=egs[b][0:n, :], in_=B(3)[0:n, :], func=AF.Exp, scale=-1.0),
                     writes=[Bt(3), "egs"])
                yield
                S.op("act", _E.activation(out=egs[b][0:n, :], in_=egs[b][0:n, :], func=AF.Ln, bias=1.0),
                     reads=["egs"], writes=["egs"])
                S.op("act", _E.activation(out=egs[b][0:n, :], in_=egs[b][0:n, :], func=AF.Exp, scale=-1.0),
                     reads=["egs"], writes=["egs"])
                yield
                inproj_tok(n, j, 1, 2560, 512)
                yield
                inproj_tok(n, j, 2, 3072, 512)
                yield
                inproj_tok(n, j, 3, 3584, 8)
                S.op("act", _E.copy(out=kfo[b][0:n, :], in_=B(1)[0:n, :]), writes=[Bt(1), "kfo%d" % b])
                S.op("act", _E.copy(out=vfo[b][0:n, :], in_=B(2)[0:n, :]), writes=[Bt(2), "vfo%d" % b])
                S.dma("sp", _E.dma_start(out=kf_dst, in_=kfo[b][0:n, :]), reads=["kfo%d" % b], sem="kfo%d" % b)
                S.dma("sp", _E.dma_start(out=vf_dst, in_=vfo[b][0:n, :]), reads=["vfo%d" % b], sem="vfo%d" % b)
                S.op("pool", _E.tensor_copy(out=vaug_dst, in_=vfo[b][0:n, :].rearrange("p (h f) -> p h f", f=64)),
                     reads=["vfo%d" % b], writes=["vaug"])
                S.op("dve", _E.tensor_tensor(out=xl[0:n, :], in0=B(3)[0:n, 0:8], in1=bfb[0:n, :], op=ALU.add),
                     writes=[Bt(3), "xl"])
                S.op("act", _E.activation(out=la[0:n, :], in_=xl[0:n, :], func=AF.Abs), reads=["xl"], writes=["la"])
                S.op("act", _E.activation(out=le[0:n, :], in_=la[0:n, :], func=AF.Exp, scale=-1.0),
                     reads=["la"], writes=["le"])
                S.op("act", _E.activation(out=ll[0:n, :], in_=le[0:n, :], func=AF.Ln, bias=1.0),
                     reads=["le"], writes=["ll"])
                S.op("dve", _E.tensor_scalar(out=lmn[0:n, :], in0=xl[0:n, :], scalar1=0.0, scalar2=None,
                                                      op0=ALU.min), reads=["xl"], writes=["lmn"])
                S.op("dve", _E.tensor_tensor(out=lft[b][0:n, :], in0=lmn[0:n, :], in1=ll[0:n, :],
                                                      op=ALU.subtract), reads=["lmn", "ll"], writes=["lft%d" % b])
                S.dma("sp", _E.dma_start(out=lf_dst, in_=lft[b][0:n, :]), reads=["lft%d" % b], sem="lft%d" % b)
                yield

            def stage_ret(n, b, kd_ap, gdec):
                yield
                for h in range(8):
                    S.op("pe", _E.transpose(out=tpb[:, h * 128:h * 128 + n],
                                                          in_=qkrot[b][0:n, h * 128:(h + 1) * 128],
                                                          identity=identb[0:n, 0:n]),
                         reads=["qkrot%da" % b, "qkrot%db" % b], writes=[Bt(0)])
                S.op("act", _E.copy(out=qkT[b][:, :, 0:n], in_=tp3[:, :, 0:n]), writes=[Bt(0), "qkT%d" % b])
                yield
                qd3 = C("qd").rearrange("p (h t) -> p h t", h=4)
                S.op("pool", _E.tensor_tensor(out=qdT[b][:, :, 0:n], in0=qkT[b][:, 0:4, 0:n], in1=qd3[:, :, 0:n],
                                                       op=ALU.mult), reads=["qkT%d" % b], writes=["qdT%d" % b])
                for h in range(4):
                    S.op("pool", _E.tensor_scalar(out=kdtok[b][0:n, h * 128:(h + 1) * 128],
                                                                in0=qkrot[b][0:n, 512 + h * 128:512 + (h + 1) * 128],
                                                                scalar1=kd_ap[0:n, h:h + 1], scalar2=None, op0=ALU.mult),
                         reads=["qkrot%da" % b, "qkrot%db" % b], writes=["kdtok%d_%d" % (b, h)])
                for h in range(4):
                    S.op("pe", _E.matmul(B(1)[0:n, h * 128:h * 128 + n], lhsT=qkT[b][:, 4 + h, 0:n],
                                                       rhs=qkT[b][:, h, 0:n], start=True, stop=True),
                         reads=["qkT%d" % b], writes=[Bt(1)])
                Mp3 = C("Mp").rearrange("p (h t) -> p h t", h=4)
                s43 = B(1).rearrange("p (h t) -> p h t", h=4)
                S.op("dve", _E.tensor_tensor(out=sTm[b][0:n, :, 0:n], in0=s43[0:n, :, 0:n], in1=Mp3[0:n, :, 0:n],
                                                      op=ALU.mult), writes=[Bt(1), "sTm%d" % b])
                yield
                for h in range(4):
                    S.op("pe", _E.matmul(B(3)[0:n, h * 128:(h + 1) * 128], lhsT=sTm[b][0:n, h, 0:n],
                                                       rhs=vtok[b][0:n, h * 128:(h + 1) * 128], start=True, stop=False),
                         reads=["sTm%d" % b, "vtok%d" % b], writes=[Bt(3)])
                    S.op("pe", _E.matmul(B(3)[0:n, h * 128:(h + 1) * 128], lhsT=qdT[b][:, h, 0:n],
                                                       rhs=Sbf[:, h, :], start=False, stop=True),
                         reads=["qdT%d" % b, "Sbf"], writes=[Bt(3)])
                for h in range(4):
                    S.op("pe", _E.matmul(B(2)[:, h * 128:(h + 1) * 128],
                                                       lhsT=kdtok[b][0:n, h * 128:(h + 1) * 128],
                                                       rhs=vtok[b][0:n, h * 128:(h + 1) * 128], start=True, stop=True),
                         reads=["kdtok%d_%d" % (b, h), "vtok%d" % b], writes=[Bt(2)])
                for h in range(4):
                    S.op("dve", _E.scalar_tensor_tensor(out=Sst[:, h, :], in0=Sst[:, h, :], scalar=gdec[h],
                                                                      in1=B(2)[:, h * 128:(h + 1) * 128],
                                                                      op0=ALU.mult, op1=ALU.add),
                         writes=[Bt(2), "Sst"])
                S.op("pool", _E.tensor_copy(out=Sbf, in_=Sst), reads=["Sst"], writes=["Sbf"])
                yield
                for h in range(4):
                    S.op("dve", _E.bn_stats(out=st6[0:n, h, :], in_=B(3)[0:n, h * 128:(h + 1) * 128]),
                         writes=[Bt(3), "st6_%d" % h])
                for h in range(4):
                    S.op("dve", _E.bn_aggr(out=mv[0:n, h, :], in_=st6[0:n, h, :]),
                         reads=["st6_%d" % h], writes=["mv%d" % h])
                S.op("dve", _E.tensor_scalar(out=rstdg[0:n, :], in0=mv[0:n, :, 1], scalar1=1e-5, scalar2=None,
                                                      op0=ALU.add), reads=["mv%d" % h for h in range(4)], writes=["rstdg"])
                S.op("pool", _E.tensor_tensor(out=rstdg[0:n, :], in0=rstdg[0:n, :], in1=mhalf[0:n, :], op=ALU.pow),
                     reads=["rstdg"], writes=["rstdg"])
                for h in range(4):
                    S.op("dve", _E.tensor_scalar(out=onb[b][0:n, h * 128:(h + 1) * 128],
                                                               in0=B(3)[0:n, h * 128:(h + 1) * 128],
                                                               scalar1=mv[0:n, h, 0:1], scalar2=rstdg[0:n, h:h + 1],
                                                               op0=ALU.subtract, op1=ALU.mult),
                         reads=["rstdg", "mv%d" % h], writes=[Bt(3), "rt%d" % (2 + b)])
                S.op("pool", _E.tensor_tensor(out=gsb[b][0:n, :], in0=gsb[b][0:n, :], in1=onb[b][0:n, :], op=ALU.mult),
                     reads=["gsb", "rt%d" % (2 + b)], writes=["gsb"])
                S.op("pool", _E.tensor_tensor(out=mixb[b][0:n, 0:512], in0=gsb[b][0:n, :], in1=egs[b][0:n, :],
                                                       op=ALU.mult), reads=["gsb", "egs"], writes=["mixr%d" % b])
                yield

            def cumsum_tile(n, b, first):
                S.op("pe", _E.matmul(B(4)[0:n, 0:8], lhsT=C("U")[0:n, 0:n], rhs=lft[b][0:n, :], start=True, stop=True),
                     reads=["lft%d" % b, "cf"], writes=[Bt(4)])
                S.op("pe", _E.matmul(B(4)[:, 8:16], lhsT=C("ones")[0:n, :], rhs=lft[b][0:n, :], start=True, stop=True),
                     reads=["lft%d" % b, "cf"], writes=[Bt(4)])

            def fox_finish_head(n, b, h, acc_bank):
                S.op("dve", _E.reciprocal(out=rcp[0:n, h:h + 1], in_=B(acc_bank)[0:n, 64:65]),
                     writes=[Bt(acc_bank), "rcp%d" % h])
                S.op("dve", _E.tensor_scalar(out=mixb[b][0:n, 512 + h * 64:512 + (h + 1) * 64],
                                                      in0=B(acc_bank)[0:n, 0:64], scalar1=rcp[0:n, h:h + 1],
                                                      scalar2=None, op0=ALU.mult),
                     reads=["rcp%d" % h], writes=[Bt(acc_bank), "mixf%d_%d" % (b, h)])

            def stage_fox_prompt(j, b, t, filler=None):
                cumsum_tile(128, b, t == 0)
                S.op("dve", _E.tensor_tensor(out=ctab[:, t, :], in0=B(4)[:, 0:8], in1=Rcar, op=ALU.add),
                     reads=["Rcar"], writes=[Bt(4), "ctab"])
                S.op("dve", _E.tensor_tensor(out=Rcar, in0=B(4)[:, 8:16], in1=Rcar, op=ALU.add),
                     writes=[Bt(4), "Rcar"])
                bt_ = Btab[t % 2]
                S.op("dve", _E.tensor_tensor(out=bt_[:, 0:t + 1, :],
                                                      in0=Rcar.unsqueeze(1).to_broadcast([128, t + 1, 8]),
                                                      in1=ctab[:, 0:t + 1, :], op=ALU.subtract),
                     reads=["Rcar", "ctab"], writes=["Btab%d" % (t % 2)])
                macros = []
                for p_ in range(4):
                    kts = list(range(t + 1))
                    for i0_ in range(0, len(kts), 2):
                        macros.append((p_, kts[i0_:i0_ + 2]))
                slots = (5, 6)
                LAGM = 2
                pend = []

                def emit_pv(mi):
                    p_, kts_ = macros[mi]
                    for q_, kt in enumerate(kts_):
                        for e_ in range(2):
                            h = 2 * p_ + e_
                            pt = pend[mi][q_ * 2 + e_]
                            accb = 7 if e_ == 0 else 4
                            S.op("pe", _E.matmul(B(accb)[:, 0:65], lhsT=PT[pt], rhs=Vaug[:, kt, h, :],
                                                 start=(kt == 0), stop=(kt == t)),
                                 reads=["PT%d" % pt, "vaug"], writes=[Bt(accb)])
                            if kt == t:
                                fox_finish_head(128, b, h, accb)

                for mi, (p_, kts_) in enumerate(macros):
                    sl = slots[cnt["sl"] % 2]
                    cnt["sl"] += 1
                    pts = []
                    for q_, kt in enumerate(kts_):
                        S.op("pe", _E.matmul(
                            B(sl)[:, q_ * 256:(q_ + 1) * 256], lhsT=kfT[:, p_, kt * 128:(kt + 1) * 128],
                            rhs=QB[:, p_, j, :, :].rearrange("p e t -> p (e t)"), start=True, stop=(kt != t)),
                            reads=["kfT%d" % p_, "qfT%d_0" % p_, "qfT%d_1" % p_], writes=[Bt(sl)])
                        if kt == t:
                            for e_ in range(2):
                                c0_ = q_ * 256 + e_ * 128
                                S.op("pe", _E.matmul(B(sl)[:, c0_:c0_ + 128], lhsT=identb, rhs=negmb,
                                                     start=False, stop=(e_ == 1)),
                                     reads=["identb", "negmb"], writes=[Bt(sl)])
                    for q_, kt in enumerate(kts_):
                        for e_ in range(2):
                            h = 2 * p_ + e_
                            c0_ = q_ * 256 + e_ * 128
                            pt = cnt["pt"] % NPT
                            cnt["pt"] += 1
                            pts.append(pt)
                            S.op("act", _E.activation(
                                out=PT[pt], in_=B(sl)[:, c0_:c0_ + 128], func=AF.Exp, bias=bt_[:, kt, h:h + 1], scale=0.125),
                                reads=["Btab%d" % (t % 2)], writes=[Bt(sl), "PT%d" % pt])
                    pend.append(pts)
                    if mi >= LAGM:
                        emit_pv(mi - LAGM)
                    if filler is not None:
                        for _ in range(3 if t < 4 else (2 if t < 8 else 1)):
                            next(filler, None)
                for mi in range(max(0, len(macros) - LAGM), len(macros)):
                    emit_pv(mi)
                if filler is not None:
                    for _ in filler:
                        pass

            def stage_E(n, j, b, x1_rows):
                mixtoks = ["mixr%d" % b] + ["mixf%d_%d" % (b, h) for h in range(8)]
                for k in range(8):
                    S.op("pe", _E.transpose(out=tpb[:, k * 128:k * 128 + n],
                                                          in_=mixb[b][0:n, k * 128:(k + 1) * 128],
                                                          identity=identb[0:n, 0:n]),
                         reads=mixtoks, writes=[Bt(0)])
                S.op("act", _E.copy(out=mixT[b][:, :, 0:n], in_=tp3[:, :, 0:n]), writes=[Bt(0), "mixT%d" % b])
                yield
                for half in range(2):
                    if half == 1:
                        yield
                    for k in range(8):
                        S.op("pe", _E.matmul(B(1 + half)[0:n, :], lhsT=mixT[b][:, k, 0:n],
                                                                      rhs=w_out_bf[:, k, half * 512:(half + 1) * 512],
                                                                      start=(k == 0), stop=(k == 7)),
                             reads=["mixT%d" % b], writes=[Bt(1 + half)])
                S.op("dve", _E.tensor_tensor(out=xt[0:n, j, :], in0=xt[0:n, j, :], in1=ps[0:n, 512:1536], op=ALU.add),
                     writes=[Bt(1), Bt(2), "xt%d" % j])
                S.dma("sp", _E.dma_start(out=x1_rows, in_=xt[0:n, j, :]), reads=["xt%d" % j], sem="x1st%d" % j)
                yield

            def load_cs(t):
                c = cst[t % 2]
                S.dma("sp", _E.dma_start(out=c[:, 0:64], in_=cosp_d[t]), writes=["cs%d" % (t % 2)], sem="cs%d" % (t % 2))
                S.dma("sp", _E.dma_start(out=c[:, 64:128], in_=sinp_d[t]), writes=["cs%d" % (t % 2)], sem="cs%d" % (t % 2))

            S.op("pool", _E.memset(Vaug.rearrange("p t h e -> p (t h e)"), 1.0), writes=["vaug"])
            S.op("pool", _E.memset(QB.rearrange("p c j e t -> p (c j e t)"), 0.0),
                 writes=["qfT%d_%d" % (c_, e_) for c_ in range(4) for e_ in range(2)])
            for s in range(2):
                S.op("pool", _E.memset(Sst.rearrange("p h e -> p (h e)"), 0.0), writes=["Sst"])
                S.op("pool", _E.memset(Sbf.rearrange("p h e -> p (h e)"), 0.0), writes=["Sbf"])
                S.op("pool", _E.memset(Rcar, 0.0), writes=["Rcar"])
                for j in range(G):
                    load_x(xp[s, j * 128:(j + 1) * 128, :], 128, j)
                load_cs(0)
                for g in range(T // (128 * G)):
                    for j in range(G):
                        stage_A(128, j)
                    stage_B(G * 128, G, g * G * 128)
                    def mixer_front(t_, j_):
                        c_ = cst[t_ % 2]
                        r_ = t_ * 128
                        yield from stage_C(128, j_, t_ % 2, c_[:, 0:64], c_[:, 64:128], "cs%d" % (t_ % 2), t_,
                                           kf_p[s, r_:r_ + 128, :], vf_p[s, r_:r_ + 128, :], lf_p[s, r_:r_ + 128, :],
                                           Vaug[:, t_, :, 0:64])
                        yield from stage_ret(128, t_ % 2, C("kdp"), g128)

                    def mixer_tail(t_, j_):
                        r_ = t_ * 128
                        yield from stage_E(128, j_, t_ % 2, x1s[s * T + r_:s * T + r_ + 128, :])
                        if g + 1 < T // (128 * G):
                            load_x(xp[s, (t_ + G) * 128:(t_ + G + 1) * 128, :], 128, j_)
                        yield

                    def chain_(*gens):
                        for g_ in gens:
                            yield from g_

                    for j in range(G):
                        t = g * G + j
                        b = t % 2
                        if t + 1 < 16:
                            load_cs(t + 1)
                        if j == 0:
                            for _ in mixer_front(t, j):
                                pass
                        parts = []
                        if j > 0:
                            parts.append(mixer_tail(t - 1, j - 1))
                        if j + 1 < G:
                            parts.append(mixer_front(t + 1, j + 1))
                        stage_fox_prompt(j, b, t, chain_(*parts) if parts else None)
                        if j == G - 1:
                            for _ in mixer_tail(t, j):
                                pass
                        if s == 0 and t == 3:
                            stop_at("g0")
                S.dma("sp", _E.dma_start(out=sr_p[s].rearrange("h d e -> d h e"), in_=Sst),
                      reads=["Sst"], sem="srout")
                if s == 0:
                    stop_at("s0")

            stop_at("pp")
            bar()
            S.op("pool", _E.memset(Vaug[:, 0, :, 64:65], 1.0), writes=["vaug"])
            for s in range(2):
                S.dma("sp", _E.dma_start(out=Sst, in_=sret[s].rearrange("h d e -> d h e")),
                      writes=["Sst"], sem="sldS")
                S.op("pool", _E.tensor_copy(out=Sbf, in_=Sst), reads=["Sst"], writes=["Sbf"])
                S.dma("sp", _E.dma_start(out=clfs, in_=clf[s].rearrange("(t p) h -> p t h", p=128)),
                      writes=["clfs"], sem="sldC")
                load_x(xs_d[s], NS, 0)
                clf2 = clfs.rearrange("p t h -> p (t h)")
                S.op("pe", _E.matmul(B(3)[:, 0:256], lhsT=C("Lst"), rhs=clf2, start=True, stop=True),
                     reads=["clfs", "cf"], writes=[Bt(3)])
                S.op("pe", _E.matmul(B(3)[:, 256:512], lhsT=C("ones"), rhs=clf2, start=True, stop=True),
                     reads=["clfs", "cf"], writes=[Bt(3)])
                S.op("dve", _E.tensor_copy(out=tots.rearrange("p t h -> p (t h)"), in_=B(3)[:, 256:512]),
                     writes=[Bt(3), "tots"])
                S.op("dve", _E.memset(sufs[:, NKT - 1, :], 0.0), writes=["sufs"])
                for tt in range(NKT - 2, -1, -1):
                    S.op("dve", _E.tensor_tensor(out=sufs[:, tt, :], in0=sufs[:, tt + 1, :],
                                                                 in1=tots[:, tt + 1, :], op=ALU.add),
                         reads=["tots"], writes=["sufs"])
                S.op("dve", _E.tensor_tensor(out=atab.rearrange("p t h -> p (t h)"), in0=B(3)[:, 0:256],
                                                      in1=sufs.rearrange("p t h -> p (t h)"), op=ALU.add),
                     reads=["sufs"], writes=[Bt(3), "atab"])
                n = NS
                stage_A(n, 0)
                stage_B(n, 1, 0)
                b = s % 2
                for _ in stage_C(n, 0, b, C("coss", n), C("sins", n), "cf", 0,
                                 kf_s[s], vf_s[s], lf_s[s], Vaug[0:n, 0, :, 0:64]):
                    pass
                for _ in stage_ret(n, b, C("kds"), g64):
                    pass
                S.dma("sp", _E.dma_start(out=sr_s[s].rearrange("h d e -> d h e"), in_=Sst),
                      reads=["Sst"], sem="srout")
                cumsum_tile(n, b, True)
                S.op("dve", _E.tensor_scalar(out=negbq[0:n, :], in0=B(4)[0:n, 0:8], scalar1=-1.0, scalar2=None,
                                                      op0=ALU.mult), reads=["b4data"], writes=[Bt(4), "negbq"])
                S.op("dve", _E.memset(B(7)[0:n, 0:260], 0.0), writes=[Bt(7)])
                S.op("dve", _E.memset(B(4)[0:n, 0:260], 0.0), writes=[Bt(4), "b4data"])
                for kt in range(NKT + 1):
                    kb = kt % 2
                    last = (kt == NKT)
                    rows = n if last else 128
                    if not last:
                        S.dma("sp", _E.dma_start(out=kst[kb], in_=ck[s, kt * 128:(kt + 1) * 128, :]),
                              writes=["kst%d" % kb], sem="kst%d" % kb)
                        S.dma("sp", _E.dma_start(out=vst[kb], in_=cv[s, kt * 128:(kt + 1) * 128, :]),
                              writes=["vst%d" % kb], sem="vst%d" % kb)
                        for c in range(4):
                            S.op("pe", _E.transpose(out=B(0)[:, c * 128:(c + 1) * 128],
                                                                          in_=kst[kb][:, c * 128:(c + 1) * 128],
                                                                          identity=C("ident")),
                                 reads=["kst%d" % kb, "cf"], writes=[Bt(0)])
                        S.op("dve", _E.tensor_copy(out=kTt[kb], in_=tpf3), writes=[Bt(0), "kTt%d" % kb])
                        S.op("pool", _E.tensor_copy(out=Vt[kb][:, :, 0:64],
                                                                    in_=vst[kb].rearrange("p (h f) -> p h f", f=64)),
                             reads=["vst%d" % kb], writes=["Vt%d" % kb])
                        if kt < 2:
                            S.op("pool", _E.memset(Vt[kb][:, :, 64:65], 1.0), writes=["Vt%d" % kb])
                        kT_src, kT_tok = kTt[kb], "kTt%d" % kb
                        v_src, v_tok = Vt[kb], "Vt%d" % kb
                        bias_of = (lambda h, kt=kt: atab[:, kt, h:h + 1])
                        bias_tok = "atab"
                    else:
                        kT_src, kT_tok = kfT[:, :, 0:n], "kfTnew"
                        v_src, v_tok = Vaug[:, 0, :, :], "vaug"
                        bias_of = (lambda h: negbq[0:n, h:h + 1])
                        bias_tok = "negbq"
                    bk0, bk1 = (5, 6) if kt % 2 == 0 else (2, 3)
                    for h in range(8):
                        p_, e_ = divmod(h, 2)
                        bk = bk0 if e_ == 0 else bk1
                        S.op("pe", _E.matmul(
                            B(bk)[0:rows, p_ * 64:(p_ + 1) * 64], lhsT=kT_src[e_ * 64:(e_ + 1) * 64, p_, 0:rows],
                            rhs=QB[e_ * 64:(e_ + 1) * 64, p_, 0, e_, 0:n], start=True, stop=(not last)),
                            reads=([kT_tok] if kT_tok != "kfTnew" else ["kfT%d" % p_]) + ["qfT%d_%d" % (p_, e_)], writes=[Bt(bk)])
                        if last:
                            S.op("pe", _E.matmul(B(bk)[0:n, p_ * 64:(p_ + 1) * 64],
                                                                        lhsT=identb[e_ * 64:e_ * 64 + n, e_ * 64:e_ * 64 + n],
                                                                        rhs=negmb[e_ * 64:e_ * 64 + n, e_ * 64:e_ * 64 + n],
                                                                        start=False, stop=True),
                                 reads=["identb", "negmb"], writes=[Bt(bk)])
                    for h in range(8):
                        p_, e_ = divmod(h, 2)
                        bk = bk0 if e_ == 0 else bk1
                        S.op("act", _E.activation(
                            out=PTs[kb][0:rows, h, :], in_=B(bk)[0:rows, p_ * 64:(p_ + 1) * 64], func=AF.Exp,
                            bias=bias_of(h)[0:rows, :], scale=0.125),
                            reads=[bias_tok], writes=[Bt(bk), "PTs%d_%d" % (kb, h)])
                    for h in range(8):
                        accb = 7 if h < 4 else 4
                        hh = h % 4
                        S.op("pe", _E.matmul(
                            B(accb)[0:n, hh * 65:(hh + 1) * 65], lhsT=PTs[kb][0:rows, h, :], rhs=v_src[0:rows, h, :],
                            start=False, stop=last, skip_group_check=True),
                            reads=["PTs%d_%d" % (kb, h), v_tok], writes=[Bt(accb)])
                for h in range(8):
                    accb = 7 if h < 4 else 4
                    hh = h % 4
                    S.op("dve", _E.reciprocal(out=rcp[0:n, h:h + 1],
                                                                             in_=B(accb)[0:n, hh * 65 + 64:hh * 65 + 65]),
                         writes=[Bt(accb), "rcp%d" % h])
                    S.op("dve", _E.tensor_scalar(
                        out=mixb[b][0:n, 512 + h * 64:512 + (h + 1) * 64], in0=B(accb)[0:n, hh * 65:hh * 65 + 64],
                        scalar1=rcp[0:n, h:h + 1], scalar2=None, op0=ALU.mult),
                        reads=["rcp%d" % h], writes=[Bt(accb), "mixf%d_%d" % (b, h)])
                for _ in stage_E(n, 0, b, x1s[2 * T + s * NS:2 * T + (s + 1) * NS, :]):
                    pass
                if s == 0:
                    stop_at("ss0")

            bar()

            stop_at("p1")
            A.off = mark_persist
            wg_bf = A.bf16(8 * DFF).rearrange("p (k c) -> p k c", k=8)
            wu_bf = A.bf16(8 * DFF).rearrange("p (k c) -> p k c", k=8)
            wd_bf = A.bf16(NFC * D).rearrange("p (k c) -> p k c", k=NFC)
            gfin = A.f32(D)
            mark_w2 = A.off
            stage2 = [A.f32(DFF) for _ in range(NSTG)]
            A.off = mark_w2
            NSL = G + 1
            x2t = A.f32(NSL * D).rearrange("p (j c) -> p j c", j=NSL)
            xsb2 = [A.bf16(D), A.bf16(D)]
            junk2 = A.bf16(D)
            h2T = A.bf16(8 * G * 128).rearrange("p (k t) -> p k t", k=8)
            actT = A.bf16(NFC * G * 128).rearrange("p (c t) -> p c t", c=NFC)
            sgt = [A.f32(G * 128), A.f32(G * 128)]
            sm2 = A.f32(32)
            print("phase2 arena words used:", A.off, "of", AW)
            ss2 = sm2[:, 0:G]
            rs2 = sm2[:, 8:8 + G]
            ss3 = sm2[:, 16:16 + G]
            rs3 = sm2[:, 24:24 + G]

            S.dma("sp", _E.dma_start(out=gfin, in_=gfin_d), writes=["gfin"], sem="gfin")
            kk_ = 0
            for (wsrc, wdst) in ((w_gate, wg_bf), (w_up, wu_bf)):
                for k in range(8):
                    sg = stage2[kk_ % NSTG]
                    tok = "stg2_%d" % (kk_ % NSTG)
                    kk_ += 1
                    S.dma("sp", _E.dma_start(out=sg[:, 0:DFF], in_=wsrc[k * 128:(k + 1) * 128, :]),
                          writes=[tok], sem=tok)
                    convert_rows(wdst[:, k, :], sg, DFF, gffnc[:, k:k + 1], rd=[tok])
            for k in range(NFC):
                sg = stage2[kk_ % NSTG]
                tok = "stg2_%d" % (kk_ % NSTG)
                kk_ += 1
                S.dma("sp", _E.dma_start(out=sg[:, 0:D], in_=w_down[k * 128:(k + 1) * 128, :]),
                      writes=[tok], sem=tok)
                convert_rows(wd_bf[:, k, :], sg, D, None, rd=[tok])
            bar()

            groups = [(g * G * 128, G, y_p[g * G * 128:(g + 1) * G * 128, :]) for g in range(2 * T // (G * 128))]
            groups.append((2 * T, 1, y_s))
            pcnt = {"b": 0}
            pre_b = {}

            def p2_pre(row0, j, sl):
                b = pcnt["b"] % 2
                pcnt["b"] += 1
                pre_b[j] = b
                S.dma("sp", _E.dma_start(out=x2t[:, sl, :], in_=x1s[row0 + j * 128:row0 + (j + 1) * 128, :]),
                      writes=["x2t%d" % sl], sem="x2t%d" % sl)
                S.op("pool", _E.memset(ss2[:, j:j + 1], 0.0), writes=["ss2_%d" % j])
                S.op("act", _E.activation(out=junk2, in_=x2t[:, sl, :], func=AF.Square, accum_out=ss2[:, j:j + 1]),
                     reads=["x2t%d" % sl], writes=["junk2", "ss2_%d" % j])
                S.op("dve", _E.tensor_scalar(out=rs2[:, j:j + 1], in0=ss2[:, j:j + 1], scalar1=1.0 / D,
                                             scalar2=1e-6, op0=ALU.mult, op1=ALU.add),
                     reads=["ss2_%d" % j], writes=["rs2_%d" % j])
                S.op("pool", _E.tensor_tensor(out=rs2[:, j:j + 1], in0=rs2[:, j:j + 1], in1=mhalf[:, 0:1], op=ALU.pow),
                     reads=["rs2_%d" % j], writes=["rs2_%d" % j])
                S.op("dve", _E.tensor_scalar(out=xsb2[b], in0=x2t[:, sl, :], scalar1=rs2[:, j:j + 1], scalar2=None,
                                             op0=ALU.mult),
                     reads=["x2t%d" % sl, "rs2_%d" % j], writes=["xsb2_%d" % b])

            def p2_tp(j):
                b = pre_b[j]
                for k in range(8):
                    S.op("pe", _E.transpose(out=tpb[:, k * 128:(k + 1) * 128], in_=xsb2[b][:, k * 128:(k + 1) * 128],
                                            identity=identb), reads=["xsb2_%d" % b], writes=[Bt(0)])
                S.op("act", _E.copy(out=h2T[:, :, j * 128:(j + 1) * 128], in_=tp3), writes=[Bt(0), "h2T%d" % j])

            def p2_chunks(NT, ntl):
                h2toks = ["h2T%d" % j for j in range(ntl)]
                for c in range(NFC):
                    gb, ub = (1, 2) if c % 2 == 0 else (3, 4)
                    sb_ = c % 2
                    for k in range(8):
                        S.op("pe", _E.matmul(B(gb)[:, 0:NT], lhsT=wg_bf[:, k, c * 128:(c + 1) * 128],
                                             rhs=h2T[:, k, 0:NT], start=(k == 0), stop=(k == 7)),
                             reads=h2toks, writes=[Bt(gb)])
                    for k in range(8):
                        S.op("pe", _E.matmul(B(ub)[:, 0:NT], lhsT=wu_bf[:, k, c * 128:(c + 1) * 128],
                                             rhs=h2T[:, k, 0:NT], start=(k == 0), stop=(k == 7)),
                             reads=h2toks, writes=[Bt(ub)])
                    S.op("act", _E.activation(out=sgt[sb_][:, 0:NT], in_=B(gb)[:, 0:NT], func=AF.Silu),
                         writes=[Bt(gb), "sgt%d" % sb_])
                    S.op("dve", _E.tensor_tensor(out=actT[:, c, 0:NT], in0=B(ub)[:, 0:NT], in1=sgt[sb_][:, 0:NT],
                                                 op=ALU.mult), reads=["sgt%d" % sb_], writes=[Bt(ub), "actT"])

            def p2_down(j):
                pb = (5, 6)
                for half in range(2):
                    for c in range(NFC):
                        S.op("pe", _E.matmul(B(pb[half]), lhsT=actT[:, c, j * 128:(j + 1) * 128],
                                             rhs=wd_bf[:, c, half * 512:(half + 1) * 512],
                                             start=(c == 0), stop=(c == NFC - 1)), reads=["actT"], writes=[Bt(pb[half])])

            def p2_fin(j, ydst, sl):
                S.op("dve", _E.tensor_tensor(out=x2t[:, sl, :], in0=x2t[:, sl, :], in1=ps[:, 2560:3584], op=ALU.add),
                     writes=[Bt(5), Bt(6), "x2t%d" % sl])
                S.op("pool", _E.memset(ss3[:, j:j + 1], 0.0), writes=["ss3_%d" % j])
                S.op("act", _E.activation(out=junk2, in_=x2t[:, sl, :], func=AF.Square, accum_out=ss3[:, j:j + 1]),
                     reads=["x2t%d" % sl], writes=["junk2", "ss3_%d" % j])
                S.op("dve", _E.tensor_scalar(out=rs3[:, j:j + 1], in0=ss3[:, j:j + 1], scalar1=1.0 / D,
                                             scalar2=1e-6, op0=ALU.mult, op1=ALU.add),
                     reads=["ss3_%d" % j], writes=["rs3_%d" % j])
                S.op("pool", _E.tensor_tensor(out=rs3[:, j:j + 1], in0=rs3[:, j:j + 1], in1=mhalf[:, 0:1], op=ALU.pow),
                     reads=["rs3_%d" % j], writes=["rs3_%d" % j])
                S.op("dve", _E.scalar_tensor_tensor(out=x2t[:, sl, :], in0=x2t[:, sl, :], scalar=rs3[:, j:j + 1],
                                                    in1=gfin, op0=ALU.mult, op1=ALU.mult),
                     reads=["rs3_%d" % j, "gfin"], writes=["x2t%d" % sl])
                S.dma("sp", _E.dma_start(out=ydst[j * 128:(j + 1) * 128, :], in_=x2t[:, sl, :]),
                      reads=["x2t%d" % sl], sem="yout%d" % sl)

            def slot(gi_, j_):
                return (gi_ * G + j_) % NSL

            for j in range(groups[0][1]):
                p2_pre(groups[0][0], j, slot(0, j))
                p2_tp(j)
            for gi, (row0, ntl, ydst) in enumerate(groups):
                nxt = groups[gi + 1] if gi + 1 < len(groups) else None
                nn = nxt[1] if nxt else 0
                p2_chunks(ntl * 128, ntl)
                if nn > 0:
                    p2_pre(nxt[0], 0, slot(gi + 1, 0))
                tp_done = 0
                for j in range(ntl):
                    p2_down(j)
                    if j >= 1 and j - 1 < nn:
                        p2_tp(j - 1)
                        tp_done = j
                    p2_fin(j, ydst, slot(gi, j))
                    if j + 1 < nn:
                        p2_pre(nxt[0], j + 1, slot(gi + 1, j + 1))
                for jj in range(tp_done, nn):
                    if jj > ntl:
                        p2_pre(nxt[0], jj, slot(gi + 1, jj))
                    p2_tp(jj)

        except _Stop as ex:
            print('STOPPED at', ex)
        S.emit(final_wait_sems=list(S.dma_cum.keys()))
        print("ops per engine:", {e: len(v) for e, v in S.by_eng.items()})
    return nc


_PROG = {}


def kernel(x_prompt, x_sample, cache_fox_k, cache_fox_v, cache_fox_logf, state_ret,
           w_in, b_forget, ret_gn_gain, w_out, norm_mix_gain, norm_ffn_gain,
           w_gate, w_up, w_down, norm_final_gain):
    f = lambda a: np.ascontiguousarray(np.asarray(a, dtype=np.float32))
    x_prompt, x_sample = f(x_prompt), f(x_sample)
    ckf, cvf, clff, srf = f(cache_fox_k), f(cache_fox_v), f(cache_fox_logf), f(state_ret)
    cf, cosp, sinp, g128, g64 = _host_consts()
    if "nc" not in _PROG:
        _PROG["nc"] = build_program(g128, g64)
    nc = _PROG["nc"]
    vecs = np.zeros((128, 28), np.float32)
    vecs[:, 0:8] = f(norm_mix_gain)[0].reshape(8, 128).T
    vecs[:, 8:16] = f(norm_ffn_gain)[0].reshape(8, 128).T
    vecs[:, 16:20] = f(ret_gn_gain)[0].reshape(4, 128).T
    vecs[:, 20:28] = np.broadcast_to(f(b_forget)[0][None, :], (128, 8))
    gfin_b = np.ascontiguousarray(np.broadcast_to(f(norm_final_gain)[None, :], (128, D)))
    shared = {
        "w_in": f(w_in)[0], "w_out": f(w_out)[0], "w_gate": f(w_gate)[0], "w_up": f(w_up)[0],
        "w_down": f(w_down)[0], "cf32": cf, "cosp": cosp, "sinp": sinp, "vecs": vecs, "gfin_b": gfin_b,
    }
    in_maps = []
    for c in range(NCORES):
        m = dict(shared)
        m["xp"] = x_prompt[2 * c:2 * c + 2]
        m["xs"] = x_sample[2 * c:2 * c + 2]
        m["ck"] = ckf[0, 2 * c:2 * c + 2].reshape(2, PAST, 512)
        m["cv"] = cvf[0, 2 * c:2 * c + 2].reshape(2, PAST, 512)
        m["clf"] = clff[0, 2 * c:2 * c + 2]
        m["sret"] = srf[0, 2 * c:2 * c + 2]
        in_maps.append(m)
    res = run_bass_kernel_spmd(nc, in_maps, core_ids=list(range(NCORES)))
    R = res.results
    cat = lambda name: np.concatenate([np.asarray(r[name]) for r in R], axis=0)
    y_prompt = cat("y_p").reshape(16, T, D)
    y_sample = cat("y_s").reshape(16, NS, D)
    sr_p = cat("sr_p").reshape(1, 16, 4, 128, 128)
    kf_p = cat("kf_p").reshape(1, 16, T, 8, 64)
    vf_p = cat("vf_p").reshape(1, 16, T, 8, 64)
    lf_p = cat("lf_p").reshape(1, 16, T, 8)
    sr_s = cat("sr_s").reshape(1, 16, 4, 128, 128)
    kf_s = cat("kf_s").reshape(1, 16, NS, 8, 64)
    vf_s = cat("vf_s").reshape(1, 16, NS, 8, 64)
    lf_s = cat("lf_s").reshape(1, 16, NS, 8)
    return (y_prompt.astype(np.float32), y_sample.astype(np.float32), sr_p, kf_p, vf_p, lf_p,
            sr_s, kf_s, vf_s, lf_s)
```

```python
import numpy as np
from contextlib import ExitStack
import concourse.bass as bass
import concourse.mybir as mybir
from concourse.bass_utils import run_bass_kernel_spmd

F32 = mybir.dt.float32
BF16 = mybir.dt.bfloat16
AF = mybir.ActivationFunctionType
ALU = mybir.AluOpType

NCORES = 8
D = 1024
DIN = 3592
DFF = 2816
NFC = DFF // 128
T = 2048
NS = 64
PAST = 4096
NKT = PAST // 128
G = 4
ENGINES = ("pe", "act", "dve", "pool", "sp")


class Op:
    __slots__ = ("eng", "fn", "reads", "writes", "gidx", "eidx", "dma_sem", "dma_val",
                 "waits", "signal", "count", "clock")

    def __init__(self, eng, fn, reads, writes, dma_sem=None):
        self.eng = eng
        self.fn = fn
        self.reads = reads
        self.writes = writes
        self.dma_sem = dma_sem
        self.dma_val = 0
        self.waits = []
        self.signal = False
        self.count = 0
        self.clock = None


class Sched:
    def __init__(self, nc, same_engine_dist=10 ** 9):
        self.nc = nc
        self.ops = []
        self.by_eng = {e: [] for e in ENGINES}
        self.last_writer = {}
        self.readers = {}
        self.dma_cum = {}
        self.observed = {}
        self.same_engine_dist = same_engine_dist
        self.bar_ops = []
        self.bank_last = {}

    def op(self, eng, fn, reads=(), writes=()):
        o = Op(eng, fn, tuple(reads), tuple(writes))
        self._add(o)
        return o

    def dma(self, eng, fn, reads=(), writes=(), sem=None):
        o = Op(eng, fn, tuple(reads), tuple(writes), dma_sem=sem)
        self.dma_cum.setdefault(sem, 0)
        self._add(o)
        self.dma_cum[sem] += 16
        o.dma_val = self.dma_cum[sem]
        return o

    def _dep_on(self, o, d, obs):
        if d.dma_sem is not None:
            key = ("dma", d.dma_sem)
            val = self.dma_cum[d.dma_sem]
            if obs.get(key, 0) >= val:
                return
            obs[key] = val
            o.waits.append(("dma", d.dma_sem, val))
        else:
            key = ("eng", d.eng)
            if obs.get(key, -1) >= d.eidx:
                return
            o.waits.append(("eng", d.eng, d))
            d.signal = True
            for k, v in d.clock.items():
                if obs.get(k, -1) < v:
                    obs[k] = v

    def _add(self, o, force_deps=()):
        o.gidx = len(self.ops)
        o.eidx = len(self.by_eng[o.eng])
        deps = set(self.bar_ops)
        for t in o.reads:
            w = self.last_writer.get(t)
            if w is not None:
                deps.add(w)
        bank_toks = []
        for t in o.writes:
            if isinstance(t, str) and len(t) == 2 and t[0] == "B":
                bank_toks.append(t)
                for eng2, last in self.bank_last.setdefault(t, {}).items():
                    if eng2 != o.eng:
                        deps.add(last)
                continue
            w = self.last_writer.get(t)
            if w is not None:
                deps.add(w)
            for r in self.readers.get(t, ()):
                deps.add(r)
        deps.discard(o)
        obs = self.observed.setdefault(o.eng, {})
        for d in sorted(deps, key=lambda x: x.gidx):
            if d.dma_sem is None and d.eng == o.eng:
                if o.eng == "pe":
                    continue
                if o.dma_sem is None and o.eidx - d.eidx > self.same_engine_dist:
                    continue
            self._dep_on(o, d, obs)
        for d in force_deps:
            self._dep_on(o, d, obs)
        if o.dma_sem is None:
            clock = dict(obs)
            clock[("eng", o.eng)] = o.eidx
            o.clock = clock
        for t in o.reads:
            self.readers.setdefault(t, []).append(o)
        for t in o.writes:
            if t in bank_toks:
                self.bank_last[t][o.eng] = o
                continue
            self.last_writer[t] = o
            self.readers[t] = []
        self.ops.append(o)
        self.by_eng[o.eng].append(o)

    def barrier(self, fns):
        new = []
        for e, fn in fns.items():
            o = Op(e, fn, (), ())
            force = []
            if self.by_eng["pe"]:
                force.append(self.by_eng["pe"][-1])
            for other in fns:
                if other != e and self.by_eng[other]:
                    force.append(self.by_eng[other][-1])
            obs = self.observed.setdefault(e, {})
            for key, val in self.dma_cum.items():
                if obs.get(("dma", key), 0) < val:
                    obs[("dma", key)] = val
                    o.waits.append(("dma", key, val))
            self._add(o, force_deps=force)
            new.append(o)
        self.bar_ops = new
        self.last_writer = {}
        self.readers = {}
        self.bank_last = {}

    def emit(self, final_wait_sems=()):
        nc = self.nc
        with ExitStack() as st:
            esem = {e: st.enter_context(nc.semaphore("s_" + e)) for e in ENGINES}
            dsem = {k: st.enter_context(nc.semaphore("d_%s" % (str(k),))) for k in self.dma_cum}
            for e in ENGINES:
                c = 0
                for o in self.by_eng[e]:
                    if o.dma_sem is None and o.signal:
                        c += 1
                        o.count = c
            block = st.enter_context(nc.Block())

            def run(e, eng):
                for o in self.by_eng[e]:
                    for w in o.waits:
                        if w[0] == "dma":
                            eng.wait_ge(dsem[w[1]], w[2])
                        else:
                            eng.wait_ge(esem[w[1]], w[2].count)
                    inst = o.fn(eng)
                    if o.dma_sem is not None:
                        inst.then_inc(dsem[o.dma_sem], 16)
                    elif o.signal:
                        inst.then_inc(esem[e], 1)
                if e == "sp":
                    for k in final_wait_sems:
                        eng.wait_ge(dsem[k], self.dma_cum[k])

            @block.tensor
            def _(eng):
                run("pe", eng)

            @block.scalar
            def _(eng):
                run("act", eng)

            @block.vector
            def _(eng):
                run("dve", eng)

            @block.gpsimd
            def _(eng):
                run("pool", eng)

            @block.sync
            def _(eng):
                run("sp", eng)


class _Rec:
    def __getattr__(self, name):
        def mk(*a, **kw):
            return lambda eng: getattr(eng, name)(*a, **kw)
        return mk


_E = _Rec()


class _Stop(Exception):
    pass


def stop_at(name):
    import os
    if os.environ.get("MK_STOP", "") == name:
        raise _Stop(name)


class Arena:
    def __init__(self, t, nwords):
        self.t = t
        self.n = nwords
        self.off = 0

    def f32(self, ncols):
        assert self.off + ncols <= self.n, ("arena overflow", self.off, ncols, self.n)
        ap = self.t[:, self.off:self.off + ncols]
        self.off += ncols
        return ap

    def bf16(self, ncols):
        w = (ncols + 1) // 2
        ap = self.f32(w).bitcast(BF16)
        return ap[:, 0:ncols]


CF_LAYOUT = {}


def _cf_layout():
    off = 0
    for name, n in (("ident", 128), ("U", 128), ("ones", 128), ("Lst", 128), ("negm", 128),
                    ("Mp", 512), ("qd", 512), ("kdp", 4), ("kds", 4), ("coss", 64), ("sins", 64),
                    ("mhalf", 4), ("mone", 4)):
        CF_LAYOUT[name] = (off, n)
        off += n
    return off


CF_N = _cf_layout()


def _host_consts():
    lg = np.log1p(-np.exp2(-5.0 - np.arange(4, dtype=np.float32))).astype(np.float64)
    sc = 128.0 ** -0.5
    i = np.arange(128)
    cf = np.zeros((128, CF_N), np.float32)

    def put(name, arr):
        o, n = CF_LAYOUT[name]
        cf[:arr.shape[0], o:o + n] = arr.reshape(arr.shape[0], n)

    put("ident", np.eye(128, dtype=np.float32))
    put("U", (i[:, None] <= i[None, :]).astype(np.float32))
    put("ones", np.ones((128, 128), np.float32))
    put("Lst", (i[:, None] > i[None, :]).astype(np.float32))
    put("negm", np.where(i[None, :] >= i[:, None], 0.0, -30000.0).astype(np.float32))
    diff = (i[None, :] - i[:, None]).astype(np.float64)
    same = (i[:, None] // 64) == (i[None, :] // 64)
    fwd = (i[:, None] < 64) & (i[None, :] >= 64)
    Mp = np.zeros((128, 4, 128), np.float64)
    qd = np.zeros((128, 4, 128), np.float64)
    for h in range(4):
        Mp[:, h, :] = np.where(same, np.exp(lg[h] * np.abs(diff)),
                               np.where(fwd, np.exp(lg[h] * diff), 0.0)) * sc
        qd[:, h, :] = (sc * np.exp(lg[h] * (i + 1.0)))[None, :]
    put("Mp", Mp.astype(np.float32))
    put("qd", qd.astype(np.float32))
    put("kdp", np.exp(lg[None, :] * (127.0 - i[:, None])).astype(np.float32))
    kds = np.exp(lg[None, :] * (63.0 - i[:64, None])).astype(np.float32)
    put("kds", kds)
    inv = (10000.0 ** (-np.arange(64, dtype=np.float32) / 64.0)).astype(np.float32)
    pos = np.arange(T, dtype=np.float32)
    ang = (pos[:, None] * inv[None, :]).astype(np.float32)
    cosp = np.cos(ang).astype(np.float32).reshape(T // 128, 128, 64)
    sinp = np.sin(ang).astype(np.float32).reshape(T // 128, 128, 64)
    poss = np.arange(NS, dtype=np.float32) + float(PAST)
    angs = (poss[:, None] * inv[None, :]).astype(np.float32)
    put("coss", np.cos(angs).astype(np.float32))
    put("sins", np.sin(angs).astype(np.float32))
    put("mhalf", np.full((128, 4), -0.5, np.float32))
    put("mone", np.full((128, 4), -1.0, np.float32))
    g128 = [float(np.exp(lg[h] * 128.0)) for h in range(4)]
    g64 = [float(np.exp(lg[h] * 64.0)) for h in range(4)]
    return cf, cosp, sinp, g128, g64


def build_program(g128, g64):
    nc = bass.Bass("TRN2", target_bir_lowering=False)

    def din(name, shape):
        return nc.dram_tensor(name, list(shape), F32, kind="ExternalInput").ap()

    def dout(name, shape):
        return nc.dram_tensor(name, list(shape), F32, kind="ExternalOutput").ap()

    xp = din("xp", (2, T, D))
    xs_d = din("xs", (2, NS, D))
    ck = din("ck", (2, PAST, 512))
    cv = din("cv", (2, PAST, 512))
    clf = din("clf", (2, PAST, 8))
    sret = din("sret", (2, 4, 128, 128))
    w_in = din("w_in", (D, DIN))
    w_out = din("w_out", (D, D))
    w_gate = din("w_gate", (D, DFF))
    w_up = din("w_up", (D, DFF))
    w_down = din("w_down", (DFF, D))
    cf_d = din("cf32", (128, CF_N))
    cosp_d = din("cosp", (T // 128, 128, 64))
    sinp_d = din("sinp", (T // 128, 128, 64))
    vec_d = din("vecs", (128, 8 + 8 + 4 + 8))
    gfin_d = din("gfin_b", (128, D))

    y_p = dout("y_p", (2 * T, D))
    y_s = dout("y_s", (2 * NS, D))
    sr_p = dout("sr_p", (2, 4, 128, 128))
    kf_p = dout("kf_p", (2, T, 512))
    vf_p = dout("vf_p", (2, T, 512))
    lf_p = dout("lf_p", (2, T, 8))
    sr_s = dout("sr_s", (2, 4, 128, 128))
    kf_s = dout("kf_s", (2, NS, 512))
    vf_s = dout("vf_s", (2, NS, 512))
    lf_s = dout("lf_s", (2, NS, 8))
    NTOK = 2 * T + 2 * NS
    x1s = nc.dram_tensor("x1s", [NTOK, D], F32, kind="Internal").ap()

    with ExitStack() as st:
        AW = 52600
        arena_t = st.enter_context(nc.sbuf_tensor("arena", [128, AW], F32))
        ps = st.enter_context(nc.psum_tensor("ps", [128, 4096], F32))
        S = Sched(nc)
        A = Arena(arena_t, AW)

        def B(i):
            return ps[:, i * 512:(i + 1) * 512]

        def Bt(i):
            return "B%d" % i

        try:
            cf = A.f32(CF_N)
            vecs = A.f32(28)
            identb = A.bf16(128)
            negmb = A.bf16(128)
            dummy = A.f32(4)

            def C(name, rows=128):
                o, n = CF_LAYOUT[name]
                return cf[0:rows, o:o + n]

            gmixc = vecs[:, 0:8]
            gffnc = vecs[:, 8:16]
            gnc = vecs[:, 16:20]
            bfb = vecs[:, 20:28]

            S.dma("sp", _E.dma_start(out=cf, in_=cf_d), writes=["cf"], sem="cf")
            S.dma("sp", _E.dma_start(out=vecs, in_=vec_d), writes=["vecs"], sem="cf")
            S.op("pool", _E.tensor_copy(out=identb, in_=C("ident")), reads=["cf"], writes=["identb"])
            S.op("pool", _E.tensor_copy(out=negmb, in_=C("negm")), reads=["cf"], writes=["negmb"])

            def bar():
                S.barrier({
                    "act": _E.copy(out=dummy[:, 0:1], in_=dummy[:, 1:2]),
                    "dve": _E.memset(dummy[:, 2:3], 0.0),
                    "pool": _E.memset(dummy[:, 3:4], 0.0),
                })

            S.op("pool", _E.memset(dummy, 0.0), writes=["dummy"])

            mark_persist = A.off
            stop_at("pro0")

            w_in_bf = A.bf16(8 * DIN).rearrange("p (k c) -> p k c", k=8)
            w_out_bf = A.bf16(8 * D).rearrange("p (k c) -> p k c", k=8)
            mark_w1 = A.off
            NSTG = 4
            stage = [A.f32(DIN) for _ in range(NSTG)]
            A.off = mark_w1
            xt = A.f32(G * D).rearrange("p (j c) -> p j c", j=G)
            xsb = [A.bf16(D), A.bf16(D)]
            hT = A.bf16(8 * G * 128).rearrange("p (k t) -> p k t", k=8)
            QB = A.bf16(4 * G * 2 * 128).rearrange("p (c j e t) -> p c j e t", c=4, j=G, e=2)
            kfT = A.bf16(4 * T).rearrange("p (c t) -> p c t", c=4)
            vaug_off = A.off
            Vaug = A.bf16(16 * 8 * 65).rearrange("p (t h e) -> p t h e", t=16, h=8)
            qkrot = [A.bf16(1024), A.bf16(1024)]
            rt = [A.f32(512) for _ in range(4)]
            vtok = [A.bf16(512), A.bf16(512)]
            gsb1 = A.f32(512)
            egs1 = A.f32(512)
            gsb = [gsb1, gsb1]
            egs = [egs1, egs1]
            onb = [rt[2], rt[3]]
            kfo = [A.f32(512), A.f32(512)]
            vfo = [A.f32(512), A.f32(512)]
            qkT = [A.bf16(1024).rearrange("p (h t) -> p h t", h=8) for _ in range(2)]
            qdT = [A.bf16(512).rearrange("p (h t) -> p h t", h=4) for _ in range(2)]
            kdtok = [A.bf16(512), A.bf16(512)]
            sTm = [A.bf16(512).rearrange("p (h t) -> p h t", h=4) for _ in range(2)]
            Sst = A.f32(512).rearrange("p (h e) -> p h e", h=4)
            Sbf = A.bf16(512).rearrange("p (h e) -> p h e", h=4)
            NPT = 12
            PT = [A.bf16(128) for _ in range(NPT)]
            mixb = [A.bf16(1024), A.bf16(1024)]
            mixT = [A.bf16(1024).rearrange("p (k t) -> p k t", k=8) for _ in range(2)]
            ctab = A.f32(128).rearrange("p (t h) -> p t h", h=8)
            Btab = [A.f32(128).rearrange("p (t h) -> p t h", h=8) for _ in range(2)]
            Rcar = A.f32(8)
            cst = [A.f32(128), A.f32(128)]
            sm = A.f32(128)
            clfs = A.f32(256).rearrange("p (t h) -> p t h", h=8)
            atab = A.f32(256).rearrange("p (t h) -> p t h", h=8)
            tots = A.f32(256).rearrange("p (t h) -> p t h", h=8)
            sufs = A.f32(256).rearrange("p (t h) -> p t h", h=8)
            negbq = A.f32(8)
            _save = A.off
            A.off = vaug_off + 8 * 65
            kst = [A.f32(512), A.f32(512)]
            vst = [A.f32(512), A.f32(512)]
            kTt = [A.bf16(512).rearrange("p (c t) -> p c t", c=4) for _ in range(2)]
            Vt = [A.bf16(8 * 65).rearrange("p (h e) -> p h e", h=8) for _ in range(2)]
            PTs = [A.bf16(512).rearrange("p (h q) -> p h q", h=8) for _ in range(2)]
            assert A.off <= vaug_off + 8 * 65 * 8, (A.off, vaug_off)
            A.off = _save
            print("phase1 arena words used:", A.off, "of", AW)

            ss = sm[:, 0:G]
            rs = sm[:, 8:8 + G]
            st6 = sm[:, 16:40].rearrange("p (h s) -> p h s", h=4)
            mv = sm[:, 40:48].rearrange("p (h s) -> p h s", h=4)
            rstdg = sm[:, 48:52]
            rcp = sm[:, 56:64]
            xl = sm[:, 64:72]
            la = sm[:, 72:80]
            le = sm[:, 80:88]
            ll = sm[:, 88:96]
            lmn = sm[:, 96:104]
            lft = [sm[:, 104:112], sm[:, 112:120]]

            mhalf = C("mhalf")

            def convert_rows(dst, src_sb, ncols, scale_col, engines=("act", "dve", "pool"), rd=()):
                shares = {"act": 0.44, "dve": 0.44, "pool": 0.12}
                bounds = [0]
                acc_ = 0.0
                for eng in engines:
                    acc_ += shares[eng]
                    bounds.append(min(ncols, int(round(ncols * acc_ / 8.0)) * 8))
                bounds[-1] = ncols
                for i, eng in enumerate(engines):
                    c0, c1 = bounds[i], bounds[i + 1]
                    if c0 >= c1:
                        continue
                    if scale_col is None:
                        if eng == "act":
                            S.op("act", _E.copy(out=dst[:, c0:c1], in_=src_sb[:, c0:c1]), reads=rd)
                        else:
                            S.op(eng, _E.tensor_copy(out=dst[:, c0:c1], in_=src_sb[:, c0:c1]), reads=rd)
                    else:
                        if eng == "act":
                            S.op("act", _E.activation(out=dst[:, c0:c1], in_=src_sb[:, c0:c1],
                                                                           func=AF.Identity, scale=scale_col), reads=rd)
                        else:
                            S.op(eng, _E.tensor_scalar(out=dst[:, c0:c1], in0=src_sb[:, c0:c1],
                                                                            scalar1=scale_col, scalar2=None,
                                                                            op0=ALU.mult), reads=rd)

            for k in range(8):
                sg = stage[k % NSTG]
                tok = "stage%d" % (k % NSTG)
                S.dma("sp", _E.dma_start(out=sg[:, 0:DIN], in_=w_in[k * 128:(k + 1) * 128, :]),
                      writes=[tok], sem=tok)
                convert_rows(w_in_bf[:, k, :], sg, DIN, gmixc[:, k:k + 1], rd=[tok, "vecs"])
            for k in range(8):
                sg = stage[k % NSTG]
                tok = "stage%d" % (k % NSTG)
                S.dma("sp", _E.dma_start(out=sg[:, 0:D], in_=w_out[k * 128:(k + 1) * 128, :]),
                      writes=[tok], sem=tok)
                convert_rows(w_out_bf[:, k, :], sg, D, gnc[:, k:k + 1] if k < 4 else None, rd=[tok, "vecs"])
            bar()
            stop_at("pro")

            tpb = B(0).bitcast(BF16)
            tp3 = tpb.rearrange("p (k t) -> p k t", k=8)
            tpf3 = B(0).rearrange("p (k t) -> p k t", k=4)
            cnt = {"tile": 0, "pt": 0, "sl": 0, "ev": 0}

            def load_x(src_rows, n, j):
                S.dma("sp", _E.dma_start(out=xt[0:n, j, :], in_=src_rows), writes=["xt%d" % j], sem="xt%d" % j)

            def stage_A(n, j):
                b = cnt["tile"] % 2
                S.op("pool", _E.memset(ss[0:n, j:j + 1], 0.0), writes=["ss%d" % j])
                S.op("act", _E.activation(out=xsb[b][0:n, :], in_=xt[0:n, j, :], func=AF.Square,
                                                   accum_out=ss[0:n, j:j + 1]),
                     reads=["xt%d" % j], writes=["xsb%d" % b, "ss%d" % j])
                S.op("dve", _E.tensor_scalar(out=rs[0:n, j:j + 1], in0=ss[0:n, j:j + 1], scalar1=1.0 / D,
                                                      scalar2=1e-6, op0=ALU.mult, op1=ALU.add),
                     reads=["ss%d" % j], writes=["rs%d" % j])
                S.op("pool", _E.tensor_tensor(out=rs[0:n, j:j + 1], in0=rs[0:n, j:j + 1], in1=mhalf[0:n, 0:1],
                                                       op=ALU.pow), reads=["rs%d" % j], writes=["rs%d" % j])
                S.op("dve", _E.tensor_scalar(out=xsb[b][0:n, :], in0=xt[0:n, j, :], scalar1=rs[0:n, j:j + 1],
                                                      scalar2=None, op0=ALU.mult),
                     reads=["xt%d" % j, "rs%d" % j], writes=["xsb%d" % b])
                for k in range(8):
                    S.op("pe", _E.transpose(out=tpb[:, k * 128:k * 128 + n],
                                                          in_=xsb[b][0:n, k * 128:(k + 1) * 128],
                                                          identity=identb[0:n, 0:n]),
                         reads=["xsb%d" % b], writes=[Bt(0)])
                S.op("act", _E.copy(out=hT[:, :, j * 128:j * 128 + n], in_=tp3[:, :, 0:n]),
                     writes=[Bt(0), "hT%d" % j])
                cnt["tile"] += 1

            def evac_copy(dst, src, btok, wtoks):
                eng = "act" if cnt["ev"] % 2 == 0 else "dve"
                cnt["ev"] += 1
                if eng == "act":
                    S.op("act", _E.copy(out=dst, in_=src), writes=[btok] + wtoks)
                else:
                    S.op("dve", _E.tensor_copy(out=dst, in_=src), writes=[btok] + wtoks)

            def stage_B(NT, ntl, tok0):
                hts = ["hT%d" % j for j in range(ntl)]
                for c in range(8):
                    col0 = 2048 + c * 128 if c < 4 else 2560 + (c - 4) * 128
                    bk = 1 + (c % 4)
                    for k in range(8):
                        S.op("pe", _E.matmul(
                            B(bk)[:, 0:NT], lhsT=w_in_bf[:, k, col0:col0 + 128], rhs=hT[:, k, 0:NT],
                            start=(k == 0), stop=(k == 7)), reads=hts, writes=[Bt(bk)])
                    if c < 4:
                        tw = min(NT, 128)
                        for e_ in range(2):
                            r0_, r1_ = e_ * 64, (e_ + 1) * 64
                            evac_copy(QB[r0_:r1_, c, 0:ntl, e_, 0:tw],
                                      B(bk)[r0_:r1_, 0:NT].rearrange("p (j t) -> p j t", t=tw),
                                      Bt(bk), ["qfT%d_%d" % (c, e_)])
                    else:
                        evac_copy(kfT[:, c - 4, tok0:tok0 + NT], B(bk)[:, 0:NT], Bt(bk), ["kfT%d" % (c - 4)])

            def inproj_tok(n, j, bk, col0, ncols):
                for k in range(8):
                    S.op("pe", _E.matmul(B(bk)[0:n, 0:ncols], lhsT=hT[:, k, j * 128:j * 128 + n],
                                                       rhs=w_in_bf[:, k, col0:col0 + ncols],
                                                       start=(k == 0), stop=(k == 7)),
                         reads=["hT%d" % j], writes=[Bt(bk)])

            def stage_C(n, j, b, cos_ap, sin_ap, cs_tok, kt, kf_dst, vf_dst, lf_dst, vaug_dst):
                inproj_tok(n, j, 1, 0, 512)
                yield
                inproj_tok(n, j, 2, 512, 512)
                yield
                inproj_tok(n, j, 3, 1024, 512)
                S.op("act", _E.copy(out=vtok[b][0:n, :], in_=B(3)[0:n, :]), writes=[Bt(3), "vtok%d" % b])
                yield
                qk4 = ps[0:n, 512:1536].rearrange("p (h t f) -> p h t f", t=2, f=64)
                x1 = qk4[:, :, 0, :]
                x2 = qk4[:, :, 1, :]
                cosb = cos_ap.unsqueeze(1).to_broadcast([n, 8, 64])
                sinb = sin_ap.unsqueeze(1).to_broadcast([n, 8, 64])
                r3 = [r[0:n, :].rearrange("p (h f) -> p h f", f=64) for r in rt]
                qr4 = qkrot[b][0:n, :].rearrange("p (h t f) -> p h t f", t=2, f=64)
                bb = [Bt(1), Bt(2)]
                S.op("dve", _E.tensor_tensor(out=r3[0], in0=x1, in1=cosb, op=ALU.mult), reads=[cs_tok], writes=bb + ["rt0"])
                S.op("dve", _E.tensor_tensor(out=r3[1], in0=x2, in1=sinb, op=ALU.mult), reads=[cs_tok], writes=bb + ["rt1"])
                S.op("pool", _E.tensor_tensor(out=qr4[:, :, 0, :], in0=r3[0], in1=r3[1], op=ALU.subtract),
                     reads=["rt0", "rt1"], writes=["qkrot%da" % b])
                S.op("dve", _E.tensor_tensor(out=r3[2], in0=x1, in1=sinb, op=ALU.mult), reads=[cs_tok], writes=bb + ["rt2"])
                S.op("dve", _E.tensor_tensor(out=r3[3], in0=x2, in1=cosb, op=ALU.mult), reads=[cs_tok], writes=bb + ["rt3"])
                S.op("pool", _E.tensor_tensor(out=qr4[:, :, 1, :], in0=r3[2], in1=r3[3], op=ALU.add),
                     reads=["rt2", "rt3"], writes=["qkrot%db" % b])
                inproj_tok(n, j, 3, 1536, 512)
                S.op("act", _E.copy(out=gsb[b][0:n, :], in_=B(3)[0:n, :]), writes=[Bt(3), "gsb"])
                S.op("act", _E.activation(out=egs[b][0:n, :], in_=B(3)[0:n, :], func=AF.Exp, scale=-1.0),
                     writes=[Bt(3), "egs"])
                yield
                S.op("act", _E.activation(out=egs[b][0:n, :], in_=egs[b][0:n, :], func=AF.Ln, bias=1.0),
                     reads=["egs"], writes=["egs"])
                S.op("act", _E.activation(out=egs[b][0:n, :], in_=egs[b][0:n, :], func=AF.Exp, scale=-1.0),
                     reads=["egs"], writes=["egs"])
                yield
                inproj_tok(n, j, 1, 2560, 512)
                yield
                inproj_tok(n, j, 2, 3072, 512)
                yield
                inproj_tok(n, j, 3, 3584, 8)
                S.op("act", _E.copy(out=kfo[b][0:n, :], in_=B(1)[0:n, :]), writes=[Bt(1), "kfo%d" % b])
                S.op("act", _E.copy(out=vfo[b][0:n, :], in_=B(2)[0:n, :]), writes=[Bt(2), "vfo%d" % b])
                S.dma("sp", _E.dma_start(out=kf_dst, in_=kfo[b][0:n, :]), reads=["kfo%d" % b], sem="kfo%d" % b)
                S.dma("sp", _E.dma_start(out=vf_dst, in_=vfo[b][0:n, :]), reads=["vfo%d" % b], sem="vfo%d" % b)
                S.op("pool", _E.tensor_copy(out=vaug_dst, in_=vfo[b][0:n, :].rearrange("p (h f) -> p h f", f=64)),
                     reads=["vfo%d" % b], writes=["vaug"])
                S.op("dve", _E.tensor_tensor(out=xl[0:n, :], in0=B(3)[0:n, 0:8], in1=bfb[0:n, :], op=ALU.add),
                     writes=[Bt(3), "xl"])
                S.op("act", _E.activation(out=la[0:n, :], in_=xl[0:n, :], func=AF.Abs), reads=["xl"], writes=["la"])
                S.op("act", _E.activation(out=le[0:n, :], in_=la[0:n, :], func=AF.Exp, scale=-1.0),
                     reads=["la"], writes=["le"])
                S.op("act", _E.activation(out=ll[0:n, :], in_=le[0:n, :], func=AF.Ln, bias=1.0),
                     reads=["le"], writes=["ll"])
                S.op("dve", _E.tensor_scalar(out=lmn[0:n, :], in0=xl[0:n, :], scalar1=0.0, scalar2=None,
                                                      op0=ALU.min), reads=["xl"], writes=["lmn"])
                S.op("dve", _E.tensor_tensor(out=lft[b][0:n, :], in0=lmn[0:n, :], in1=ll[0:n, :],
                                                      op=ALU.subtract), reads=["lmn", "ll"], writes=["lft%d" % b])
                S.dma("sp", _E.dma_start(out=lf_dst, in_=lft[b][0:n, :]), reads=["lft%d" % b], sem="lft%d" % b)
                yield

            def stage_ret(n, b, kd_ap, gdec):
                yield
                for h in range(8):
                    S.op("pe", _E.transpose(out=tpb[:, h * 128:h * 128 + n],
                                                          in_=qkrot[b][0:n, h * 128:(h + 1) * 128],
                                                          identity=identb[0:n, 0:n]),
                         reads=["qkrot%da" % b, "qkrot%db" % b], writes=[Bt(0)])
                S.op("act", _E.copy(out=qkT[b][:, :, 0:n], in_=tp3[:, :, 0:n]), writes=[Bt(0), "qkT%d" % b])
                yield
                qd3 = C("qd").rearrange("p (h t) -> p h t", h=4)
                S.op("pool", _E.tensor_tensor(out=qdT[b][:, :, 0:n], in0=qkT[b][:, 0:4, 0:n], in1=qd3[:, :, 0:n],
                                                       op=ALU.mult), reads=["qkT%d" % b], writes=["qdT%d" % b])
                for h in range(4):
                    S.op("pool", _E.tensor_scalar(out=kdtok[b][0:n, h * 128:(h + 1) * 128],
                                                                in0=qkrot[b][0:n, 512 + h * 128:512 + (h + 1) * 128],
                                                                scalar1=kd_ap[0:n, h:h + 1], scalar2=None, op0=ALU.mult),
                         reads=["qkrot%da" % b, "qkrot%db" % b], writes=["kdtok%d_%d" % (b, h)])
                for h in range(4):
                    S.op("pe", _E.matmul(B(1)[0:n, h * 128:h * 128 + n], lhsT=qkT[b][:, 4 + h, 0:n],
                                                       rhs=qkT[b][:, h, 0:n], start=True, stop=True),
                         reads=["qkT%d" % b], writes=[Bt(1)])
                Mp3 = C("Mp").rearrange("p (h t) -> p h t", h=4)
                s43 = B(1).rearrange("p (h t) -> p h t", h=4)
                S.op("dve", _E.tensor_tensor(out=sTm[b][0:n, :, 0:n], in0=s43[0:n, :, 0:n], in1=Mp3[0:n, :, 0:n],
                                                      op=ALU.mult), writes=[Bt(1), "sTm%d" % b])
                yield
                for h in range(4):
                    S.op("pe", _E.matmul(B(3)[0:n, h * 128:(h + 1) * 128], lhsT=sTm[b][0:n, h, 0:n],
                                                       rhs=vtok[b][0:n, h * 128:(h + 1) * 128], start=True, stop=False),
                         reads=["sTm%d" % b, "vtok%d" % b], writes=[Bt(3)])
                    S.op("pe", _E.matmul(B(3)[0:n, h * 128:(h + 1) * 128], lhsT=qdT[b][:, h, 0:n],
                                                       rhs=Sbf[:, h, :], start=False, stop=True),
                         reads=["qdT%d" % b, "Sbf"], writes=[Bt(3)])
                for h in range(4):
                    S.op("pe", _E.matmul(B(2)[:, h * 128:(h + 1) * 128],
                                                       lhsT=kdtok[b][0:n, h * 128:(h + 1) * 128],
                                                       rhs=vtok[b][0:n, h * 128:(h + 1) * 128], start=True, stop=True),
                         reads=["kdtok%d_%d" % (b, h), "vtok%d" % b], writes=[Bt(2)])
                for h in range(4):
                    S.op("dve", _E.scalar_tensor_tensor(out=Sst[:, h, :], in0=Sst[:, h, :], scalar=gdec[h],
                                                                      in1=B(2)[:, h * 128:(h + 1) * 128],
                                                                      op0=ALU.mult, op1=ALU.add),
                         writes=[Bt(2), "Sst"])
                S.op("pool", _E.tensor_copy(out=Sbf, in_=Sst), reads=["Sst"], writes=["Sbf"])
                yield
                for h in range(4):
                    S.op("dve", _E.bn_stats(out=st6[0:n, h, :], in_=B(3)[0:n, h * 128:(h + 1) * 128]),
                         writes=[Bt(3), "st6_%d" % h])
                for h in range(4):
                    S.op("dve", _E.bn_aggr(out=mv[0:n, h, :], in_=st6[0:n, h, :]),
                         reads=["st6_%d" % h], writes=["mv%d" % h])
                S.op("dve", _E.tensor_scalar(out=rstdg[0:n, :], in0=mv[0:n, :, 1], scalar1=1e-5, scalar2=None,
                                                      op0=ALU.add), reads=["mv%d" % h for h in range(4)], writes=["rstdg"])
                S.op("pool", _E.tensor_tensor(out=rstdg[0:n, :], in0=rstdg[0:n, :], in1=mhalf[0:n, :], op=ALU.pow),
                     reads=["rstdg"], writes=["rstdg"])
                for h in range(4):
                    S.op("dve", _E.tensor_scalar(out=onb[b][0:n, h * 128:(h + 1) * 128],
                                                               in0=B(3)[0:n, h * 128:(h + 1) * 128],
                                                               scalar1=mv[0:n, h, 0:1], scalar2=rstdg[0:n, h:h + 1],
                                                               op0=ALU.subtract, op1=ALU.mult),
                         reads=["rstdg", "mv%d" % h], writes=[Bt(3), "rt%d" % (2 + b)])
                S.op("pool", _E.tensor_tensor(out=gsb[b][0:n, :], in0=gsb[b][0:n, :], in1=onb[b][0:n, :], op=ALU.mult),
                     reads=["gsb", "rt%d" % (2 + b)], writes=["gsb"])
                S.op("pool", _E.tensor_tensor(out=mixb[b][0:n, 0:512], in0=gsb[b][0:n, :], in1=egs[b][0:n, :],
                                                       op=ALU.mult), reads=["gsb", "egs"], writes=["mixr%d" % b])
                yield

            def cumsum_tile(n, b, first):
                S.op("pe", _E.matmul(B(4)[0:n, 0:8], lhsT=C("U")[0:n, 0:n], rhs=lft[b][0:n, :], start=True, stop=True),
                     reads=["lft%d" % b, "cf"], writes=[Bt(4)])
                S.op("pe", _E.matmul(B(4)[:, 8:16], lhsT=C("ones")[0:n, :], rhs=lft[b][0:n, :], start=True, stop=True),
                     reads=["lft%d" % b, "cf"], writes=[Bt(4)])

            def fox_finish_head(n, b, h, acc_bank):
                S.op("dve", _E.reciprocal(out=rcp[0:n, h:h + 1], in_=B(acc_bank)[0:n, 64:65]),
                     writes=[Bt(acc_bank), "rcp%d" % h])
                S.op("dve", _E.tensor_scalar(out=mixb[b][0:n, 512 + h * 64:512 + (h + 1) * 64],
                                                      in0=B(acc_bank)[0:n, 0:64], scalar1=rcp[0:n, h:h + 1],
                                                      scalar2=None, op0=ALU.mult),
                     reads=["rcp%d" % h], writes=[Bt(acc_bank), "mixf%d_%d" % (b, h)])

            def stage_fox_prompt(j, b, t, filler=None):
                cumsum_tile(128, b, t == 0)
                S.op("dve", _E.tensor_tensor(out=ctab[:, t, :], in0=B(4)[:, 0:8], in1=Rcar, op=ALU.add),
                     reads=["Rcar"], writes=[Bt(4), "ctab"])
                S.op("dve", _E.tensor_tensor(out=Rcar, in0=B(4)[:, 8:16], in1=Rcar, op=ALU.add),
                     writes=[Bt(4), "Rcar"])
                bt_ = Btab[t % 2]
                S.op("dve", _E.tensor_tensor(out=bt_[:, 0:t + 1, :],
                                                      in0=Rcar.unsqueeze(1).to_broadcast([128, t + 1, 8]),
                                                      in1=ctab[:, 0:t + 1, :], op=ALU.subtract),
                     reads=["Rcar", "ctab"], writes=["Btab%d" % (t % 2)])
                macros = []
                for p_ in range(4):
                    kts = list(range(t + 1))
                    for i0_ in range(0, len(kts), 2):
                        macros.append((p_, kts[i0_:i0_ + 2]))
                slots = (5, 6)
                LAGM = 2
                pend = []

                def emit_pv(mi):
                    p_, kts_ = macros[mi]
                    for q_, kt in enumerate(kts_):
                        for e_ in range(2):
                            h = 2 * p_ + e_
                            pt = pend[mi][q_ * 2 + e_]
                            accb = 7 if e_ == 0 else 4
                            S.op("pe", _E.matmul(B(accb)[:, 0:65], lhsT=PT[pt], rhs=Vaug[:, kt, h, :],
                                                 start=(kt == 0), stop=(kt == t)),
                                 reads=["PT%d" % pt, "vaug"], writes=[Bt(accb)])
                            if kt == t:
                                fox_finish_head(128, b, h, accb)

                for mi, (p_, kts_) in enumerate(macros):
                    sl = slots[cnt["sl"] % 2]
                    cnt["sl"] += 1
                    pts = []
                    for q_, kt in enumerate(kts_):
                        S.op("pe", _E.matmul(
                            B(sl)[:, q_ * 256:(q_ + 1) * 256], lhsT=kfT[:, p_, kt * 128:(kt + 1) * 128],
                            rhs=QB[:, p_, j, :, :].rearrange("p e t -> p (e t)"), start=True, stop=(kt != t)),
                            reads=["kfT%d" % p_, "qfT%d_0" % p_, "qfT%d_1" % p_], writes=[Bt(sl)])
                        if kt == t:
                            for e_ in range(2):
                                c0_ = q_ * 256 + e_ * 128
                                S.op("pe", _E.matmul(B(sl)[:, c0_:c0_ + 128], lhsT=identb, rhs=negmb,
                                                     start=False, stop=(e_ == 1)),
                                     reads=["identb", "negmb"], writes=[Bt(sl)])
                    for q_, kt in enumerate(kts_):
                        for e_ in range(2):
                            h = 2 * p_ + e_
                            c0_ = q_ * 256 + e_ * 128
                            pt = cnt["pt"] % NPT
                            cnt["pt"] += 1
                            pts.append(pt)
                            S.op("act", _E.activation(
                                out=PT[pt], in_=B(sl)[:, c0_:c0_ + 128], func=AF.Exp, bias=bt_[:, kt, h:h + 1], scale=0.125),
                                reads=["Btab%d" % (t % 2)], writes=[Bt(sl), "PT%d" % pt])
                    pend.append(pts)
                    if mi >= LAGM:
                        emit_pv(mi - LAGM)
                    if filler is not None:
                        for _ in range(3 if t < 4 else (2 if t < 8 else 1)):
                            next(filler, None)
                for mi in range(max(0, len(macros) - LAGM), len(macros)):
                    emit_pv(mi)
                if filler is not None:
                    for _ in filler:
                        pass

            def stage_E(n, j, b, x1_rows):
                mixtoks = ["mixr%d" % b] + ["mixf%d_%d" % (b, h) for h in range(8)]
                for k in range(8):
                    S.op("pe", _E.transpose(out=tpb[:, k * 128:k * 128 + n],
                                                          in_=mixb[b][0:n, k * 128:(k + 1) * 128],
                                                          identity=identb[0:n, 0:n]),
                         reads=mixtoks, writes=[Bt(0)])
                S.op("act", _E.copy(out=mixT[b][:, :, 0:n], in_=tp3[:, :, 0:n]), writes=[Bt(0), "mixT%d" % b])
                yield
                for half in range(2):
                    if half == 1:
                        yield
                    for k in range(8):
                        S.op("pe", _E.matmul(B(1 + half)[0:n, :], lhsT=mixT[b][:, k, 0:n],
                                                                      rhs=w_out_bf[:, k, half * 512:(half + 1) * 512],
                                                                      start=(k == 0), stop=(k == 7)),
                             reads=["mixT%d" % b], writes=[Bt(1 + half)])
                S.op("dve", _E.tensor_tensor(out=xt[0:n, j, :], in0=xt[0:n, j, :], in1=ps[0:n, 512:1536], op=ALU.add),
                     writes=[Bt(1), Bt(2), "xt%d" % j])
                S.dma("sp", _E.dma_start(out=x1_rows, in_=xt[0:n, j, :]), reads=["xt%d" % j], sem="x1st%d" % j)
                yield

            def load_cs(t):
                c = cst[t % 2]
                S.dma("sp", _E.dma_start(out=c[:, 0:64], in_=cosp_d[t]), writes=["cs%d" % (t % 2)], sem="cs%d" % (t % 2))
                S.dma("sp", _E.dma_start(out=c[:, 64:128], in_=sinp_d[t]), writes=["cs%d" % (t % 2)], sem="cs%d" % (t % 2))

            S.op("pool", _E.memset(Vaug.rearrange("p t h e -> p (t h e)"), 1.0), writes=["vaug"])
            S.op("pool", _E.memset(QB.rearrange("p c j e t -> p (c j e t)"), 0.0),
                 writes=["qfT%d_%d" % (c_, e_) for c_ in range(4) for e_ in range(2)])
            for s in range(2):
                S.op("pool", _E.memset(Sst.rearrange("p h e -> p (h e)"), 0.0), writes=["Sst"])
                S.op("pool", _E.memset(Sbf.rearrange("p h e -> p (h e)"), 0.0), writes=["Sbf"])
                S.op("pool", _E.memset(Rcar, 0.0), writes=["Rcar"])
                for j in range(G):
                    load_x(xp[s, j * 128:(j + 1) * 128, :], 128, j)
                load_cs(0)
                for g in range(T // (128 * G)):
                    for j in range(G):
                        stage_A(128, j)
                    stage_B(G * 128, G, g * G * 128)
                    def mixer_front(t_, j_):
                        c_ = cst[t_ % 2]
                        r_ = t_ * 128
                        yield from stage_C(128, j_, t_ % 2, c_[:, 0:64], c_[:, 64:128], "cs%d" % (t_ % 2), t_,
                                           kf_p[s, r_:r_ + 128, :], vf_p[s, r_:r_ + 128, :], lf_p[s, r_:r_ + 128, :],
                                           Vaug[:, t_, :, 0:64])
                        yield from stage_ret(128, t_ % 2, C("kdp"), g128)

                    def mixer_tail(t_, j_):
                        r_ = t_ * 128
                        yield from stage_E(128, j_, t_ % 2, x1s[s * T + r_:s * T + r_ + 128, :])
                        if g + 1 < T // (128 * G):
                            load_x(xp[s, (t_ + G) * 128:(t_ + G + 1) * 128, :], 128, j_)
                        yield

                    def chain_(*gens):
                        for g_ in gens:
                            yield from g_

                    for j in range(G):
                        t = g * G + j
                        b = t % 2
                        if t + 1 < 16:
                            load_cs(t + 1)
                        if j == 0:
                            for _ in mixer_front(t, j):
                                pass
                        parts = []
                        if j > 0:
                            parts.append(mixer_tail(t - 1, j - 1))
                        if j + 1 < G:
                            parts.append(mixer_front(t + 1, j + 1))
                        stage_fox_prompt(j, b, t, chain_(*parts) if parts else None)
                        if j == G - 1:
                            for _ in mixer_tail(t, j):
                                pass
                        if s == 0 and t == 3:
                            stop_at("g0")
                S.dma("sp", _E.dma_start(out=sr_p[s].rearrange("h d e -> d h e"), in_=Sst),
                      reads=["Sst"], sem="srout")
                if s == 0:
                    stop_at("s0")

            stop_at("pp")
            bar()
            S.op("pool", _E.memset(Vaug[:, 0, :, 64:65], 1.0), writes=["vaug"])
            for s in range(2):
                S.dma("sp", _E.dma_start(out=Sst, in_=sret[s].rearrange("h d e -> d h e")),
                      writes=["Sst"], sem="sldS")
                S.op("pool", _E.tensor_copy(out=Sbf, in_=Sst), reads=["Sst"], writes=["Sbf"])
                S.dma("sp", _E.dma_start(out=clfs, in_=clf[s].rearrange("(t p) h -> p t h", p=128)),
                      writes=["clfs"], sem="sldC")
                load_x(xs_d[s], NS, 0)
                clf2 = clfs.rearrange("p t h -> p (t h)")
                S.op("pe", _E.matmul(B(3)[:, 0:256], lhsT=C("Lst"), rhs=clf2, start=True, stop=True),
                     reads=["clfs", "cf"], writes=[Bt(3)])
                S.op("pe", _E.matmul(B(3)[:, 256:512], lhsT=C("ones"), rhs=clf2, start=True, stop=True),
                     reads=["clfs", "cf"], writes=[Bt(3)])
                S.op("dve", _E.tensor_copy(out=tots.rearrange("p t h -> p (t h)"), in_=B(3)[:, 256:512]),
                     writes=[Bt(3), "tots"])
                S.op("dve", _E.memset(sufs[:, NKT - 1, :], 0.0), writes=["sufs"])
                for tt in range(NKT - 2, -1, -1):
                    S.op("dve", _E.tensor_tensor(out=sufs[:, tt, :], in0=sufs[:, tt + 1, :],
                                                                 in1=tots[:, tt + 1, :], op=ALU.add),
                         reads=["tots"], writes=["sufs"])
                S.op("dve", _E.tensor_tensor(out=atab.rearrange("p t h -> p (t h)"), in0=B(3)[:, 0:256],
                                                      in1=sufs.rearrange("p t h -> p (t h)"), op=ALU.add),
                     reads=["sufs"], writes=[Bt(3), "atab"])
                n = NS
                stage_A(n, 0)
                stage_B(n, 1, 0)
                b = s % 2
                for _ in stage_C(n, 0, b, C("coss", n), C("sins", n), "cf", 0,
                                 kf_s[s], vf_s[s], lf_s[s], Vaug[0:n, 0, :, 0:64]):
                    pass
                for _ in stage_ret(n, b, C("kds"), g64):
                    pass
                S.dma("sp", _E.dma_start(out=sr_s[s].rearrange("h d e -> d h e"), in_=Sst),
                      reads=["Sst"], sem="srout")
                cumsum_tile(n, b, True)
                S.op("dve", _E.tensor_scalar(out=negbq[0:n, :], in0=B(4)[0:n, 0:8], scalar1=-1.0, scalar2=None,
                                                      op0=ALU.mult), reads=["b4data"], writes=[Bt(4), "negbq"])
                S.op("dve", _E.memset(B(7)[0:n, 0:260], 0.0), writes=[Bt(7)])
                S.op("dve", _E.memset(B(4)[0:n, 0:260], 0.0), writes=[Bt(4), "b4data"])
                spv_pending = [None]
                for kt in range(NKT + 1):
                    kb = kt % 2
                    last = (kt == NKT)
                    rows = n if last else 128
                    if not last:
                        S.dma("sp", _E.dma_start(out=kst[kb], in_=ck[s, kt * 128:(kt + 1) * 128, :]),
                              writes=["kst%d" % kb], sem="kst%d" % kb)
                        S.dma("sp", _E.dma_start(out=vst[kb], in_=cv[s, kt * 128:(kt + 1) * 128, :]),
                              writes=["vst%d" % kb], sem="vst%d" % kb)
                        for c in range(4):
                            S.op("pe", _E.transpose(out=B(0)[:, c * 128:(c + 1) * 128],
                                                                          in_=kst[kb][:, c * 128:(c + 1) * 128],
                                                                          identity=C("ident")),
                                 reads=["kst%d" % kb, "cf"], writes=[Bt(0)])
                        S.op("dve", _E.tensor_copy(out=kTt[kb], in_=tpf3), writes=[Bt(0), "kTt%d" % kb])
                        S.op("pool", _E.tensor_copy(out=Vt[kb][:, :, 0:64],
                                                                    in_=vst[kb].rearrange("p (h f) -> p h f", f=64)),
                             reads=["vst%d" % kb], writes=["Vt%d" % kb])
                        if kt < 2:
                            S.op("pool", _E.memset(Vt[kb][:, :, 64:65], 1.0), writes=["Vt%d" % kb])
                        kT_src, kT_tok = kTt[kb], "kTt%d" % kb
                        v_src, v_tok = Vt[kb], "Vt%d" % kb
                        bias_of = (lambda h, kt=kt: atab[:, kt, h:h + 1])
                        bias_tok = "atab"
                    else:
                        kT_src, kT_tok = kfT[:, :, 0:n], "kfTnew"
                        v_src, v_tok = Vaug[:, 0, :, :], "vaug"
                        bias_of = (lambda h: negbq[0:n, h:h + 1])
                        bias_tok = "negbq"
                    bk0, bk1 = (5, 6) if kt % 2 == 0 else (2, 3)
                    for h in range(8):
                        p_, e_ = divmod(h, 2)
                        bk = bk0 if e_ == 0 else bk1
                        S.op("pe", _E.matmul(
                            B(bk)[0:rows, p_ * 64:(p_ + 1) * 64], lhsT=kT_src[e_ * 64:(e_ + 1) * 64, p_, 0:rows],
                            rhs=QB[e_ * 64:(e_ + 1) * 64, p_, 0, e_, 0:n], start=True, stop=(not last)),
                            reads=([kT_tok] if kT_tok != "kfTnew" else ["kfT%d" % p_]) + ["qfT%d_%d" % (p_, e_)], writes=[Bt(bk)])
                        if last:
                            S.op("pe", _E.matmul(B(bk)[0:n, p_ * 64:(p_ + 1) * 64],
                                                                        lhsT=identb[e_ * 64:e_ * 64 + n, e_ * 64:e_ * 64 + n],
                                                                        rhs=negmb[e_ * 64:e_ * 64 + n, e_ * 64:e_ * 64 + n],
                                                                        start=False, stop=True),
                                 reads=["identb", "negmb"], writes=[Bt(bk)])
                    for h in range(8):
                        p_, e_ = divmod(h, 2)
                        bk = bk0 if e_ == 0 else bk1
                        S.op("act", _E.activation(
                            out=PTs[kb][0:rows, h, :], in_=B(bk)[0:rows, p_ * 64:(p_ + 1) * 64], func=AF.Exp,
                            bias=bias_of(h)[0:rows, :], scale=0.125),
                            reads=[bias_tok], writes=[Bt(bk), "PTs%d_%d" % (kb, h)])
                    def emit_spv(kb=kb, rows=rows, v_src=v_src, v_tok=v_tok, last=last):
                        for h in range(8):
                            accb = 7 if h < 4 else 4
                            hh = h % 4
                            S.op("pe", _E.matmul(
                                B(accb)[0:n, hh * 65:(hh + 1) * 65], lhsT=PTs[kb][0:rows, h, :], rhs=v_src[0:rows, h, :],
                                start=False, stop=last, skip_group_check=True),
                                reads=["PTs%d_%d" % (kb, h), v_tok], writes=[Bt(accb)])

                    if spv_pending[0] is not None:
                        spv_pending[0]()
                    spv_pending[0] = emit_spv
                spv_pending[0]()
                spv_pending[0] = None
                for h in range(8):
                    accb = 7 if h < 4 else 4
                    hh = h % 4
                    S.op("dve", _E.reciprocal(out=rcp[0:n, h:h + 1],
                                                                             in_=B(accb)[0:n, hh * 65 + 64:hh * 65 + 65]),
                         writes=[Bt(accb), "rcp%d" % h])
                    S.op("dve", _E.tensor_scalar(
                        out=mixb[b][0:n, 512 + h * 64:512 + (h + 1) * 64], in0=B(accb)[0:n, hh * 65:hh * 65 + 64],
                        scalar1=rcp[0:n, h:h + 1], scalar2=None, op0=ALU.mult),
                        reads=["rcp%d" % h], writes=[Bt(accb), "mixf%d_%d" % (b, h)])
                for _ in stage_E(n, 0, b, x1s[2 * T + s * NS:2 * T + (s + 1) * NS, :]):
                    pass
                if s == 0:
                    stop_at("ss0")

            bar()

            stop_at("p1")
            A.off = mark_persist
            wg_bf = A.bf16(8 * DFF).rearrange("p (k c) -> p k c", k=8)
            wu_bf = A.bf16(8 * DFF).rearrange("p (k c) -> p k c", k=8)
            wd_bf = A.bf16(NFC * D).rearrange("p (k c) -> p k c", k=NFC)
            gfin = A.f32(D)
            mark_w2 = A.off
            stage2 = [A.f32(DFF) for _ in range(NSTG)]
            A.off = mark_w2
            NSL = G + 1
            x2t = A.f32(NSL * D).rearrange("p (j c) -> p j c", j=NSL)
            xsb2 = [A.bf16(D), A.bf16(D)]
            junk2 = A.bf16(D)
            h2T = A.bf16(8 * G * 128).rearrange("p (k t) -> p k t", k=8)
            actT = A.bf16(NFC * G * 128).rearrange("p (c t) -> p c t", c=NFC)
            sgt = [A.f32(G * 128), A.f32(G * 128)]
            sm2 = A.f32(32)
            print("phase2 arena words used:", A.off, "of", AW)
            ss2 = sm2[:, 0:G]
            rs2 = sm2[:, 8:8 + G]
            ss3 = sm2[:, 16:16 + G]
            rs3 = sm2[:, 24:24 + G]

            S.dma("sp", _E.dma_start(out=gfin, in_=gfin_d), writes=["gfin"], sem="gfin")
            kk_ = 0
            for (wsrc, wdst) in ((w_gate, wg_bf), (w_up, wu_bf)):
                for k in range(8):
                    sg = stage2[kk_ % NSTG]
                    tok = "stg2_%d" % (kk_ % NSTG)
                    kk_ += 1
                    S.dma("sp", _E.dma_start(out=sg[:, 0:DFF], in_=wsrc[k * 128:(k + 1) * 128, :]),
                          writes=[tok], sem=tok)
                    convert_rows(wdst[:, k, :], sg, DFF, gffnc[:, k:k + 1], rd=[tok])
            for k in range(NFC):
                sg = stage2[kk_ % NSTG]
                tok = "stg2_%d" % (kk_ % NSTG)
                kk_ += 1
                S.dma("sp", _E.dma_start(out=sg[:, 0:D], in_=w_down[k * 128:(k + 1) * 128, :]),
                      writes=[tok], sem=tok)
                convert_rows(wd_bf[:, k, :], sg, D, None, rd=[tok])
            bar()

            groups = [(g * G * 128, G, y_p[g * G * 128:(g + 1) * G * 128, :]) for g in range(2 * T // (G * 128))]
            groups.append((2 * T, 1, y_s))
            pcnt = {"b": 0}
            pre_b = {}

            def p2_pre(row0, j, sl):
                b = pcnt["b"] % 2
                pcnt["b"] += 1
                pre_b[j] = b
                S.dma("sp", _E.dma_start(out=x2t[:, sl, :], in_=x1s[row0 + j * 128:row0 + (j + 1) * 128, :]),
                      writes=["x2t%d" % sl], sem="x2t%d" % sl)
                S.op("pool", _E.memset(ss2[:, j:j + 1], 0.0), writes=["ss2_%d" % j])
                S.op("act", _E.activation(out=junk2, in_=x2t[:, sl, :], func=AF.Square, accum_out=ss2[:, j:j + 1]),
                     reads=["x2t%d" % sl], writes=["junk2", "ss2_%d" % j])
                S.op("dve", _E.tensor_scalar(out=rs2[:, j:j + 1], in0=ss2[:, j:j + 1], scalar1=1.0 / D,
                                             scalar2=1e-6, op0=ALU.mult, op1=ALU.add),
                     reads=["ss2_%d" % j], writes=["rs2_%d" % j])
                S.op("pool", _E.tensor_tensor(out=rs2[:, j:j + 1], in0=rs2[:, j:j + 1], in1=mhalf[:, 0:1], op=ALU.pow),
                     reads=["rs2_%d" % j], writes=["rs2_%d" % j])
                S.op("dve", _E.tensor_scalar(out=xsb2[b], in0=x2t[:, sl, :], scalar1=rs2[:, j:j + 1], scalar2=None,
                                             op0=ALU.mult),
                     reads=["x2t%d" % sl, "rs2_%d" % j], writes=["xsb2_%d" % b])

            def p2_tp(j):
                b = pre_b[j]
                for k in range(8):
                    S.op("pe", _E.transpose(out=tpb[:, k * 128:(k + 1) * 128], in_=xsb2[b][:, k * 128:(k + 1) * 128],
                                            identity=identb), reads=["xsb2_%d" % b], writes=[Bt(0)])
                S.op("act", _E.copy(out=h2T[:, :, j * 128:(j + 1) * 128], in_=tp3), writes=[Bt(0), "h2T%d" % j])

            def p2_chunks(NT, ntl):
                h2toks = ["h2T%d" % j for j in range(ntl)]
                for c in range(NFC):
                    gb, ub = (1, 2) if c % 2 == 0 else (3, 4)
                    sb_ = c % 2
                    for k in range(8):
                        S.op("pe", _E.matmul(B(gb)[:, 0:NT], lhsT=wg_bf[:, k, c * 128:(c + 1) * 128],
                                             rhs=h2T[:, k, 0:NT], start=(k == 0), stop=(k == 7)),
                             reads=h2toks, writes=[Bt(gb)])
                    for k in range(8):
                        S.op("pe", _E.matmul(B(ub)[:, 0:NT], lhsT=wu_bf[:, k, c * 128:(c + 1) * 128],
                                             rhs=h2T[:, k, 0:NT], start=(k == 0), stop=(k == 7)),
                             reads=h2toks, writes=[Bt(ub)])
                    S.op("act", _E.activation(out=sgt[sb_][:, 0:NT], in_=B(gb)[:, 0:NT], func=AF.Silu),
                         writes=[Bt(gb), "sgt%d" % sb_])
                    S.op("dve", _E.tensor_tensor(out=actT[:, c, 0:NT], in0=B(ub)[:, 0:NT], in1=sgt[sb_][:, 0:NT],
                                                 op=ALU.mult), reads=["sgt%d" % sb_], writes=[Bt(ub), "actT"])

            def p2_down(j):
                pb = (5, 6)
                for half in range(2):
                    for c in range(NFC):
                        S.op("pe", _E.matmul(B(pb[half]), lhsT=actT[:, c, j * 128:(j + 1) * 128],
                                             rhs=wd_bf[:, c, half * 512:(half + 1) * 512],
                                             start=(c == 0), stop=(c == NFC - 1)), reads=["actT"], writes=[Bt(pb[half])])

            def p2_fin(j, ydst, sl):
                S.op("dve", _E.tensor_tensor(out=x2t[:, sl, :], in0=x2t[:, sl, :], in1=ps[:, 2560:3584], op=ALU.add),
                     writes=[Bt(5), Bt(6), "x2t%d" % sl])
                S.op("pool", _E.memset(ss3[:, j:j + 1], 0.0), writes=["ss3_%d" % j])
                S.op("act", _E.activation(out=junk2, in_=x2t[:, sl, :], func=AF.Square, accum_out=ss3[:, j:j + 1]),
                     reads=["x2t%d" % sl], writes=["junk2", "ss3_%d" % j])
                S.op("dve", _E.tensor_scalar(out=rs3[:, j:j + 1], in0=ss3[:, j:j + 1], scalar1=1.0 / D,
                                             scalar2=1e-6, op0=ALU.mult, op1=ALU.add),
                     reads=["ss3_%d" % j], writes=["rs3_%d" % j])
                S.op("pool", _E.tensor_tensor(out=rs3[:, j:j + 1], in0=rs3[:, j:j + 1], in1=mhalf[:, 0:1], op=ALU.pow),
                     reads=["rs3_%d" % j], writes=["rs3_%d" % j])
                S.op("dve", _E.scalar_tensor_tensor(out=x2t[:, sl, :], in0=x2t[:, sl, :], scalar=rs3[:, j:j + 1],
                                                    in1=gfin, op0=ALU.mult, op1=ALU.mult),
                     reads=["rs3_%d" % j, "gfin"], writes=["x2t%d" % sl])
                S.dma("sp", _E.dma_start(out=ydst[j * 128:(j + 1) * 128, :], in_=x2t[:, sl, :]),
                      reads=["x2t%d" % sl], sem="yout%d" % sl)

            def slot(gi_, j_):
                return (gi_ * G + j_) % NSL

            for j in range(groups[0][1]):
                p2_pre(groups[0][0], j, slot(0, j))
                p2_tp(j)
            for gi, (row0, ntl, ydst) in enumerate(groups):
                nxt = groups[gi + 1] if gi + 1 < len(groups) else None
                nn = nxt[1] if nxt else 0
                p2_chunks(ntl * 128, ntl)
                if nn > 0:
                    p2_pre(nxt[0], 0, slot(gi + 1, 0))
                tp_done = 0
                for j in range(ntl):
                    p2_down(j)
                    if j >= 1 and j - 1 < nn:
                        p2_tp(j - 1)
                        tp_done = j
                    p2_fin(j, ydst, slot(gi, j))
                    if j + 1 < nn:
                        p2_pre(nxt[0], j + 1, slot(gi + 1, j + 1))
                for jj in range(tp_done, nn):
                    if jj > ntl:
                        p2_pre(nxt[0], jj, slot(gi + 1, jj))
                    p2_tp(jj)

        except _Stop as ex:
            print('STOPPED at', ex)
        S.emit(final_wait_sems=list(S.dma_cum.keys()))
        print("ops per engine:", {e: len(v) for e, v in S.by_eng.items()})
    return nc


_PROG = {}


def kernel(x_prompt, x_sample, cache_fox_k, cache_fox_v, cache_fox_logf, state_ret,
           w_in, b_forget, ret_gn_gain, w_out, norm_mix_gain, norm_ffn_gain,
           w_gate, w_up, w_down, norm_final_gain):
    f = lambda a: np.ascontiguousarray(np.asarray(a, dtype=np.float32))
    x_prompt, x_sample = f(x_prompt), f(x_sample)
    ckf, cvf, clff, srf = f(cache_fox_k), f(cache_fox_v), f(cache_fox_logf), f(state_ret)
    cf, cosp, sinp, g128, g64 = _host_consts()
    if "nc" not in _PROG:
        _PROG["nc"] = build_program(g128, g64)
    nc = _PROG["nc"]
    vecs = np.zeros((128, 28), np.float32)
    vecs[:, 0:8] = f(norm_mix_gain)[0].reshape(8, 128).T
    vecs[:, 8:16] = f(norm_ffn_gain)[0].reshape(8, 128).T
    vecs[:, 16:20] = f(ret_gn_gain)[0].reshape(4, 128).T
    vecs[:, 20:28] = np.broadcast_to(f(b_forget)[0][None, :], (128, 8))
    gfin_b = np.ascontiguousarray(np.broadcast_to(f(norm_final_gain)[None, :], (128, D)))
    shared = {
        "w_in": f(w_in)[0], "w_out": f(w_out)[0], "w_gate": f(w_gate)[0], "w_up": f(w_up)[0],
        "w_down": f(w_down)[0], "cf32": cf, "cosp": cosp, "sinp": sinp, "vecs": vecs, "gfin_b": gfin_b,
    }
    in_maps = []
    for c in range(NCORES):
        m = dict(shared)
        m["xp"] = x_prompt[2 * c:2 * c + 2]
        m["xs"] = x_sample[2 * c:2 * c + 2]
        m["ck"] = ckf[0, 2 * c:2 * c + 2].reshape(2, PAST, 512)
        m["cv"] = cvf[0, 2 * c:2 * c + 2].reshape(2, PAST, 512)
        m["clf"] = clff[0, 2 * c:2 * c + 2]
        m["sret"] = srf[0, 2 * c:2 * c + 2]
        in_maps.append(m)
    res = run_bass_kernel_spmd(nc, in_maps, core_ids=list(range(NCORES)))
    R = res.results
    cat = lambda name: np.concatenate([np.asarray(r[name]) for r in R], axis=0)
    y_prompt = cat("y_p").reshape(16, T, D)
    y_sample = cat("y_s").reshape(16, NS, D)
    sr_p = cat("sr_p").reshape(1, 16, 4, 128, 128)
    kf_p = cat("kf_p").reshape(1, 16, T, 8, 64)
    vf_p = cat("vf_p").reshape(1, 16, T, 8, 64)
    lf_p = cat("lf_p").reshape(1, 16, T, 8)
    sr_s = cat("sr_s").reshape(1, 16, 4, 128, 128)
    kf_s = cat("kf_s").reshape(1, 16, NS, 8, 64)
    vf_s = cat("vf_s").reshape(1, 16, NS, 8, 64)
    lf_s = cat("lf_s").reshape(1, 16, NS, 8)
    return (y_prompt.astype(np.float32), y_sample.astype(np.float32), sr_p, kf_p, vf_p, lf_p,
            sr_s, kf_s, vf_s, lf_s)
```

```python
import numpy as np
from contextlib import ExitStack
import concourse.bass as bass
import concourse.mybir as mybir
from concourse.bass_utils import run_bass_kernel_spmd

F32 = mybir.dt.float32
BF16 = mybir.dt.bfloat16
AF = mybir.ActivationFunctionType
ALU = mybir.AluOpType

NCORES = 8
D = 1024
DIN = 3592
DFF = 2816
NFC = DFF // 128
T = 2048
NS = 64
PAST = 4096
NKT = PAST // 128
G = 4
ENGINES = ("pe", "act", "dve", "pool", "sp")


class Op:
    __slots__ = ("eng", "fn", "reads", "writes", "gidx", "eidx", "dma_sem", "dma_val",
                 "waits", "signal", "count", "clock")

    def __init__(self, eng, fn, reads, writes, dma_sem=None):
        self.eng = eng
        self.fn = fn
        self.reads = reads
        self.writes = writes
        self.dma_sem = dma_sem
        self.dma_val = 0
        self.waits = []
        self.signal = False
        self.count = 0
        self.clock = None


class Sched:
    def __init__(self, nc, same_engine_dist=10 ** 9):
        self.nc = nc
        self.ops = []
        self.by_eng = {e: [] for e in ENGINES}
        self.last_writer = {}
        self.readers = {}
        self.dma_cum = {}
        self.observed = {}
        self.same_engine_dist = same_engine_dist
        self.bar_ops = []
        self.bank_last = {}

    def op(self, eng, fn, reads=(), writes=()):
        o = Op(eng, fn, tuple(reads), tuple(writes))
        self._add(o)
        return o

    def dma(self, eng, fn, reads=(), writes=(), sem=None):
        o = Op(eng, fn, tuple(reads), tuple(writes), dma_sem=sem)
        self.dma_cum.setdefault(sem, 0)
        self._add(o)
        self.dma_cum[sem] += 16
        o.dma_val = self.dma_cum[sem]
        return o

    def _dep_on(self, o, d, obs):
        if d.dma_sem is not None:
            key = ("dma", d.dma_sem)
            val = self.dma_cum[d.dma_sem]
            if obs.get(key, 0) >= val:
                return
            obs[key] = val
            o.waits.append(("dma", d.dma_sem, val))
        else:
            key = ("eng", d.eng)
            if obs.get(key, -1) >= d.eidx:
                return
            o.waits.append(("eng", d.eng, d))
            d.signal = True
            for k, v in d.clock.items():
                if obs.get(k, -1) < v:
                    obs[k] = v

    def _add(self, o, force_deps=()):
        o.gidx = len(self.ops)
        o.eidx = len(self.by_eng[o.eng])
        deps = set(self.bar_ops)
        for t in o.reads:
            w = self.last_writer.get(t)
            if w is not None:
                deps.add(w)
        bank_toks = []
        for t in o.writes:
            if isinstance(t, str) and len(t) == 2 and t[0] == "B":
                bank_toks.append(t)
                for eng2, last in self.bank_last.setdefault(t, {}).items():
                    if eng2 != o.eng:
                        deps.add(last)
                continue
            w = self.last_writer.get(t)
            if w is not None:
                deps.add(w)
            for r in self.readers.get(t, ()):
                deps.add(r)
        deps.discard(o)
        obs = self.observed.setdefault(o.eng, {})
        for d in sorted(deps, key=lambda x: x.gidx):
            if d.dma_sem is None and d.eng == o.eng:
                if o.eng == "pe":
                    continue
                if o.dma_sem is None and o.eidx - d.eidx > self.same_engine_dist:
                    continue
            self._dep_on(o, d, obs)
        for d in force_deps:
            self._dep_on(o, d, obs)
        if o.dma_sem is None:
            clock = dict(obs)
            clock[("eng", o.eng)] = o.eidx
            o.clock = clock
        for t in o.reads:
            self.readers.setdefault(t, []).append(o)
        for t in o.writes:
            if t in bank_toks:
                self.bank_last[t][o.eng] = o
                continue
            self.last_writer[t] = o
            self.readers[t] = []
        self.ops.append(o)
        self.by_eng[o.eng].append(o)

    def barrier(self, fns):
        new = []
        for e, fn in fns.items():
            o = Op(e, fn, (), ())
            force = []
            if self.by_eng["pe"]:
                force.append(self.by_eng["pe"][-1])
            for other in fns:
                if other != e and self.by_eng[other]:
                    force.append(self.by_eng[other][-1])
            obs = self.observed.setdefault(e, {})
            for key, val in self.dma_cum.items():
                if obs.get(("dma", key), 0) < val:
                    obs[("dma", key)] = val
                    o.waits.append(("dma", key, val))
            self._add(o, force_deps=force)
            new.append(o)
        self.bar_ops = new
        self.last_writer = {}
        self.readers = {}
        self.bank_last = {}

    def emit(self, final_wait_sems=()):
        nc = self.nc
        with ExitStack() as st:
            esem = {e: st.enter_context(nc.semaphore("s_" + e)) for e in ENGINES}
            dsem = {k: st.enter_context(nc.semaphore("d_%s" % (str(k),))) for k in self.dma_cum}
            for e in ENGINES:
                c = 0
                for o in self.by_eng[e]:
                    if o.dma_sem is None and o.signal:
                        c += 1
                        o.count = c
            block = st.enter_context(nc.Block())

            def run(e, eng):
                for o in self.by_eng[e]:
                    for w in o.waits:
                        if w[0] == "dma":
                            eng.wait_ge(dsem[w[1]], w[2])
                        else:
                            eng.wait_ge(esem[w[1]], w[2].count)
                    inst = o.fn(eng)
                    if o.dma_sem is not None:
                        inst.then_inc(dsem[o.dma_sem], 16)
                    elif o.signal:
                        inst.then_inc(esem[e], 1)
                if e == "sp":
                    for k in final_wait_sems:
                        eng.wait_ge(dsem[k], self.dma_cum[k])

            @block.tensor
            def _(eng):
                run("pe", eng)

            @block.scalar
            def _(eng):
                run("act", eng)

            @block.vector
            def _(eng):
                run("dve", eng)

            @block.gpsimd
            def _(eng):
                run("pool", eng)

            @block.sync
            def _(eng):
                run("sp", eng)


class _Rec:
    def __getattr__(self, name):
        def mk(*a, **kw):
            return lambda eng: getattr(eng, name)(*a, **kw)
        return mk


_E = _Rec()


class _Stop(Exception):
    pass


def stop_at(name):
    import os
    if os.environ.get("MK_STOP", "") == name:
        raise _Stop(name)


class Arena:
    def __init__(self, t, nwords):
        self.t = t
        self.n = nwords
        self.off = 0

    def f32(self, ncols):
        assert self.off + ncols <= self.n, ("arena overflow", self.off, ncols, self.n)
        ap = self.t[:, self.off:self.off + ncols]
        self.off += ncols
        return ap

    def bf16(self, ncols):
        w = (ncols + 1) // 2
        ap = self.f32(w).bitcast(BF16)
        return ap[:, 0:ncols]


CF_LAYOUT = {}


def _cf_layout():
    off = 0
    for name, n in (("ident", 128), ("U", 128), ("ones", 128), ("Lst", 128), ("negm", 128),
                    ("Mp", 512), ("qd", 512), ("kdp", 4), ("kds", 4), ("coss", 64), ("sins", 64),
                    ("mhalf", 4), ("mone", 4)):
        CF_LAYOUT[name] = (off, n)
        off += n
    return off


CF_N = _cf_layout()


def _host_consts():
    lg = np.log1p(-np.exp2(-5.0 - np.arange(4, dtype=np.float32))).astype(np.float64)
    sc = 128.0 ** -0.5
    i = np.arange(128)
    cf = np.zeros((128, CF_N), np.float32)

    def put(name, arr):
        o, n = CF_LAYOUT[name]
        cf[:arr.shape[0], o:o + n] = arr.reshape(arr.shape[0], n)

    put("ident", np.eye(128, dtype=np.float32))
    put("U", (i[:, None] <= i[None, :]).astype(np.float32))
    put("ones", np.ones((128, 128), np.float32))
    put("Lst", (i[:, None] > i[None, :]).astype(np.float32))
    put("negm", np.where(i[None, :] >= i[:, None], 0.0, -30000.0).astype(np.float32))
    diff = (i[None, :] - i[:, None]).astype(np.float64)
    same = (i[:, None] // 64) == (i[None, :] // 64)
    fwd = (i[:, None] < 64) & (i[None, :] >= 64)
    Mp = np.zeros((128, 4, 128), np.float64)
    qd = np.zeros((128, 4, 128), np.float64)
    for h in range(4):
        Mp[:, h, :] = np.where(same, np.exp(lg[h] * np.abs(diff)),
                               np.where(fwd, np.exp(lg[h] * diff), 0.0)) * sc
        qd[:, h, :] = (sc * np.exp(lg[h] * (i + 1.0)))[None, :]
    put("Mp", Mp.astype(np.float32))
    put("qd", qd.astype(np.float32))
    put("kdp", np.exp(lg[None, :] * (127.0 - i[:, None])).astype(np.float32))
    kds = np.exp(lg[None, :] * (63.0 - i[:64, None])).astype(np.float32)
    put("kds", kds)
    inv = (10000.0 ** (-np.arange(64, dtype=np.float32) / 64.0)).astype(np.float32)
    pos = np.arange(T, dtype=np.float32)
    ang = (pos[:, None] * inv[None, :]).astype(np.float32)
    cosp = np.cos(ang).astype(np.float32).reshape(T // 128, 128, 64)
    sinp = np.sin(ang).astype(np.float32).reshape(T // 128, 128, 64)
    poss = np.arange(NS, dtype=np.float32) + float(PAST)
    angs = (poss[:, None] * inv[None, :]).astype(np.float32)
    put("coss", np.cos(angs).astype(np.float32))
    put("sins", np.sin(angs).astype(np.float32))
    put("mhalf", np.full((128, 4), -0.5, np.float32))
    put("mone", np.full((128, 4), -1.0, np.float32))
    g128 = [float(np.exp(lg[h] * 128.0)) for h in range(4)]
    g64 = [float(np.exp(lg[h] * 64.0)) for h in range(4)]
    return cf, cosp, sinp, g128, g64


def build_program(g128, g64):
    nc = bass.Bass("TRN2", target_bir_lowering=False)

    def din(name, shape):
        return nc.dram_tensor(name, list(shape), F32, kind="ExternalInput").ap()

    def dout(name, shape):
        return nc.dram_tensor(name, list(shape), F32, kind="ExternalOutput").ap()

    xp = din("xp", (2, T, D))
    xs_d = din("xs", (2, NS, D))
    ck = din("ck", (2, PAST, 512))
    cv = din("cv", (2, PAST, 512))
    clf = din("clf", (2, PAST, 8))
    sret = din("sret", (2, 4, 128, 128))
    w_in = din("w_in", (D, DIN))
    w_out = din("w_out", (D, D))
    w_gate = din("w_gate", (D, DFF))
    w_up = din("w_up", (D, DFF))
    w_down = din("w_down", (DFF, D))
    cf_d = din("cf32", (128, CF_N))
    cosp_d = din("cosp", (T // 128, 128, 64))
    sinp_d = din("sinp", (T // 128, 128, 64))
    vec_d = din("vecs", (128, 8 + 8 + 4 + 8))
    gfin_d = din("gfin_b", (128, D))

    y_p = dout("y_p", (2 * T, D))
    y_s = dout("y_s", (2 * NS, D))
    sr_p = dout("sr_p", (2, 4, 128, 128))
    kf_p = dout("kf_p", (2, T, 512))
    vf_p = dout("vf_p", (2, T, 512))
    lf_p = dout("lf_p", (2, T, 8))
    sr_s = dout("sr_s", (2, 4, 128, 128))
    kf_s = dout("kf_s", (2, NS, 512))
    vf_s = dout("vf_s", (2, NS, 512))
    lf_s = dout("lf_s", (2, NS, 8))
    NTOK = 2 * T + 2 * NS
    x1s = nc.dram_tensor("x1s", [NTOK, D], F32, kind="Internal").ap()

    with ExitStack() as st:
        AW = 52600
        arena_t = st.enter_context(nc.sbuf_tensor("arena", [128, AW], F32))
        ps = st.enter_context(nc.psum_tensor("ps", [128, 4096], F32))
        S = Sched(nc)
        A = Arena(arena_t, AW)

        def B(i):
            return ps[:, i * 512:(i + 1) * 512]

        def Bt(i):
            return "B%d" % i

        try:
            cf = A.f32(CF_N)
            vecs = A.f32(28)
            identb = A.bf16(128)
            negmb = A.bf16(128)
            dummy = A.f32(4)

            def C(name, rows=128):
                o, n = CF_LAYOUT[name]
                return cf[0:rows, o:o + n]

            gmixc = vecs[:, 0:8]
            gffnc = vecs[:, 8:16]
            gnc = vecs[:, 16:20]
            bfb = vecs[:, 20:28]

            S.dma("sp", _E.dma_start(out=cf, in_=cf_d), writes=["cf"], sem="cf")
            S.dma("sp", _E.dma_start(out=vecs, in_=vec_d), writes=["vecs"], sem="cf")
            S.op("pool", _E.tensor_copy(out=identb, in_=C("ident")), reads=["cf"], writes=["identb"])
            S.op("pool", _E.tensor_copy(out=negmb, in_=C("negm")), reads=["cf"], writes=["negmb"])

            def bar():
                S.barrier({
                    "act": _E.copy(out=dummy[:, 0:1], in_=dummy[:, 1:2]),
                    "dve": _E.memset(dummy[:, 2:3], 0.0),
                    "pool": _E.memset(dummy[:, 3:4], 0.0),
                })

            S.op("pool", _E.memset(dummy, 0.0), writes=["dummy"])

            mark_persist = A.off
            stop_at("pro0")

            w_in_bf = A.bf16(8 * DIN).rearrange("p (k c) -> p k c", k=8)
            w_out_bf = A.bf16(8 * D).rearrange("p (k c) -> p k c", k=8)
            mark_w1 = A.off
            NSTG = 4
            stage = [A.f32(DIN) for _ in range(NSTG)]
            A.off = mark_w1
            xt = A.f32(G * D).rearrange("p (j c) -> p j c", j=G)
            xsb = [A.bf16(D), A.bf16(D)]
            hT = A.bf16(8 * G * 128).rearrange("p (k t) -> p k t", k=8)
            QB = A.bf16(4 * G * 2 * 128).rearrange("p (c j e t) -> p c j e t", c=4, j=G, e=2)
            kfT = A.bf16(4 * T).rearrange("p (c t) -> p c t", c=4)
            vaug_off = A.off
            Vaug = A.bf16(16 * 8 * 65).rearrange("p (t h e) -> p t h e", t=16, h=8)
            qkrot = [A.bf16(1024), A.bf16(1024)]
            rt = [A.f32(512) for _ in range(4)]
            vtok = [A.bf16(512), A.bf16(512)]
            gsb1 = A.f32(512)
            egs1 = A.f32(512)
            gsb = [gsb1, gsb1]
            egs = [egs1, egs1]
            onb = [rt[2], rt[3]]
            kfo = [A.f32(512), A.f32(512)]
            vfo = [A.f32(512), A.f32(512)]
            qkT = [A.bf16(1024).rearrange("p (h t) -> p h t", h=8) for _ in range(2)]
            qdT = [A.bf16(512).rearrange("p (h t) -> p h t", h=4) for _ in range(2)]
            kdtok = [A.bf16(512), A.bf16(512)]
            sTm = [A.bf16(512).rearrange("p (h t) -> p h t", h=4) for _ in range(2)]
            Sst = A.f32(512).rearrange("p (h e) -> p h e", h=4)
            Sbf = A.bf16(512).rearrange("p (h e) -> p h e", h=4)
            NPT = 12
            PT = [A.bf16(128) for _ in range(NPT)]
            mixb = [A.bf16(1024), A.bf16(1024)]
            mixT = [A.bf16(1024).rearrange("p (k t) -> p k t", k=8) for _ in range(2)]
            ctab = A.f32(128).rearrange("p (t h) -> p t h", h=8)
            Btab = [A.f32(128).rearrange("p (t h) -> p t h", h=8) for _ in range(2)]
            Rcar = A.f32(8)
            cst = [A.f32(128), A.f32(128)]
            sm = A.f32(128)
            clfs = A.f32(256).rearrange("p (t h) -> p t h", h=8)
            atab = A.f32(256).rearrange("p (t h) -> p t h", h=8)
            tots = A.f32(256).rearrange("p (t h) -> p t h", h=8)
            sufs = A.f32(256).rearrange("p (t h) -> p t h", h=8)
            negbq = A.f32(8)
            _save = A.off
            A.off = vaug_off + 8 * 65
            kst = [A.f32(512), A.f32(512)]
            vst = [A.f32(512), A.f32(512)]
            kTt = [A.bf16(512).rearrange("p (c t) -> p c t", c=4) for _ in range(2)]
            Vt = [A.bf16(8 * 65).rearrange("p (h e) -> p h e", h=8) for _ in range(2)]
            PTs = [A.bf16(512).rearrange("p (h q) -> p h q", h=8) for _ in range(2)]
            assert A.off <= vaug_off + 8 * 65 * 8, (A.off, vaug_off)
            A.off = _save
            print("phase1 arena words used:", A.off, "of", AW)

            ss = sm[:, 0:G]
            rs = sm[:, 8:8 + G]
            st6 = sm[:, 16:40].rearrange("p (h s) -> p h s", h=4)
            mv = sm[:, 40:48].rearrange("p (h s) -> p h s", h=4)
            rstdg = sm[:, 48:52]
            rcp = sm[:, 56:64]
            xl = sm[:, 64:72]
            la = sm[:, 72:80]
            le = sm[:, 80:88]
            ll = sm[:, 88:96]
            lmn = sm[:, 96:104]
            lft = [sm[:, 104:112], sm[:, 112:120]]

            mhalf = C("mhalf")

            def convert_rows(dst, src_sb, ncols, scale_col, engines=("act", "dve", "pool"), rd=()):
                shares = {"act": 0.44, "dve": 0.44, "pool": 0.12}
                bounds = [0]
                acc_ = 0.0
                for eng in engines:
                    acc_ += shares[eng]
                    bounds.append(min(ncols, int(round(ncols * acc_ / 8.0)) * 8))
                bounds[-1] = ncols
                for i, eng in enumerate(engines):
                    c0, c1 = bounds[i], bounds[i + 1]
                    if c0 >= c1:
                        continue
                    if scale_col is None:
                        if eng == "act":
                            S.op("act", _E.copy(out=dst[:, c0:c1], in_=src_sb[:, c0:c1]), reads=rd)
                        else:
                            S.op(eng, _E.tensor_copy(out=dst[:, c0:c1], in_=src_sb[:, c0:c1]), reads=rd)
                    else:
                        if eng == "act":
                            S.op("act", _E.activation(out=dst[:, c0:c1], in_=src_sb[:, c0:c1],
                                                                           func=AF.Identity, scale=scale_col), reads=rd)
                        else:
                            S.op(eng, _E.tensor_scalar(out=dst[:, c0:c1], in0=src_sb[:, c0:c1],
                                                                            scalar1=scale_col, scalar2=None,
                                                                            op0=ALU.mult), reads=rd)

            for k in range(8):
                sg = stage[k % NSTG]
                tok = "stage%d" % (k % NSTG)
                S.dma("sp", _E.dma_start(out=sg[:, 0:DIN], in_=w_in[k * 128:(k + 1) * 128, :]),
                      writes=[tok], sem=tok)
                convert_rows(w_in_bf[:, k, :], sg, DIN, gmixc[:, k:k + 1], rd=[tok, "vecs"])
            for k in range(8):
                sg = stage[k % NSTG]
                tok = "stage%d" % (k % NSTG)
                S.dma("sp", _E.dma_start(out=sg[:, 0:D], in_=w_out[k * 128:(k + 1) * 128, :]),
                      writes=[tok], sem=tok)
                convert_rows(w_out_bf[:, k, :], sg, D, gnc[:, k:k + 1] if k < 4 else None, rd=[tok, "vecs"])
            bar()
            stop_at("pro")

            tpb = B(0).bitcast(BF16)
            tp3 = tpb.rearrange("p (k t) -> p k t", k=8)
            tpf3 = B(0).rearrange("p (k t) -> p k t", k=4)
            cnt = {"tile": 0, "pt": 0, "sl": 0, "ev": 0}

            def load_x(src_rows, n, j):
                S.dma("sp", _E.dma_start(out=xt[0:n, j, :], in_=src_rows), writes=["xt%d" % j], sem="xt%d" % j)

            def stage_A(n, j):
                b = cnt["tile"] % 2
                S.op("pool", _E.memset(ss[0:n, j:j + 1], 0.0), writes=["ss%d" % j])
                S.op("act", _E.activation(out=xsb[b][0:n, :], in_=xt[0:n, j, :], func=AF.Square,
                                                   accum_out=ss[0:n, j:j + 1]),
                     reads=["xt%d" % j], writes=["xsb%d" % b, "ss%d" % j])
                S.op("dve", _E.tensor_scalar(out=rs[0:n, j:j + 1], in0=ss[0:n, j:j + 1], scalar1=1.0 / D,
                                                      scalar2=1e-6, op0=ALU.mult, op1=ALU.add),
                     reads=["ss%d" % j], writes=["rs%d" % j])
                S.op("pool", _E.tensor_tensor(out=rs[0:n, j:j + 1], in0=rs[0:n, j:j + 1], in1=mhalf[0:n, 0:1],
                                                       op=ALU.pow), reads=["rs%d" % j], writes=["rs%d" % j])
                S.op("dve", _E.tensor_scalar(out=xsb[b][0:n, :], in0=xt[0:n, j, :], scalar1=rs[0:n, j:j + 1],
                                                      scalar2=None, op0=ALU.mult),
                     reads=["xt%d" % j, "rs%d" % j], writes=["xsb%d" % b])
                for k in range(8):
                    S.op("pe", _E.transpose(out=tpb[:, k * 128:k * 128 + n],
                                                          in_=xsb[b][0:n, k * 128:(k + 1) * 128],
                                                          identity=identb[0:n, 0:n]),
                         reads=["xsb%d" % b], writes=[Bt(0)])
                S.op("act", _E.copy(out=hT[:, :, j * 128:j * 128 + n], in_=tp3[:, :, 0:n]),
                     writes=[Bt(0), "hT%d" % j])
                cnt["tile"] += 1

            def evac_copy(dst, src, btok, wtoks):
                eng = "act" if cnt["ev"] % 2 == 0 else "dve"
                cnt["ev"] += 1
                if eng == "act":
                    S.op("act", _E.copy(out=dst, in_=src), writes=[btok] + wtoks)
                else:
                    S.op("dve", _E.tensor_copy(out=dst, in_=src), writes=[btok] + wtoks)

            def stage_B(NT, ntl, tok0):
                hts = ["hT%d" % j for j in range(ntl)]
                for c in range(8):
                    col0 = 2048 + c * 128 if c < 4 else 2560 + (c - 4) * 128
                    bk = 1 + (c % 4)
                    for k in range(8):
                        S.op("pe", _E.matmul(
                            B(bk)[:, 0:NT], lhsT=w_in_bf[:, k, col0:col0 + 128], rhs=hT[:, k, 0:NT],
                            start=(k == 0), stop=(k == 7)), reads=hts, writes=[Bt(bk)])
                    if c < 4:
                        tw = min(NT, 128)
                        for e_ in range(2):
                            r0_, r1_ = e_ * 64, (e_ + 1) * 64
                            evac_copy(QB[r0_:r1_, c, 0:ntl, e_, 0:tw],
                                      B(bk)[r0_:r1_, 0:NT].rearrange("p (j t) -> p j t", t=tw),
                                      Bt(bk), ["qfT%d_%d" % (c, e_)])
                    else:
                        evac_copy(kfT[:, c - 4, tok0:tok0 + NT], B(bk)[:, 0:NT], Bt(bk), ["kfT%d" % (c - 4)])

            def inproj_tok(n, j, bk, col0, ncols):
                for k in range(8):
                    S.op("pe", _E.matmul(B(bk)[0:n, 0:ncols], lhsT=hT[:, k, j * 128:j * 128 + n],
                                                       rhs=w_in_bf[:, k, col0:col0 + ncols],
                                                       start=(k == 0), stop=(k == 7)),
                         reads=["hT%d" % j], writes=[Bt(bk)])

            def stage_C(n, j, b, cos_ap, sin_ap, cs_tok, kt, kf_dst, vf_dst, lf_dst, vaug_dst):
                inproj_tok(n, j, 1, 0, 512)
                yield
                inproj_tok(n, j, 2, 512, 512)
                yield
                inproj_tok(n, j, 3, 1024, 512)
                S.op("act", _E.copy(out=vtok[b][0:n, :], in_=B(3)[0:n, :]), writes=[Bt(3), "vtok%d" % b])
                yield
                qk4 = ps[0:n, 512:1536].rearrange("p (h t f) -> p h t f", t=2, f=64)
                x1 = qk4[:, :, 0, :]
                x2 = qk4[:, :, 1, :]
                cosb = cos_ap.unsqueeze(1).to_broadcast([n, 8, 64])
                sinb = sin_ap.unsqueeze(1).to_broadcast([n, 8, 64])
                r3 = [r[0:n, :].rearrange("p (h f) -> p h f", f=64) for r in rt]
                qr4 = qkrot[b][0:n, :].rearrange("p (h t f) -> p h t f", t=2, f=64)
                bb = [Bt(1), Bt(2)]
                S.op("dve", _E.tensor_tensor(out=r3[0], in0=x1, in1=cosb, op=ALU.mult), reads=[cs_tok], writes=bb + ["rt0"])
                S.op("dve", _E.tensor_tensor(out=r3[1], in0=x2, in1=sinb, op=ALU.mult), reads=[cs_tok], writes=bb + ["rt1"])
                S.op("pool", _E.tensor_tensor(out=qr4[:, :, 0, :], in0=r3[0], in1=r3[1], op=ALU.subtract),
                     reads=["rt0", "rt1"], writes=["qkrot%da" % b])
                S.op("dve", _E.tensor_tensor(out=r3[2], in0=x1, in1=sinb, op=ALU.mult), reads=[cs_tok], writes=bb + ["rt2"])
                S.op("dve", _E.tensor_tensor(out=r3[3], in0=x2, in1=cosb, op=ALU.mult), reads=[cs_tok], writes=bb + ["rt3"])
                S.op("pool", _E.tensor_tensor(out=qr4[:, :, 1, :], in0=r3[2], in1=r3[3], op=ALU.add),
                     reads=["rt2", "rt3"], writes=["qkrot%db" % b])
                inproj_tok(n, j, 3, 1536, 512)
                S.op("act", _E.copy(out=gsb[b][0:n, :], in_=B(3)[0:n, :]), writes=[Bt(3), "gsb"])
                S.op("act", _E.activation(out=egs[b][0:n, :], in_=B(3)[0:n, :], func=AF.Exp, scale=-1.0),
                     writes=[Bt(3), "egs"])
                yield
                S.op("act", _E.activation(out=egs[b][0:n, :], in_=egs[b][0:n, :], func=AF.Ln, bias=1.0),
                     reads=["egs"], writes=["egs"])
                S.op("act", _E.activation(out=egs[b][0:n, :], in_=egs[b][0:n, :], func=AF.Exp, scale=-1.0),
                     reads=["egs"], writes=["egs"])
                yield
                inproj_tok(n, j, 1, 2560, 512)
                yield
                inproj_tok(n, j, 2, 3072, 512)
                yield
                inproj_tok(n, j, 3, 3584, 8)
                S.op("act", _E.copy(out=kfo[b][0:n, :], in_=B(1)[0:n, :]), writes=[Bt(1), "kfo%d" % b])
                S.op("act", _E.copy(out=vfo[b][0:n, :], in_=B(2)[0:n, :]), writes=[Bt(2), "vfo%d" % b])
                S.dma("sp", _E.dma_start(out=kf_dst, in_=kfo[b][0:n, :]), reads=["kfo%d" % b], sem="kfo%d" % b)
                S.dma("sp", _E.dma_start(out=vf_dst, in_=vfo[b][0:n, :]), reads=["vfo%d" % b], sem="vfo%d" % b)
                S.op("pool", _E.tensor_copy(out=vaug_dst, in_=vfo[b][0:n, :].rearrange("p (h f) -> p h f", f=64)),
                     reads=["vfo%d" % b], writes=["vaug"])
                S.op("dve", _E.tensor_tensor(out=xl[0:n, :], in0=B(3)[0:n, 0:8], in1=bfb[0:n, :], op=ALU.add),
                     writes=[Bt(3), "xl"])
                S.op("act", _E.activation(out=la[0:n, :], in_=xl[0:n, :], func=AF.Abs), reads=["xl"], writes=["la"])
                S.op("act", _E.activation(out=le[0:n, :], in_=la[0:n, :], func=AF.Exp, scale=-1.0),
                     reads=["la"], writes=["le"])
                S.op("act", _E.activation(out=ll[0:n, :], in_=le[0:n, :], func=AF.Ln, bias=1.0),
                     reads=["le"], writes=["ll"])
                S.op("dve", _E.tensor_scalar(out=lmn[0:n, :], in0=xl[0:n, :], scalar1=0.0, scalar2=None,
                                                      op0=ALU.min), reads=["xl"], writes=["lmn"])
                S.op("dve", _E.tensor_tensor(out=lft[b][0:n, :], in0=lmn[0:n, :], in1=ll[0:n, :],
                                                      op=ALU.subtract), reads=["lmn", "ll"], writes=["lft%d" % b])
                S.dma("sp", _E.dma_start(out=lf_dst, in_=lft[b][0:n, :]), reads=["lft%d" % b], sem="lft%d" % b)
                yield

            def stage_ret(n, b, kd_ap, gdec):
                yield
                for h in range(8):
                    S.op("pe", _E.transpose(out=tpb[:, h * 128:h * 128 + n],
                                                          in_=qkrot[b][0:n, h * 128:(h + 1) * 128],
                                                          identity=identb[0:n, 0:n]),
                         reads=["qkrot%da" % b, "qkrot%db" % b], writes=[Bt(0)])
                S.op("act", _E.copy(out=qkT[b][:, :, 0:n], in_=tp3[:, :, 0:n]), writes=[Bt(0), "qkT%d" % b])
                yield
                qd3 = C("qd").rearrange("p (h t) -> p h t", h=4)
                S.op("pool", _E.tensor_tensor(out=qdT[b][:, :, 0:n], in0=qkT[b][:, 0:4, 0:n], in1=qd3[:, :, 0:n],
                                                       op=ALU.mult), reads=["qkT%d" % b], writes=["qdT%d" % b])
                for h in range(4):
                    S.op("pool", _E.tensor_scalar(out=kdtok[b][0:n, h * 128:(h + 1) * 128],
                                                                in0=qkrot[b][0:n, 512 + h * 128:512 + (h + 1) * 128],
                                                                scalar1=kd_ap[0:n, h:h + 1], scalar2=None, op0=ALU.mult),
                         reads=["qkrot%da" % b, "qkrot%db" % b], writes=["kdtok%d_%d" % (b, h)])
                for h in range(4):
                    S.op("pe", _E.matmul(B(1)[0:n, h * 128:h * 128 + n], lhsT=qkT[b][:, 4 + h, 0:n],
                                                       rhs=qkT[b][:, h, 0:n], start=True, stop=True),
                         reads=["qkT%d" % b], writes=[Bt(1)])
                Mp3 = C("Mp").rearrange("p (h t) -> p h t", h=4)
                s43 = B(1).rearrange("p (h t) -> p h t", h=4)
                S.op("dve", _E.tensor_tensor(out=sTm[b][0:n, :, 0:n], in0=s43[0:n, :, 0:n], in1=Mp3[0:n, :, 0:n],
                                                      op=ALU.mult), writes=[Bt(1), "sTm%d" % b])
                yield
                for h in range(4):
                    S.op("pe", _E.matmul(B(3)[0:n, h * 128:(h + 1) * 128], lhsT=sTm[b][0:n, h, 0:n],
                                                       rhs=vtok[b][0:n, h * 128:(h + 1) * 128], start=True, stop=False),
                         reads=["sTm%d" % b, "vtok%d" % b], writes=[Bt(3)])
                    S.op("pe", _E.matmul(B(3)[0:n, h * 128:(h + 1) * 128], lhsT=qdT[b][:, h, 0:n],
                                                       rhs=Sbf[:, h, :], start=False, stop=True),
                         reads=["qdT%d" % b, "Sbf"], writes=[Bt(3)])
                for h in range(4):
                    S.op("pe", _E.matmul(B(2)[:, h * 128:(h + 1) * 128],
                                                       lhsT=kdtok[b][0:n, h * 128:(h + 1) * 128],
                                                       rhs=vtok[b][0:n, h * 128:(h + 1) * 128], start=True, stop=True),
                         reads=["kdtok%d_%d" % (b, h), "vtok%d" % b], writes=[Bt(2)])
                for h in range(4):
                    S.op("dve", _E.scalar_tensor_tensor(out=Sst[:, h, :], in0=Sst[:, h, :], scalar=gdec[h],
                                                                      in1=B(2)[:, h * 128:(h + 1) * 128],
                                                                      op0=ALU.mult, op1=ALU.add),
                         writes=[Bt(2), "Sst"])
                S.op("pool", _E.tensor_copy(out=Sbf, in_=Sst), reads=["Sst"], writes=["Sbf"])
                yield
                for h in range(4):
                    S.op("dve", _E.bn_stats(out=st6[0:n, h, :], in_=B(3)[0:n, h * 128:(h + 1) * 128]),
                         writes=[Bt(3), "st6_%d" % h])
                for h in range(4):
                    S.op("dve", _E.bn_aggr(out=mv[0:n, h, :], in_=st6[0:n, h, :]),
                         reads=["st6_%d" % h], writes=["mv%d" % h])
                S.op("dve", _E.tensor_scalar(out=rstdg[0:n, :], in0=mv[0:n, :, 1], scalar1=1e-5, scalar2=None,
                                                      op0=ALU.add), reads=["mv%d" % h for h in range(4)], writes=["rstdg"])
                S.op("pool", _E.tensor_tensor(out=rstdg[0:n, :], in0=rstdg[0:n, :], in1=mhalf[0:n, :], op=ALU.pow),
                     reads=["rstdg"], writes=["rstdg"])
                for h in range(4):
                    S.op("dve", _E.tensor_scalar(out=onb[b][0:n, h * 128:(h + 1) * 128],
                                                               in0=B(3)[0:n, h * 128:(h + 1) * 128],
                                                               scalar1=mv[0:n, h, 0:1], scalar2=rstdg[0:n, h:h + 1],
                                                               op0=ALU.subtract, op1=ALU.mult),
                         reads=["rstdg", "mv%d" % h], writes=[Bt(3), "rt%d" % (2 + b)])
                S.op("pool", _E.tensor_tensor(out=gsb[b][0:n, :], in0=gsb[b][0:n, :], in1=onb[b][0:n, :], op=ALU.mult),
                     reads=["gsb", "rt%d" % (2 + b)], writes=["gsb"])
                S.op("pool", _E.tensor_tensor(out=mixb[b][0:n, 0:512], in0=gsb[b][0:n, :], in1=egs[b][0:n, :],
                                                       op=ALU.mult), reads=["gsb", "egs"], writes=["mixr%d" % b])
                yield

            def cumsum_tile(n, b, first):
                S.op("pe", _E.matmul(B(4)[0:n, 0:8], lhsT=C("U")[0:n, 0:n], rhs=lft[b][0:n, :], start=True, stop=True),
                     reads=["lft%d" % b, "cf"], writes=[Bt(4)])
                S.op("pe", _E.matmul(B(4)[:, 8:16], lhsT=C("ones")[0:n, :], rhs=lft[b][0:n, :], start=True, stop=True),
                     reads=["lft%d" % b, "cf"], writes=[Bt(4)])

            def fox_finish_head(n, b, h, acc_bank):
                S.op("dve", _E.reciprocal(out=rcp[0:n, h:h + 1], in_=B(acc_bank)[0:n, 64:65]),
                     writes=[Bt(acc_bank), "rcp%d" % h])
                S.op("dve", _E.tensor_scalar(out=mixb[b][0:n, 512 + h * 64:512 + (h + 1) * 64],
                                                      in0=B(acc_bank)[0:n, 0:64], scalar1=rcp[0:n, h:h + 1],
                                                      scalar2=None, op0=ALU.mult),
                     reads=["rcp%d" % h], writes=[Bt(acc_bank), "mixf%d_%d" % (b, h)])

            def stage_fox_prompt(j, b, t, filler=None):
                cumsum_tile(128, b, t == 0)
                S.op("dve", _E.tensor_tensor(out=ctab[:, t, :], in0=B(4)[:, 0:8], in1=Rcar, op=ALU.add),
                     reads=["Rcar"], writes=[Bt(4), "ctab"])
                S.op("dve", _E.tensor_tensor(out=Rcar, in0=B(4)[:, 8:16], in1=Rcar, op=ALU.add),
                     writes=[Bt(4), "Rcar"])
                bt_ = Btab[t % 2]
                S.op("dve", _E.tensor_tensor(out=bt_[:, 0:t + 1, :],
                                                      in0=Rcar.unsqueeze(1).to_broadcast([128, t + 1, 8]),
                                                      in1=ctab[:, 0:t + 1, :], op=ALU.subtract),
                     reads=["Rcar", "ctab"], writes=["Btab%d" % (t % 2)])
                macros = []
                for p_ in range(4):
                    kts = list(range(t + 1))
                    for i0_ in range(0, len(kts), 2):
                        macros.append((p_, kts[i0_:i0_ + 2]))
                slots = (5, 6)
                LAGM = 2
                pend = []

                def emit_pv(mi):
                    p_, kts_ = macros[mi]
                    for q_, kt in enumerate(kts_):
                        for e_ in range(2):
                            h = 2 * p_ + e_
                            pt = pend[mi][q_ * 2 + e_]
                            accb = 7 if e_ == 0 else 4
                            S.op("pe", _E.matmul(B(accb)[:, 0:65], lhsT=PT[pt], rhs=Vaug[:, kt, h, :],
                                                 start=(kt == 0), stop=(kt == t)),
                                 reads=["PT%d" % pt, "vaug"], writes=[Bt(accb)])
                            if kt == t:
                                fox_finish_head(128, b, h, accb)

                for mi, (p_, kts_) in enumerate(macros):
                    sl = slots[cnt["sl"] % 2]
                    cnt["sl"] += 1
                    pts = []
                    for q_, kt in enumerate(kts_):
                        S.op("pe", _E.matmul(
                            B(sl)[:, q_ * 256:(q_ + 1) * 256], lhsT=kfT[:, p_, kt * 128:(kt + 1) * 128],
                            rhs=QB[:, p_, j, :, :].rearrange("p e t -> p (e t)"), start=True, stop=(kt != t)),
                            reads=["kfT%d" % p_, "qfT%d_0" % p_, "qfT%d_1" % p_], writes=[Bt(sl)])
                        if kt == t:
                            for e_ in range(2):
                                c0_ = q_ * 256 + e_ * 128
                                S.op("pe", _E.matmul(B(sl)[:, c0_:c0_ + 128], lhsT=identb, rhs=negmb,
                                                     start=False, stop=(e_ == 1)),
                                     reads=["identb", "negmb"], writes=[Bt(sl)])
                    for q_, kt in enumerate(kts_):
                        for e_ in range(2):
                            h = 2 * p_ + e_
                            c0_ = q_ * 256 + e_ * 128
                            pt = cnt["pt"] % NPT
                            cnt["pt"] += 1
                            pts.append(pt)
                            S.op("act", _E.activation(
                                out=PT[pt], in_=B(sl)[:, c0_:c0_ + 128], func=AF.Exp, bias=bt_[:, kt, h:h + 1], scale=0.125),
                                reads=["Btab%d" % (t % 2)], writes=[Bt(sl), "PT%d" % pt])
                    pend.append(pts)
                    if mi >= LAGM:
                        emit_pv(mi - LAGM)
                    if filler is not None:
                        for _ in range(3 if t < 4 else (2 if t < 8 else 1)):
                            next(filler, None)
                for mi in range(max(0, len(macros) - LAGM), len(macros)):
                    emit_pv(mi)
                if filler is not None:
                    for _ in filler:
                        pass

            def stage_E(n, j, b, x1_rows):
                mixtoks = ["mixr%d" % b] + ["mixf%d_%d" % (b, h) for h in range(8)]
                for k in range(8):
                    S.op("pe", _E.transpose(out=tpb[:, k * 128:k * 128 + n],
                                                          in_=mixb[b][0:n, k * 128:(k + 1) * 128],
                                                          identity=identb[0:n, 0:n]),
                         reads=mixtoks, writes=[Bt(0)])
                S.op("act", _E.copy(out=mixT[b][:, :, 0:n], in_=tp3[:, :, 0:n]), writes=[Bt(0), "mixT%d" % b])
                yield
                for half in range(2):
                    if half == 1:
                        yield
                    for k in range(8):
                        S.op("pe", _E.matmul(B(1 + half)[0:n, :], lhsT=mixT[b][:, k, 0:n],
                                                                      rhs=w_out_bf[:, k, half * 512:(half + 1) * 512],
                                                                      start=(k == 0), stop=(k == 7)),
                             reads=["mixT%d" % b], writes=[Bt(1 + half)])
                S.op("dve", _E.tensor_tensor(out=xt[0:n, j, :], in0=xt[0:n, j, :], in1=ps[0:n, 512:1536], op=ALU.add),
                     writes=[Bt(1), Bt(2), "xt%d" % j])
                S.dma("sp", _E.dma_start(out=x1_rows, in_=xt[0:n, j, :]), reads=["xt%d" % j], sem="x1st%d" % j)
                yield

            def load_cs(t):
                c = cst[t % 2]
                S.dma("sp", _E.dma_start(out=c[:, 0:64], in_=cosp_d[t]), writes=["cs%d" % (t % 2)], sem="cs%d" % (t % 2))
                S.dma("sp", _E.dma_start(out=c[:, 64:128], in_=sinp_d[t]), writes=["cs%d" % (t % 2)], sem="cs%d" % (t % 2))

            S.op("pool", _E.memset(Vaug.rearrange("p t h e -> p (t h e)"), 1.0), writes=["vaug"])
            S.op("pool", _E.memset(QB.rearrange("p c j e t -> p (c j e t)"), 0.0),
                 writes=["qfT%d_%d" % (c_, e_) for c_ in range(4) for e_ in range(2)])
            for s in range(2):
                S.op("pool", _E.memset(Sst.rearrange("p h e -> p (h e)"), 0.0), writes=["Sst"])
                S.op("pool", _E.memset(Sbf.rearrange("p h e -> p (h e)"), 0.0), writes=["Sbf"])
                S.op("pool", _E.memset(Rcar, 0.0), writes=["Rcar"])
                for j in range(G):
                    load_x(xp[s, j * 128:(j + 1) * 128, :], 128, j)
                load_cs(0)
                for g in range(T // (128 * G)):
                    for j in range(G):
                        stage_A(128, j)
                    stage_B(G * 128, G, g * G * 128)
                    def mixer_front(t_, j_):
                        c_ = cst[t_ % 2]
                        r_ = t_ * 128
                        yield from stage_C(128, j_, t_ % 2, c_[:, 0:64], c_[:, 64:128], "cs%d" % (t_ % 2), t_,
                                           kf_p[s, r_:r_ + 128, :], vf_p[s, r_:r_ + 128, :], lf_p[s, r_:r_ + 128, :],
                                           Vaug[:, t_, :, 0:64])
                        yield from stage_ret(128, t_ % 2, C("kdp"), g128)

                    def mixer_tail(t_, j_):
                        r_ = t_ * 128
                        yield from stage_E(128, j_, t_ % 2, x1s[s * T + r_:s * T + r_ + 128, :])
                        if g + 1 < T // (128 * G):
                            load_x(xp[s, (t_ + G) * 128:(t_ + G + 1) * 128, :], 128, j_)
                        yield

                    def chain_(*gens):
                        for g_ in gens:
                            yield from g_

                    for j in range(G):
                        t = g * G + j
                        b = t % 2
                        if t + 1 < 16:
                            load_cs(t + 1)
                        if j == 0:
                            for _ in mixer_front(t, j):
                                pass
                        parts = []
                        if j > 0:
                            parts.append(mixer_tail(t - 1, j - 1))
                        if j + 1 < G:
                            parts.append(mixer_front(t + 1, j + 1))
                        stage_fox_prompt(j, b, t, chain_(*parts) if parts else None)
                        if j == G - 1:
                            for _ in mixer_tail(t, j):
                                pass
                        if s == 0 and t == 3:
                            stop_at("g0")
                S.dma("sp", _E.dma_start(out=sr_p[s].rearrange("h d e -> d h e"), in_=Sst),
                      reads=["Sst"], sem="srout")
                if s == 0:
                    stop_at("s0")

            stop_at("pp")
            bar()
            S.op("pool", _E.memset(Vaug[:, 0, :, 64:65], 1.0), writes=["vaug"])
            for s in range(2):
                S.dma("sp", _E.dma_start(out=Sst, in_=sret[s].rearrange("h d e -> d h e")),
                      writes=["Sst"], sem="sldS")
                S.op("pool", _E.tensor_copy(out=Sbf, in_=Sst), reads=["Sst"], writes=["Sbf"])
                S.dma("sp", _E.dma_start(out=clfs, in_=clf[s].rearrange("(t p) h -> p t h", p=128)),
                      writes=["clfs"], sem="sldC")
                load_x(xs_d[s], NS, 0)
                clf2 = clfs.rearrange("p t h -> p (t h)")
                S.op("pe", _E.matmul(B(3)[:, 0:256], lhsT=C("Lst"), rhs=clf2, start=True, stop=True),
                     reads=["clfs", "cf"], writes=[Bt(3)])
                S.op("pe", _E.matmul(B(3)[:, 256:512], lhsT=C("ones"), rhs=clf2, start=True, stop=True),
                     reads=["clfs", "cf"], writes=[Bt(3)])
                S.op("dve", _E.tensor_copy(out=tots.rearrange("p t h -> p (t h)"), in_=B(3)[:, 256:512]),
                     writes=[Bt(3), "tots"])
                S.op("dve", _E.memset(sufs[:, NKT - 1, :], 0.0), writes=["sufs"])
                for tt in range(NKT - 2, -1, -1):
                    S.op("dve", _E.tensor_tensor(out=sufs[:, tt, :], in0=sufs[:, tt + 1, :],
                                                                 in1=tots[:, tt + 1, :], op=ALU.add),
                         reads=["tots"], writes=["sufs"])
                S.op("dve", _E.tensor_tensor(out=atab.rearrange("p t h -> p (t h)"), in0=B(3)[:, 0:256],
                                                      in1=sufs.rearrange("p t h -> p (t h)"), op=ALU.add),
                     reads=["sufs"], writes=[Bt(3), "atab"])
                n = NS
                stage_A(n, 0)
                stage_B(n, 1, 0)
                b = s % 2
                for _ in stage_C(n, 0, b, C("coss", n), C("sins", n), "cf", 0,
                                 kf_s[s], vf_s[s], lf_s[s], Vaug[0:n, 0, :, 0:64]):
                    pass
                for _ in stage_ret(n, b, C("kds"), g64):
                    pass
                S.dma("sp", _E.dma_start(out=sr_s[s].rearrange("h d e -> d h e"), in_=Sst),
                      reads=["Sst"], sem="srout")
                cumsum_tile(n, b, True)
                S.op("dve", _E.tensor_scalar(out=negbq[0:n, :], in0=B(4)[0:n, 0:8], scalar1=-1.0, scalar2=None,
                                                      op0=ALU.mult), reads=["b4data"], writes=[Bt(4), "negbq"])
                S.op("dve", _E.memset(B(7)[0:n, 0:260], 0.0), writes=[Bt(7)])
                S.op("dve", _E.memset(B(4)[0:n, 0:260], 0.0), writes=[Bt(4), "b4data"])
                spv_pending = [None]

                def s_prep(kt_):
                    kb_ = kt_ % 2
                    tb_ = kt_ % 2
                    S.dma("sp", _E.dma_start(out=kst[kb_], in_=ck[s, kt_ * 128:(kt_ + 1) * 128, :]),
                          writes=["kst%d" % kb_], sem="kst%d" % kb_)
                    S.dma("sp", _E.dma_start(out=vst[kb_], in_=cv[s, kt_ * 128:(kt_ + 1) * 128, :]),
                          writes=["vst%d" % kb_], sem="vst%d" % kb_)
                    for c in range(4):
                        S.op("pe", _E.transpose(out=B(tb_)[:, c * 128:(c + 1) * 128],
                                                in_=kst[kb_][:, c * 128:(c + 1) * 128], identity=C("ident")),
                             reads=["kst%d" % kb_, "cf"], writes=[Bt(tb_)])
                    S.op("dve", _E.tensor_copy(out=kTt[kb_], in_=B(tb_).rearrange("p (k t) -> p k t", k=4)),
                         writes=[Bt(tb_), "kTt%d" % kb_])

                def s_vcast(kt_):
                    kb_ = kt_ % 2
                    S.op("pool", _E.tensor_copy(out=Vt[kb_][:, :, 0:64], in_=vst[kb_].rearrange("p (h f) -> p h f", f=64)),
                         reads=["vst%d" % kb_], writes=["Vt%d" % kb_])
                    if kt_ < 2:
                        S.op("pool", _E.memset(Vt[kb_][:, :, 64:65], 1.0), writes=["Vt%d" % kb_])

                s_prep(0)
                s_vcast(0)
                for kt in range(NKT + 1):
                    kb = kt % 2
                    last = (kt == NKT)
                    rows = n if last else 128
                    if kt + 1 < NKT:
                        s_prep(kt + 1)
                    if not last:
                        kT_src, kT_tok = kTt[kb], "kTt%d" % kb
                        v_src, v_tok = Vt[kb], "Vt%d" % kb
                        bias_of = (lambda h, kt=kt: atab[:, kt, h:h + 1])
                        bias_tok = "atab"
                    else:
                        kT_src, kT_tok = kfT[:, :, 0:n], "kfTnew"
                        v_src, v_tok = Vaug[:, 0, :, :], "vaug"
                        bias_of = (lambda h: negbq[0:n, h:h + 1])
                        bias_tok = "negbq"
                    bk0, bk1 = (5, 6) if kt % 2 == 0 else (2, 3)
                    for h in range(8):
                        p_, e_ = divmod(h, 2)
                        bk = bk0 if e_ == 0 else bk1
                        S.op("pe", _E.matmul(
                            B(bk)[0:rows, p_ * 64:(p_ + 1) * 64], lhsT=kT_src[e_ * 64:(e_ + 1) * 64, p_, 0:rows],
                            rhs=QB[e_ * 64:(e_ + 1) * 64, p_, 0, e_, 0:n], start=True, stop=(not last)),
                            reads=([kT_tok] if kT_tok != "kfTnew" else ["kfT%d" % p_]) + ["qfT%d_%d" % (p_, e_)], writes=[Bt(bk)])
                        if last:
                            S.op("pe", _E.matmul(B(bk)[0:n, p_ * 64:(p_ + 1) * 64],
                                                                        lhsT=identb[e_ * 64:e_ * 64 + n, e_ * 64:e_ * 64 + n],
                                                                        rhs=negmb[e_ * 64:e_ * 64 + n, e_ * 64:e_ * 64 + n],
                                                                        start=False, stop=True),
                                 reads=["identb", "negmb"], writes=[Bt(bk)])
                    for h in range(8):
                        p_, e_ = divmod(h, 2)
                        bk = bk0 if e_ == 0 else bk1
                        S.op("act", _E.activation(
                            out=PTs[kb][0:rows, h, :], in_=B(bk)[0:rows, p_ * 64:(p_ + 1) * 64], func=AF.Exp,
                            bias=bias_of(h)[0:rows, :], scale=0.125),
                            reads=[bias_tok], writes=[Bt(bk), "PTs%d_%d" % (kb, h)])
                    def emit_spv(kb=kb, rows=rows, v_src=v_src, v_tok=v_tok, last=last):
                        for h in range(8):
                            accb = 7 if h < 4 else 4
                            hh = h % 4
                            S.op("pe", _E.matmul(
                                B(accb)[0:n, hh * 65:(hh + 1) * 65], lhsT=PTs[kb][0:rows, h, :], rhs=v_src[0:rows, h, :],
                                start=False, stop=last, skip_group_check=True),
                                reads=["PTs%d_%d" % (kb, h), v_tok], writes=[Bt(accb)])

                    if spv_pending[0] is not None:
                        spv_pending[0]()
                    spv_pending[0] = emit_spv
                    if kt + 1 < NKT:
                        s_vcast(kt + 1)
                spv_pending[0]()
                spv_pending[0] = None
                for h in range(8):
                    accb = 7 if h < 4 else 4
                    hh = h % 4
                    S.op("dve", _E.reciprocal(out=rcp[0:n, h:h + 1],
                                                                             in_=B(accb)[0:n, hh * 65 + 64:hh * 65 + 65]),
                         writes=[Bt(accb), "rcp%d" % h])
                    S.op("dve", _E.tensor_scalar(
                        out=mixb[b][0:n, 512 + h * 64:512 + (h + 1) * 64], in0=B(accb)[0:n, hh * 65:hh * 65 + 64],
                        scalar1=rcp[0:n, h:h + 1], scalar2=None, op0=ALU.mult),
                        reads=["rcp%d" % h], writes=[Bt(accb), "mixf%d_%d" % (b, h)])
                for _ in stage_E(n, 0, b, x1s[2 * T + s * NS:2 * T + (s + 1) * NS, :]):
                    pass
                if s == 0:
                    stop_at("ss0")

            bar()

            stop_at("p1")
            A.off = mark_persist
            wg_bf = A.bf16(8 * DFF).rearrange("p (k c) -> p k c", k=8)
            wu_bf = A.bf16(8 * DFF).rearrange("p (k c) -> p k c", k=8)
            wd_bf = A.bf16(NFC * D).rearrange("p (k c) -> p k c", k=NFC)
            gfin = A.f32(D)
            mark_w2 = A.off
            stage2 = [A.f32(DFF) for _ in range(NSTG)]
            A.off = mark_w2
            NSL = G + 1
            x2t = A.f32(NSL * D).rearrange("p (j c) -> p j c", j=NSL)
            xsb2 = [A.bf16(D), A.bf16(D)]
            junk2 = A.bf16(D)
            h2T = A.bf16(8 * G * 128).rearrange("p (k t) -> p k t", k=8)
            actT = A.bf16(NFC * G * 128).rearrange("p (c t) -> p c t", c=NFC)
            sgt = [A.f32(G * 128), A.f32(G * 128)]
            sm2 = A.f32(32)
            print("phase2 arena words used:", A.off, "of", AW)
            ss2 = sm2[:, 0:G]
            rs2 = sm2[:, 8:8 + G]
            ss3 = sm2[:, 16:16 + G]
            rs3 = sm2[:, 24:24 + G]

            S.dma("sp", _E.dma_start(out=gfin, in_=gfin_d), writes=["gfin"], sem="gfin")
            kk_ = 0
            for (wsrc, wdst) in ((w_gate, wg_bf), (w_up, wu_bf)):
                for k in range(8):
                    sg = stage2[kk_ % NSTG]
                    tok = "stg2_%d" % (kk_ % NSTG)
                    kk_ += 1
                    S.dma("sp", _E.dma_start(out=sg[:, 0:DFF], in_=wsrc[k * 128:(k + 1) * 128, :]),
                          writes=[tok], sem=tok)
                    convert_rows(wdst[:, k, :], sg, DFF, gffnc[:, k:k + 1], rd=[tok])
            for k in range(NFC):
                sg = stage2[kk_ % NSTG]
                tok = "stg2_%d" % (kk_ % NSTG)
                kk_ += 1
                S.dma("sp", _E.dma_start(out=sg[:, 0:D], in_=w_down[k * 128:(k + 1) * 128, :]),
                      writes=[tok], sem=tok)
                convert_rows(wd_bf[:, k, :], sg, D, None, rd=[tok])
            bar()

            groups = [(g * G * 128, G, y_p[g * G * 128:(g + 1) * G * 128, :]) for g in range(2 * T // (G * 128))]
            groups.append((2 * T, 1, y_s))
            pcnt = {"b": 0}
            pre_b = {}

            def p2_pre(row0, j, sl):
                b = pcnt["b"] % 2
                pcnt["b"] += 1
                pre_b[j] = b
                S.dma("sp", _E.dma_start(out=x2t[:, sl, :], in_=x1s[row0 + j * 128:row0 + (j + 1) * 128, :]),
                      writes=["x2t%d" % sl], sem="x2t%d" % sl)
                S.op("pool", _E.memset(ss2[:, j:j + 1], 0.0), writes=["ss2_%d" % j])
                S.op("act", _E.activation(out=junk2, in_=x2t[:, sl, :], func=AF.Square, accum_out=ss2[:, j:j + 1]),
                     reads=["x2t%d" % sl], writes=["junk2", "ss2_%d" % j])
                S.op("dve", _E.tensor_scalar(out=rs2[:, j:j + 1], in0=ss2[:, j:j + 1], scalar1=1.0 / D,
                                             scalar2=1e-6, op0=ALU.mult, op1=ALU.add),
                     reads=["ss2_%d" % j], writes=["rs2_%d" % j])
                S.op("pool", _E.tensor_tensor(out=rs2[:, j:j + 1], in0=rs2[:, j:j + 1], in1=mhalf[:, 0:1], op=ALU.pow),
                     reads=["rs2_%d" % j], writes=["rs2_%d" % j])
                S.op("dve", _E.tensor_scalar(out=xsb2[b], in0=x2t[:, sl, :], scalar1=rs2[:, j:j + 1], scalar2=None,
                                             op0=ALU.mult),
                     reads=["x2t%d" % sl, "rs2_%d" % j], writes=["xsb2_%d" % b])

            def p2_tp(j):
                b = pre_b[j]
                for k in range(8):
                    S.op("pe", _E.transpose(out=tpb[:, k * 128:(k + 1) * 128], in_=xsb2[b][:, k * 128:(k + 1) * 128],
                                            identity=identb), reads=["xsb2_%d" % b], writes=[Bt(0)])
                S.op("act", _E.copy(out=h2T[:, :, j * 128:(j + 1) * 128], in_=tp3), writes=[Bt(0), "h2T%d" % j])

            def p2_chunks(NT, ntl):
                h2toks = ["h2T%d" % j for j in range(ntl)]
                for c in range(NFC):
                    gb, ub = (1, 2) if c % 2 == 0 else (3, 4)
                    sb_ = c % 2
                    for k in range(8):
                        S.op("pe", _E.matmul(B(gb)[:, 0:NT], lhsT=wg_bf[:, k, c * 128:(c + 1) * 128],
                                             rhs=h2T[:, k, 0:NT], start=(k == 0), stop=(k == 7)),
                             reads=h2toks, writes=[Bt(gb)])
                    for k in range(8):
                        S.op("pe", _E.matmul(B(ub)[:, 0:NT], lhsT=wu_bf[:, k, c * 128:(c + 1) * 128],
                                             rhs=h2T[:, k, 0:NT], start=(k == 0), stop=(k == 7)),
                             reads=h2toks, writes=[Bt(ub)])
                    S.op("act", _E.activation(out=sgt[sb_][:, 0:NT], in_=B(gb)[:, 0:NT], func=AF.Silu),
                         writes=[Bt(gb), "sgt%d" % sb_])
                    S.op("dve", _E.tensor_tensor(out=actT[:, c, 0:NT], in0=B(ub)[:, 0:NT], in1=sgt[sb_][:, 0:NT],
                                                 op=ALU.mult), reads=["sgt%d" % sb_], writes=[Bt(ub), "actT"])

            def p2_down(j):
                pb = (5, 6)
                for half in range(2):
                    for c in range(NFC):
                        S.op("pe", _E.matmul(B(pb[half]), lhsT=actT[:, c, j * 128:(j + 1) * 128],
                                             rhs=wd_bf[:, c, half * 512:(half + 1) * 512],
                                             start=(c == 0), stop=(c == NFC - 1)), reads=["actT"], writes=[Bt(pb[half])])

            def p2_fin(j, ydst, sl):
                S.op("dve", _E.tensor_tensor(out=x2t[:, sl, :], in0=x2t[:, sl, :], in1=ps[:, 2560:3584], op=ALU.add),
                     writes=[Bt(5), Bt(6), "x2t%d" % sl])
                S.op("pool", _E.memset(ss3[:, j:j + 1], 0.0), writes=["ss3_%d" % j])
                S.op("act", _E.activation(out=junk2, in_=x2t[:, sl, :], func=AF.Square, accum_out=ss3[:, j:j + 1]),
                     reads=["x2t%d" % sl], writes=["junk2", "ss3_%d" % j])
                S.op("dve", _E.tensor_scalar(out=rs3[:, j:j + 1], in0=ss3[:, j:j + 1], scalar1=1.0 / D,
                                             scalar2=1e-6, op0=ALU.mult, op1=ALU.add),
                     reads=["ss3_%d" % j], writes=["rs3_%d" % j])
                S.op("pool", _E.tensor_tensor(out=rs3[:, j:j + 1], in0=rs3[:, j:j + 1], in1=mhalf[:, 0:1], op=ALU.pow),
                     reads=["rs3_%d" % j], writes=["rs3_%d" % j])
                S.op("dve", _E.scalar_tensor_tensor(out=x2t[:, sl, :], in0=x2t[:, sl, :], scalar=rs3[:, j:j + 1],
                                                    in1=gfin, op0=ALU.mult, op1=ALU.mult),
                     reads=["rs3_%d" % j, "gfin"], writes=["x2t%d" % sl])
                S.dma("sp", _E.dma_start(out=ydst[j * 128:(j + 1) * 128, :], in_=x2t[:, sl, :]),
                      reads=["x2t%d" % sl], sem="yout%d" % sl)

            def slot(gi_, j_):
                return (gi_ * G + j_) % NSL

            for j in range(groups[0][1]):
                p2_pre(groups[0][0], j, slot(0, j))
                p2_tp(j)
            for gi, (row0, ntl, ydst) in enumerate(groups):
                nxt = groups[gi + 1] if gi + 1 < len(groups) else None
                nn = nxt[1] if nxt else 0
                p2_chunks(ntl * 128, ntl)
                if nn > 0:
                    p2_pre(nxt[0], 0, slot(gi + 1, 0))
                tp_done = 0
                for j in range(ntl):
                    p2_down(j)
                    if j >= 1 and j - 1 < nn:
                        p2_tp(j - 1)
                        tp_done = j
                    p2_fin(j, ydst, slot(gi, j))
                    if j + 1 < nn:
                        p2_pre(nxt[0], j + 1, slot(gi + 1, j + 1))
                for jj in range(tp_done, nn):
                    if jj > ntl:
                        p2_pre(nxt[0], jj, slot(gi + 1, jj))
                    p2_tp(jj)

        except _Stop as ex:
            print('STOPPED at', ex)
        S.emit(final_wait_sems=list(S.dma_cum.keys()))
        print("ops per engine:", {e: len(v) for e, v in S.by_eng.items()})
    return nc


_PROG = {}


def kernel(x_prompt, x_sample, cache_fox_k, cache_fox_v, cache_fox_logf, state_ret,
           w_in, b_forget, ret_gn_gain, w_out, norm_mix_gain, norm_ffn_gain,
           w_gate, w_up, w_down, norm_final_gain):
    f = lambda a: np.ascontiguousarray(np.asarray(a, dtype=np.float32))
    x_prompt, x_sample = f(x_prompt), f(x_sample)
    ckf, cvf, clff, srf = f(cache_fox_k), f(cache_fox_v), f(cache_fox_logf), f(state_ret)
    cf, cosp, sinp, g128, g64 = _host_consts()
    if "nc" not in _PROG:
        _PROG["nc"] = build_program(g128, g64)
    nc = _PROG["nc"]
    vecs = np.zeros((128, 28), np.float32)
    vecs[:, 0:8] = f(norm_mix_gain)[0].reshape(8, 128).T
    vecs[:, 8:16] = f(norm_ffn_gain)[0].reshape(8, 128).T
    vecs[:, 16:20] = f(ret_gn_gain)[0].reshape(4, 128).T
    vecs[:, 20:28] = np.broadcast_to(f(b_forget)[0][None, :], (128, 8))
    gfin_b = np.ascontiguousarray(np.broadcast_to(f(norm_final_gain)[None, :], (128, D)))
    shared = {
        "w_in": f(w_in)[0], "w_out": f(w_out)[0], "w_gate": f(w_gate)[0], "w_up": f(w_up)[0],
        "w_down": f(w_down)[0], "cf32": cf, "cosp": cosp, "sinp": sinp, "vecs": vecs, "gfin_b": gfin_b,
    }
    in_maps = []
    for c in range(NCORES):
        m = dict(shared)
        m["xp"] = x_prompt[2 * c:2 * c + 2]
        m["xs"] = x_sample[2 * c:2 * c + 2]
        m["ck"] = ckf[0, 2 * c:2 * c + 2].reshape(2, PAST, 512)
        m["cv"] = cvf[0, 2 * c:2 * c + 2].reshape(2, PAST, 512)
        m["clf"] = clff[0, 2 * c:2 * c + 2]
        m["sret"] = srf[0, 2 * c:2 * c + 2]
        in_maps.append(m)
    res = run_bass_kernel_spmd(nc, in_maps, core_ids=list(range(NCORES)))
    R = res.results
    cat = lambda name: np.concatenate([np.asarray(r[name]) for r in R], axis=0)
    y_prompt = cat("y_p").reshape(16, T, D)
    y_sample = cat("y_s").reshape(16, NS, D)
    sr_p = cat("sr_p").reshape(1, 16, 4, 128, 128)
    kf_p = cat("kf_p").reshape(1, 16, T, 8, 64)
    vf_p = cat("vf_p").reshape(1, 16, T, 8, 64)
    lf_p = cat("lf_p").reshape(1, 16, T, 8)
    sr_s = cat("sr_s").reshape(1, 16, 4, 128, 128)
    kf_s = cat("kf_s").reshape(1, 16, NS, 8, 64)
    vf_s = cat("vf_s").reshape(1, 16, NS, 8, 64)
    lf_s = cat("lf_s").reshape(1, 16, NS, 8)
    return (y_prompt.astype(np.float32), y_sample.astype(np.float32), sr_p, kf_p, vf_p, lf_p,
            sr_s, kf_s, vf_s, lf_s)
```
